# Optimizing a Trainium2 kernel written in Bass

```python
import math
import jax
import jax.numpy as jnp
from jax import lax
import numpy as np

D_MODEL = 2048
BATCH = 1
SEQ = 16384
DEPTH = 4

S5_GROUP = 16
S5_W = 3 * D_MODEL // 8
S5_GROUPS = S5_W // S5_GROUP
S5_STATE = 64
S5_CHUNK = 128
S5_DT_MIN = 0.001
S5_DT_MAX = 0.1
RWKV_HEAD = 64
RWKV_W = 5 * D_MODEL // 16
RWKV_HEADS = RWKV_W // RWKV_HEAD
RWKV_DECAY_LORA = 96
RWKV_AAA_LORA = 128
RWKV_MV_LORA = 64
RWKV_GATE_LORA = 256
RWKV_COLS = 3 * RWKV_W + RWKV_DECAY_LORA + RWKV_AAA_LORA + RWKV_GATE_LORA
RWKV_LNX_EPS = 64e-5
GLA_DV = 128
GLA_DK = 64
GLA_V = 5 * D_MODEL // 16
GLA_HEADS = GLA_V // GLA_DV
GLA_K = GLA_HEADS * GLA_DK
GLA_LORA = 16
GLA_TAU = 16.0
GLA_CHUNK = 64
GLA_COLS = 2 * GLA_K + 2 * GLA_V + GLA_LORA
N_BRANCH = 3
GATE_COLS = N_BRANCH * D_MODEL
IN_COLS = S5_W + RWKV_COLS + GLA_COLS + GATE_COLS
MIX_W = S5_W + RWKV_W + GLA_V
D_FF = 4 * D_MODEL
NORM_EPS = 1e-6

kernel_name = "hybrid_s5_rwkv7_gla_gated_trunk"


def rmsnorm(x, g):
    xf = x.astype(jnp.float32)
    y = xf * lax.rsqrt(jnp.mean(xf * xf, axis=-1, keepdims=True) + NORM_EPS)
    return (y * g.astype(jnp.float32)).astype(x.dtype)


def token_shift(z, mu):
    prev = jnp.pad(z, ((0, 0), (1, 0), (0, 0)))[:, :-1]
    return z + (prev - z) * mu


def s5_branch(xa, lam_re, lam_im, log_step, b_re, b_im, c_re, c_im, d, glu_w, glu_b):
    bsz, length, _ = xa.shape
    f32 = jnp.float32
    xf = xa.astype(f32)
    lam = lax.complex(jnp.minimum(lam_re.astype(f32), -1e-4), lam_im.astype(f32))
    dt = jnp.exp(log_step.astype(f32))[:, None]
    lam_dt = lam * dt
    lam_bar = jnp.exp(lam_dt)
    b_mat = lax.complex(b_re.astype(f32), b_im.astype(f32))
    b_bar = ((lam_bar - 1.0) / lam)[..., None] * b_mat
    c_mat = lax.complex(c_re.astype(f32), c_im.astype(f32))
    n_chunks = length // S5_CHUNK
    u = xf.reshape(bsz, n_chunks, S5_CHUNK, S5_GROUPS, S5_GROUP).transpose(1, 0, 2, 3, 4)
    pows = jnp.exp(lam_dt[None] * jnp.arange(1, S5_CHUNK + 1, dtype=f32)[:, None, None])

    def binop(e1, e2):
        a1, b1 = e1
        a2, b2 = e2
        return a1 * a2, a2 * b1 + b2

    def step(s, u_c):
        bu = jnp.einsum('bcgi,gpi->bcgp', u_c.astype(jnp.complex64), b_bar)
        a = jnp.broadcast_to(lam_bar, bu.shape)
        _, local = lax.associative_scan(binop, (a, bu), axis=1)
        states = local + pows[None] * s[:, None]
        y = jnp.einsum('bcgp,gip->bcgi', states, c_mat).real
        return states[:, -1], y

    s0 = jnp.zeros((bsz, S5_GROUPS, S5_STATE), jnp.complex64)
    _, ys = lax.scan(step, s0, u)
    y = ys.transpose(1, 0, 2, 3, 4).reshape(bsz, length, S5_W) + d.astype(f32) * xf
    y = jax.nn.gelu(y)
    y = y * jax.nn.sigmoid(y @ glu_w.astype(f32) + glu_b.astype(f32))
    return y.astype(xa.dtype)


def rwkv7_branch(z, mu, w_lora, w0, a_lora, a0, g_lora, k_k, k_a, r_k, lnx_w, lnx_b,
                 v_first, v_gate):
    bsz, length, _ = z.shape
    f32 = jnp.float32
    hh, nn = RWKV_HEADS, RWKV_HEAD
    zs = token_shift(z, mu).astype(f32)
    o1 = 3 * RWKV_W
    r, k, v, w_in, a_in, g_in = jnp.split(
        zs, [RWKV_W, 2 * RWKV_W, o1, o1 + RWKV_DECAY_LORA, o1 + RWKV_DECAY_LORA + RWKV_AAA_LORA], axis=-1)
    w = -jax.nn.softplus(-(w0.astype(f32) + jnp.tanh(w_in) @ w_lora.astype(f32))) - 0.5
    decay = jnp.exp(-jnp.exp(w))
    a = jax.nn.sigmoid(a0.astype(f32) + a_in @ a_lora.astype(f32))
    g = jax.nn.sigmoid(g_in) @ g_lora.astype(f32)
    if v_gate is not None:
        v = v + (v_first - v) * v_gate.astype(f32)
    kk = (k * k_k.astype(f32)).reshape(bsz, length, hh, nn)
    kk = kk * lax.rsqrt(jnp.maximum(jnp.sum(kk * kk, axis=-1, keepdims=True), 1e-24))
    k = k * (1.0 + (a - 1.0) * k_a.astype(f32))

    def heads(t):
        return t.reshape(bsz, length, hh, nn)

    rh, kh, vh, ah, wh = heads(r), heads(k), heads(v), heads(a), heads(decay)

    def tm(t):
        return t.transpose(1, 0, 2, 3)

    def step(s, inp):
        r_t, w_t, k_t, v_t, aa_t, bb_t = inp
        sa = jnp.einsum('bhvk,bhk->bhv', s, aa_t)
        s = s * w_t[:, :, None, :] + sa[..., None] * bb_t[:, :, None, :] + v_t[..., None] * k_t[:, :, None, :]
        return s, jnp.einsum('bhvk,bhk->bhv', s, r_t)

    s0 = jnp.zeros((bsz, hh, nn, nn), f32)
    _, ys = lax.scan(step, s0, (tm(rh), tm(wh), tm(kh), tm(vh), tm(-kk), tm(kk * ah)))
    y = ys.transpose(1, 0, 2, 3)
    mean = jnp.mean(y, axis=-1, keepdims=True)
    var = jnp.mean(jnp.square(y - mean), axis=-1, keepdims=True)
    y = ((y - mean) * lax.rsqrt(var + RWKV_LNX_EPS)).reshape(bsz, length, RWKV_W)
    y = y * lnx_w.astype(f32) + lnx_b.astype(f32)
    bonus = jnp.sum(rh * kh * r_k.astype(f32).reshape(hh, nn), axis=-1, keepdims=True) * vh
    y = (y + bonus.reshape(bsz, length, RWKV_W)) * g
    return y.astype(z.dtype), v


def gla_branch(z, alpha_lora, alpha_bias, norm_g):
    bsz, length, _ = z.shape
    f32 = jnp.float32
    zf = z.astype(f32)
    q, k, v, g, a_in = jnp.split(zf, [GLA_K, 2 * GLA_K, 2 * GLA_K + GLA_V, 2 * GLA_K + 2 * GLA_V], axis=-1)
    log_a = jax.nn.log_sigmoid(a_in @ alpha_lora.astype(f32) + alpha_bias.astype(f32)) / GLA_TAU
    n_chunks = length // GLA_CHUNK

    def heads(t, dh):
        return t.reshape(bsz, n_chunks, GLA_CHUNK, GLA_HEADS, dh).transpose(1, 0, 3, 2, 4)

    qh = heads(q * (GLA_DK ** -0.5), GLA_DK)
    kh, lh, vh = heads(k, GLA_DK), heads(log_a, GLA_DK), heads(v, GLA_DV)
    causal = jnp.tril(jnp.ones((GLA_CHUNK, GLA_CHUNK), dtype=bool))[:, :, None]

    def step(s, inp):
        q_c, k_c, v_c, l_c = inp
        b = jnp.cumsum(l_c, axis=2)
        inter = jnp.einsum('bhtk,bhkv->bhtv', q_c * jnp.exp(b), s)
        diff = b[:, :, :, None, :] - b[:, :, None, :, :]
        dec = jnp.exp(jnp.where(causal, diff, -jnp.inf))
        att = jnp.einsum('bhtk,bhsk,bhtsk->bhts', q_c, k_c, dec)
        intra = jnp.einsum('bhts,bhsv->bhtv', att, v_c)
        b_last = b[:, :, -1:, :]
        s = s * jnp.exp(b_last[:, :, 0, :])[..., None] + jnp.einsum(
            'bhsk,bhsv->bhkv', k_c * jnp.exp(b_last - b), v_c)
        return s, inter + intra

    s0 = jnp.zeros((bsz, GLA_HEADS, GLA_DK, GLA_DV), f32)
    _, os_ = lax.scan(step, s0, (qh, kh, vh, lh))
    o = os_.transpose(1, 0, 3, 2, 4).reshape(bsz, length, GLA_HEADS, GLA_DV)
    o = o * lax.rsqrt(jnp.mean(o * o, axis=-1, keepdims=True) + NORM_EPS)
    o = o.reshape(bsz, length, GLA_V) * norm_g.astype(f32) * jax.nn.silu(g)
    return o.astype(z.dtype)


def setup_inputs(seed: int = 0) -> dict:
    key = jax.random.key(seed)
    ks = iter(jax.random.split(key, 64))
    f32 = jnp.float32

    def nrm(shape, scale):
        return jax.random.normal(next(ks), shape, f32) * scale

    def unif(shape, lo, hi):
        return jax.random.uniform(next(ks), shape, f32, lo, hi)

    d = D_MODEL
    x = nrm((BATCH, SEQ, d), 1.0)
    norm_mix = 1.0 + nrm((DEPTH, d), 0.02)
    w_in = nrm((DEPTH, d, IN_COLS), d ** -0.5)
    gate_bias = nrm((DEPTH, GATE_COLS), 0.1)
    n_state = jnp.arange(S5_STATE, dtype=f32)
    s5_lambda_re = -0.5 + nrm((DEPTH, S5_GROUPS, S5_STATE), 0.01)
    s5_lambda_im = math.pi * n_state + nrm((DEPTH, S5_GROUPS, S5_STATE), 0.01)
    s5_log_step = unif((DEPTH, S5_GROUPS), math.log(S5_DT_MIN), math.log(S5_DT_MAX))
    s5_b_re = nrm((DEPTH, S5_GROUPS, S5_STATE, S5_GROUP), (2 * S5_GROUP) ** -0.5)
    s5_b_im = nrm((DEPTH, S5_GROUPS, S5_STATE, S5_GROUP), (2 * S5_GROUP) ** -0.5)
    s5_c_re = nrm((DEPTH, S5_GROUPS, S5_GROUP, S5_STATE), S5_STATE ** -0.5)
    s5_c_im = nrm((DEPTH, S5_GROUPS, S5_GROUP, S5_STATE), S5_STATE ** -0.5)
    s5_d = nrm((DEPTH, S5_W), 1.0)
    s5_glu_w = nrm((DEPTH, S5_W, S5_W), S5_W ** -0.5)
    s5_glu_b = nrm((DEPTH, S5_W), 0.02)
    rwkv_mu = unif((DEPTH, RWKV_COLS), 0.0, 1.0)
    rwkv_w_lora = nrm((DEPTH, RWKV_DECAY_LORA, RWKV_W), 0.5 * RWKV_DECAY_LORA ** -0.5)
    ratio = jnp.arange(DEPTH, dtype=f32) / max(DEPTH - 1, 1)
    nch = jnp.arange(RWKV_W, dtype=f32) / (RWKV_W - 1)
    decay_speed = -7.0 + 5.0 * nch[None, :] ** (0.85 + jnp.sqrt(ratio)[:, None])
    rwkv_w0 = decay_speed + 0.5 + nrm((DEPTH, RWKV_W), 0.05)
    rwkv_a_lora = nrm((DEPTH, RWKV_AAA_LORA, RWKV_W), 0.5 * RWKV_AAA_LORA ** -0.5)
    rwkv_a0 = nrm((DEPTH, RWKV_W), 0.1)
    rwkv_g_lora = nrm((DEPTH, RWKV_GATE_LORA, RWKV_W), RWKV_GATE_LORA ** -0.5)
    rwkv_k_k = 0.85 + nrm((DEPTH, RWKV_W), 0.02)
    rwkv_k_a = 1.0 + nrm((DEPTH, RWKV_W), 0.02)
    rwkv_r_k = -0.04 + nrm((DEPTH, RWKV_W), 0.02)
    rwkv_lnx_w = 1.0 + nrm((DEPTH, RWKV_W), 0.02)
    rwkv_lnx_b = nrm((DEPTH, RWKV_W), 0.02)
    rwkv_vres_a = nrm((DEPTH - 1, d, RWKV_MV_LORA), d ** -0.5)
    rwkv_vres_mu = unif((DEPTH - 1, RWKV_MV_LORA), 0.0, 1.0)
    rwkv_vres_b = nrm((DEPTH - 1, RWKV_MV_LORA, RWKV_W), RWKV_MV_LORA ** -0.5)
    rwkv_vres_bias = 1.0 + nrm((DEPTH - 1, RWKV_W), 0.1)
    gla_alpha_lora = nrm((DEPTH, GLA_LORA, GLA_K), GLA_LORA ** -0.5)
    gla_alpha_bias = nrm((DEPTH, GLA_K), 0.1)
    gla_norm_g = 1.0 + nrm((DEPTH, GLA_V), 0.02)
    w_up = jnp.concatenate([nrm((DEPTH, S5_W, d), S5_W ** -0.5),
                            nrm((DEPTH, RWKV_W, d), RWKV_W ** -0.5),
                            nrm((DEPTH, GLA_V, d), GLA_V ** -0.5)], axis=1)
    w_out = nrm((DEPTH, d, d), d ** -0.5)
    norm_mlp = 1.0 + nrm((DEPTH, d), 0.02)
    mlp_w1 = nrm((DEPTH, d, D_FF), d ** -0.5)
    mlp_w2 = nrm((DEPTH, D_FF, d), D_FF ** -0.5)
    final_norm = 1.0 + nrm((d,), 0.02)
    return {"x": x, "norm_mix": norm_mix, "w_in": w_in, "gate_bias": gate_bias,
            "s5_lambda_re": s5_lambda_re, "s5_lambda_im": s5_lambda_im, "s5_log_step": s5_log_step,
            "s5_b_re": s5_b_re, "s5_b_im": s5_b_im, "s5_c_re": s5_c_re, "s5_c_im": s5_c_im,
            "s5_d": s5_d, "s5_glu_w": s5_glu_w, "s5_glu_b": s5_glu_b,
            "rwkv_mu": rwkv_mu, "rwkv_w_lora": rwkv_w_lora, "rwkv_w0": rwkv_w0,
            "rwkv_a_lora": rwkv_a_lora, "rwkv_a0": rwkv_a0, "rwkv_g_lora": rwkv_g_lora,
            "rwkv_k_k": rwkv_k_k, "rwkv_k_a": rwkv_k_a, "rwkv_r_k": rwkv_r_k,
            "rwkv_lnx_w": rwkv_lnx_w, "rwkv_lnx_b": rwkv_lnx_b,
            "rwkv_vres_a": rwkv_vres_a, "rwkv_vres_mu": rwkv_vres_mu, "rwkv_vres_b": rwkv_vres_b,
            "rwkv_vres_bias": rwkv_vres_bias,
            "gla_alpha_lora": gla_alpha_lora, "gla_alpha_bias": gla_alpha_bias, "gla_norm_g": gla_norm_g,
            "w_up": w_up, "w_out": w_out, "norm_mlp": norm_mlp, "mlp_w1": mlp_w1, "mlp_w2": mlp_w2,
            "final_norm": final_norm}


def reference(x, norm_mix, w_in, gate_bias,
              s5_lambda_re, s5_lambda_im, s5_log_step, s5_b_re, s5_b_im, s5_c_re, s5_c_im,
              s5_d, s5_glu_w, s5_glu_b,
              rwkv_mu, rwkv_w_lora, rwkv_w0, rwkv_a_lora, rwkv_a0, rwkv_g_lora,
              rwkv_k_k, rwkv_k_a, rwkv_r_k, rwkv_lnx_w, rwkv_lnx_b,
              rwkv_vres_a, rwkv_vres_mu, rwkv_vres_b, rwkv_vres_bias,
              gla_alpha_lora, gla_alpha_bias, gla_norm_g,
              w_up, w_out, norm_mlp, mlp_w1, mlp_w2, final_norm):
    o_rw = S5_W
    o_gla = o_rw + RWKV_COLS
    o_gate = o_gla + GLA_COLS
    r_b = S5_W
    r_c = S5_W + RWKV_W
    v_first = None
    for l in range(DEPTH):
        u = rmsnorm(x, norm_mix[l])
        z = u @ w_in[l]
        y_a = s5_branch(z[..., :o_rw], s5_lambda_re[l], s5_lambda_im[l], s5_log_step[l],
                        s5_b_re[l], s5_b_im[l], s5_c_re[l], s5_c_im[l], s5_d[l],
                        s5_glu_w[l], s5_glu_b[l])
        if l == 0:
            v_gate = None
        else:
            v_gate = jax.nn.sigmoid(rwkv_vres_bias[l - 1] + token_shift(
                u @ rwkv_vres_a[l - 1], rwkv_vres_mu[l - 1]) @ rwkv_vres_b[l - 1])
        y_b, v_l = rwkv7_branch(z[..., o_rw:o_gla], rwkv_mu[l], rwkv_w_lora[l], rwkv_w0[l],
                                rwkv_a_lora[l], rwkv_a0[l], rwkv_g_lora[l], rwkv_k_k[l],
                                rwkv_k_a[l], rwkv_r_k[l], rwkv_lnx_w[l], rwkv_lnx_b[l],
                                v_first, v_gate)
        if l == 0:
            v_first = v_l
        y_c = gla_branch(z[..., o_gla:o_gate], gla_alpha_lora[l], gla_alpha_bias[l], gla_norm_g[l])
        gates = jax.nn.sigmoid(z[..., o_gate:] + gate_bias[l])
        g_a, g_b, g_c = jnp.split(gates, N_BRANCH, axis=-1)
        wu = w_up[l]
        merged = (g_a * (y_a @ wu[:r_b]) + g_b * (y_b @ wu[r_b:r_c]) + g_c * (y_c @ wu[r_c:]))
        x = x + merged @ w_out[l]
        h = rmsnorm(x, norm_mlp[l])
        x = x + jnp.square(jax.nn.relu(h @ mlp_w1[l])) @ mlp_w2[l]
    return rmsnorm(x, final_norm)
```

```python
import math
import numpy as np
from contextlib import ExitStack
import concourse.bass as bass
import concourse.mybir as mybir
F32 = mybir.dt.float32; BF16 = mybir.dt.bfloat16
AF = mybir.ActivationFunctionType; ALU = mybir.AluOpType; AX = mybir.AxisListType

class Buf:
    def __init__(self, t, name):
        self.t = t; self.name = name; self.lw = {}; self.rd = {}; self.sem = None; self.dcount = 0; self.excl = False
    def __getitem__(self, k):
        return self.t[k]

class Group:
    def __init__(self, name, sem):
        self.name = name; self.sem = sem; self.dcount = 0

class View:
    def __init__(self, parent, ap):
        object.__setattr__(self, 'parent', parent); object.__setattr__(self, 't', ap)
    def __getitem__(self, k): return self.t[k]
    def __getattr__(self, n): return getattr(self.parent, n)
    def __setattr__(self, n, v): setattr(self.parent, n, v)
    def __eq__(self, o): return (o.parent if isinstance(o, View) else o) is self.parent
    def __hash__(self): return id(self.parent)

class KB:
    SAME_ENGINE_SYNC = ('dve', 'act', 'pool')
    def view(self, parent, ap, name=None):
        return View(parent, ap)
    def __init__(self):
        self.nc = bass.Bass("TRN2", target_bir_lowering=False)
        self.es = ExitStack()
        nc = self.nc
        self.eng = {'pe': nc.tensor, 'dve': nc.vector, 'act': nc.scalar, 'pool': nc.gpsimd, 'sp': nc.sync}
        self.sem = {k: self.es.enter_context(nc.semaphore("s_" + k)) for k in self.eng}
        self.cnt = {k: 0 for k in self.eng}
        self.waited = {}
        self.bufs = []
        self.n_inst = 0
        self.dma_sems = []
    def dram(self, name, shape, dt, kind):
        return self.nc.dram_tensor(name, list(shape), dt, kind=kind).ap()
    def sb(self, name, shape, dt=F32):
        b = Buf(self.es.enter_context(self.nc.sbuf_tensor(name, list(shape), dt)), name); self.bufs.append(b); return b
    def ps(self, name, shape, dt=F32):
        b = Buf(self.es.enter_context(self.nc.psum_tensor(name, list(shape), dt)), name); b.excl = True; self.bufs.append(b); return b
    def group(self, name):
        g_ = Group(name, self.es.enter_context(self.nc.semaphore("g_" + name))); self.dma_sems.append(g_); return g_
    def _wait(self, e, key, semh, count):
        if isinstance(semh, Group):
            count = semh.dcount; semh = semh.sem
        if self.waited.get((e, key), 0) >= count: return
        self.eng[e].wait_ge(semh, count); self.waited[(e, key)] = count
    def _deps(self, e, reads, writes):
        for b in reads:
            for key, (semh, c) in b.lw.items():
                if key == e and e not in self.SAME_ENGINE_SYNC: continue
                self._wait(e, key, semh, c)
            if b.excl:
                for key, (semh, c) in b.rd.items():
                    if key != e: self._wait(e, key, semh, c)
        for b in writes:
            for d in (b.lw, b.rd):
                for key, (semh, c) in d.items():
                    if key == e and e not in self.SAME_ENGINE_SYNC: continue
                    self._wait(e, key, semh, c)
    def op(self, e, fn, reads=(), writes=()):
        self._deps(e, reads, writes)
        ins = fn(self.eng[e])
        self.cnt[e] += 1; c = self.cnt[e]
        ins.then_inc(self.sem[e], 1)
        for b in writes:
            b.lw = {e: (self.sem[e], c)}; b.rd = {}
        for b in reads:
            if b not in writes: b.rd[e] = (self.sem[e], c)
        self.n_inst += 1
        return ins
    def dma(self, q, out, in_, rbuf=None, wbuf=None, grp=None, **kw):
        b = wbuf if wbuf is not None else rbuf
        if grp is None:
            if b.sem is None:
                b.sem = self.es.enter_context(self.nc.semaphore("d_" + b.name)); self.dma_sems.append(b)
            holder = b
        else:
            holder = grp
        self._deps(q, [rbuf] if rbuf is not None else [], [wbuf] if wbuf is not None else [])
        ins = self.eng[q].dma_start(out=out, in_=in_, **kw)
        holder.dcount += 16
        ins.then_inc(holder.sem, 16)
        key = 'dma_' + holder.name
        ent = (grp, None) if grp is not None else (b.sem, b.dcount)
        if wbuf is not None:
            wbuf.lw = {key: ent}; wbuf.rd = {}
        else:
            rbuf.rd[key] = ent
        self.n_inst += 1
        return ins
    def finish(self, e='sp'):
        for b in self.dma_sems:
            self._wait(e, 'dma_' + b.name, b.sem, b.dcount)
        self.es.close()
        return self.nc

D = 2048; KC = 16
def build_A(ntok, ncols, TB=512):
    k = KB()
    xT = k.dram("xT", [D, ntok], F32, "ExternalInput")
    w = k.dram("w", [D, ncols], F32, "ExternalInput")
    g = k.dram("g", [128, KC], F32, "ExternalInput")
    zT = k.dram("zT", [ncols, ntok], F32, "ExternalOutput")
    ntb = ntok // TB
    xk = [k.sb(f"xk{i}", [128, KC, TB]) for i in range(2)]
    u = k.sb("u", [128, KC, ntok], BF16)
    sq = [k.sb(f"sq{i}", [128, TB], BF16) for i in range(2)]
    ones = k.sb("ones", [128, 128], BF16)
    gt = k.sb("gt", [128, KC])
    rstd = k.sb("rstd", [128, TB])
    psn = k.ps("psn", [128, TB])
    pz = [k.ps(f"pz{i}", [128, TB]) for i in range(4)]
    zo = [k.sb(f"zo{i}", [128, TB]) for i in range(3)]
    wb = [k.sb(f"wb{i}", [128, KC, 128], BF16) for i in range(2)]
    k.op('pool', lambda e: e.memset(ones[:], 1.0), [], [ones])
    k.dma('sp', gt[:], g[:, :], wbuf=gt)
    xv = xT.rearrange("(kc p) t -> p kc t", p=128)
    for tb in range(ntb):
        X = xk[tb % 2]
        k.dma('sp', X[:], xv[:, :, tb * TB:(tb + 1) * TB], wbuf=X)
        for kc in range(KC):
            S = sq[kc % 2]
            k.op('act', lambda e: e.activation(out=S[:], in_=X[:, kc, :], func=AF.Square), [X], [S])
            k.op('pe', lambda e: e.matmul(psn[:], ones[:], S[:], start=(kc == 0), stop=(kc == KC - 1)), [ones, S], [psn])
        k.op('act', lambda e: e.activation(out=rstd[:], in_=psn[:], func=AF.Sqrt, scale=1.0 / D, bias=1e-6), [psn], [rstd])
        k.op('dve', lambda e: e.reciprocal(out=rstd[:], in_=rstd[:]), [rstd], [rstd])
        for kc in range(KC):
            k.op('dve', lambda e: e.scalar_tensor_tensor(out=u[:, kc, tb * TB:(tb + 1) * TB], in0=X[:, kc, :], scalar=gt[:, kc:kc + 1],
                                                        in1=rstd[:], op0=ALU.mult, op1=ALU.mult), [X, gt, rstd], [u])
    wv = w.rearrange("(kc p) c -> p kc c", p=128)
    ncc = (ncols + 127) // 128
    i = 0
    for cc in range(ncc):
        cw = min(128, ncols - cc * 128)
        W = wb[cc % 2]
        k.dma('pool', W[:, :, :cw], wv[:, :, cc * 128:cc * 128 + cw], wbuf=W)
        for tb in range(ntb):
            P = pz[i % 4]; Z = zo[i % 3]
            for kc in range(KC):
                k.op('pe', lambda e: e.matmul(P[:cw, :], W[:, kc, :cw], u[:, kc, tb * TB:(tb + 1) * TB], start=(kc == 0), stop=(kc == KC - 1)), [W, u], [P])
            if i % 2 == 0:
                k.op('act', lambda e: e.copy(out=Z[:cw, :], in_=P[:cw, :]), [P], [Z])
            else:
                k.op('dve', lambda e: e.tensor_copy(out=Z[:cw, :], in_=P[:cw, :]), [P], [Z])
            k.dma('sp', zT[cc * 128:cc * 128 + cw, tb * TB:(tb + 1) * TB], Z[:cw, :], rbuf=Z)
            i += 1
    print("A n_inst", k.n_inst)
    return k.finish()

D = 2048; KC = 16; FF = 8192; FC = 64
def build_C(ntok, last, TB=512):
    k = KB()
    xT = k.dram("xT", [D, ntok], F32, "ExternalInput")
    zgT = k.dram("zgT", [3 * D, ntok], F32, "ExternalInput")
    gb = k.dram("gb", [128, 48], F32, "ExternalInput")
    yT = k.dram("yT", [D, ntok], F32, "ExternalInput")
    w_up = k.dram("w_up", [D, D], F32, "ExternalInput")
    w_out = k.dram("w_out", [D, D], F32, "ExternalInput")
    nm = k.dram("nm", [128, KC], F32, "ExternalInput")
    w1 = k.dram("w1", [D, FF], F32, "ExternalInput")
    w2 = k.dram("w2", [FF, D], F32, "ExternalInput")
    fn = k.dram("fn", [128, KC], F32, "ExternalInput")
    gluw = k.dram("gluw", [768, 768], F32, "ExternalInput")
    glub = k.dram("glub", [128, 6], F32, "ExternalInput")
    xoT = k.dram("xoT", [D, ntok], F32, "ExternalOutput")
    ntb = ntok // TB
    X = k.sb("X", [128, KC, TB])
    YH = k.sb("YH", [128, KC, TB], BF16)
    M = k.sb("M", [128, KC, TB], BF16)
    ACTB = k.sb("ACTB", [128, FC, TB], BF16)
    wb = [k.sb(f"wb{i}", [128, KC, 128], BF16) for i in range(2)]
    w2b = [k.sb(f"w2b{i}", [128, 32, 128], BF16) for i in range(2)]
    G = [[k.sb(f"G{i}_{j}", [128, TB]) for j in range(3)] for i in range(2)]
    sq = [k.sb(f"sq{i}", [128, TB], BF16) for i in range(2)]
    tmp = [k.sb(f"tmp{i}", [128, TB]) for i in range(2)]
    mm = k.sb("mm", [128, TB])
    ones = k.sb("ones", [128, 128], BF16)
    gbt = k.sb("gbt", [128, 48]); nmt = k.sb("nmt", [128, KC]); fnt = k.sb("fnt", [128, KC]); glubt = k.sb("glubt", [128, 6])
    k.dma('sp', glubt[:], glub[:, :], wbuf=glubt)
    AF32 = ACTB[:].rearrange("p a b -> p (a b)").bitcast(F32)
    S5N = 6 * TB
    YA = k.view(ACTB, AF32[:, 0:S5N]); YG = k.view(ACTB, AF32[:, S5N:2 * S5N]); SQ = k.view(ACTB, AF32[:, 2 * S5N:3 * S5N])
    YGb = k.view(ACTB, ACTB[:].rearrange("p a b -> p (a b)")[:, 6 * S5N:7 * S5N])
    gluv = gluw.rearrange("(kc p) c -> p kc c", p=128)
    C1 = 0.7978845608028654; C2 = C1 * 0.044715
    rstd = k.sb("rstd", [128, TB])
    psn = k.ps("psn", [128, TB])
    pz = [k.ps(f"pz{i}", [128, TB]) for i in range(6)]
    k.op('pool', lambda e: e.memset(ones[:], 1.0), [], [ones])
    k.dma('sp', gbt[:], gb[:, :], wbuf=gbt); k.dma('sp', nmt[:], nm[:, :], wbuf=nmt); k.dma('sp', fnt[:], fn[:, :], wbuf=fnt)
    xv = xT.rearrange("(kc p) t -> p kc t", p=128); yv = yT.rearrange("(kc p) t -> p kc t", p=128)
    xov = xoT.rearrange("(kc p) t -> p kc t", p=128)
    wupv = w_up.rearrange("(kc p) c -> p kc c", p=128); woutv = w_out.rearrange("(kc p) c -> p kc c", p=128)
    w1v = w1.rearrange("(kc p) c -> p kc c", p=128); w2v = w2.rearrange("(fc p) c -> p fc c", p=128)
    BR = [(0, 6), (6, 11), (11, 16)]
    wi = 0; pi = 0
    def rmsn(gt_tile, dst_fn):
        for kc in range(KC):
            S = sq[kc % 2]
            k.op('act', lambda e: e.activation(out=S[:], in_=X[:, kc, :], func=AF.Square), [X], [S])
            k.op('pe', lambda e: e.matmul(psn[:], ones[:], S[:], start=(kc == 0), stop=(kc == KC - 1)), [ones, S], [psn])
        k.op('act', lambda e: e.activation(out=rstd[:], in_=psn[:], func=AF.Sqrt, scale=1.0 / D, bias=1e-6), [psn], [rstd])
        k.op('dve', lambda e: e.reciprocal(out=rstd[:], in_=rstd[:]), [rstd], [rstd])
    for tb in range(ntb):
        ts = slice(tb * TB, (tb + 1) * TB)
        k.dma('sp', X[:], xv[:, :, ts], wbuf=X)
        k.dma('pool', YH[:, 6:16, :], yv[:, 6:16, ts], wbuf=YH)
        k.dma('sp', YA[:].rearrange("p (a b) -> p a b", a=6), yv[:, 0:6, ts], wbuf=YA)
        k.op('pool', lambda e: e.tensor_tensor(out=SQ[:], in0=YA[:], in1=YA[:], op=ALU.mult), [YA], [SQ])
        k.op('dve', lambda e: e.tensor_scalar(out=SQ[:], in0=SQ[:], scalar1=C2, scalar2=C1, op0=ALU.mult, op1=ALU.add), [SQ], [SQ])
        k.op('dve', lambda e: e.tensor_tensor(out=SQ[:], in0=SQ[:], in1=YA[:], op=ALU.mult), [SQ, YA], [SQ])
        k.op('act', lambda e: e.activation(out=SQ[:], in_=SQ[:], func=AF.Tanh), [SQ], [SQ])
        k.op('dve', lambda e: e.tensor_scalar(out=SQ[:], in0=SQ[:], scalar1=0.5, scalar2=0.5, op0=ALU.mult, op1=ALU.add), [SQ], [SQ])
        k.op('dve', lambda e: e.tensor_tensor(out=YG[:], in0=SQ[:], in1=YA[:], op=ALU.mult), [SQ, YA], [YG])
        k.op('pool', lambda e: e.tensor_copy(out=YGb[:], in_=YG[:]), [YG], [YGb])
        for fo in range(6):
            W = wb[wi % 2]; wi += 1
            k.dma('pool', W[:, 0:6, :], gluv[:, :, fo * 128:(fo + 1) * 128], wbuf=W)
            P = pz[pi % 6]; pi += 1
            for kc in range(6):
                k.op('pe', lambda e: e.matmul(P[:], W[:, kc, :], YGb[:, kc * TB:(kc + 1) * TB], start=(kc == 0), stop=(kc == 5)), [W, YGb], [P])
            T = tmp[fo % 2]
            k.op('act', lambda e: e.activation(out=T[:], in_=P[:], func=AF.Sigmoid, bias=glubt[:, fo:fo + 1]), [P, glubt], [T])
            k.op('dve', lambda e: e.tensor_tensor(out=YH[:, fo, :], in0=YG[:, fo * TB:(fo + 1) * TB], in1=T[:], op=ALU.mult), [YG, T], [YH])
        for fo in range(KC):
            W = wb[wi % 2]; wi += 1
            k.dma('pool', W[:], wupv[:, :, fo * 128:(fo + 1) * 128], wbuf=W)
            Gs = G[fo % 2]
            for br in range(3):
                r0 = br * D + fo * 128
                k.dma('sp', Gs[br][:], zgT[r0:r0 + 128, ts], wbuf=Gs[br])
                k.op('act', lambda e: e.activation(out=Gs[br][:], in_=Gs[br][:], func=AF.Sigmoid, bias=gbt[:, br * 16 + fo:br * 16 + fo + 1]), [Gs[br], gbt], [Gs[br]])
            Ps = []
            for br in range(3):
                P = pz[pi % 6]; pi += 1; Ps.append(P)
                a, b = BR[br]
                for kc in range(a, b):
                    k.op('pe', lambda e: e.matmul(P[:], W[:, kc, :], YH[:, kc, :], start=(kc == a), stop=(kc == b - 1)), [W, YH], [P])
            k.op('dve', lambda e: e.tensor_tensor(out=mm[:], in0=Ps[0][:], in1=Gs[0][:], op=ALU.mult), [Ps[0], Gs[0]], [mm])
            T = tmp[0]
            k.op('dve', lambda e: e.tensor_tensor(out=T[:], in0=Ps[1][:], in1=Gs[1][:], op=ALU.mult), [Ps[1], Gs[1]], [T])
            k.op('dve', lambda e: e.tensor_tensor(out=mm[:], in0=mm[:], in1=T[:], op=ALU.add), [mm, T], [mm])
            T = tmp[1]
            k.op('dve', lambda e: e.tensor_tensor(out=T[:], in0=Ps[2][:], in1=Gs[2][:], op=ALU.mult), [Ps[2], Gs[2]], [T])
            k.op('dve', lambda e: e.tensor_tensor(out=M[:, fo, :], in0=mm[:], in1=T[:], op=ALU.add), [mm, T], [M])
        for fo in range(KC):
            W = wb[wi % 2]; wi += 1
            k.dma('pool', W[:], woutv[:, :, fo * 128:(fo + 1) * 128], wbuf=W)
            P = pz[pi % 6]; pi += 1
            for kc in range(KC):
                k.op('pe', lambda e: e.matmul(P[:], W[:, kc, :], M[:, kc, :], start=(kc == 0), stop=(kc == KC - 1)), [W, M], [P])
            k.op('dve', lambda e: e.tensor_tensor(out=X[:, fo, :], in0=X[:, fo, :], in1=P[:], op=ALU.add), [X, P], [X])
        rmsn(nmt, None)
        for kc in range(KC):
            k.op('dve', lambda e: e.scalar_tensor_tensor(out=YH[:, kc, :], in0=X[:, kc, :], scalar=nmt[:, kc:kc + 1], in1=rstd[:], op0=ALU.mult, op1=ALU.mult), [X, nmt, rstd], [YH])
        for fc in range(FC):
            W = wb[wi % 2]; wi += 1
            k.dma('pool', W[:], w1v[:, :, fc * 128:(fc + 1) * 128], wbuf=W)
            P = pz[pi % 6]; pi += 1
            for kc in range(KC):
                k.op('pe', lambda e: e.matmul(P[:], W[:, kc, :], YH[:, kc, :], start=(kc == 0), stop=(kc == KC - 1)), [W, YH], [P])
            T = tmp[fc % 2]
            k.op('act', lambda e: e.activation(out=T[:], in_=P[:], func=AF.Relu), [P], [T])
            k.op('pool', lambda e: e.tensor_tensor(out=ACTB[:, fc, :], in0=T[:], in1=T[:], op=ALU.mult), [T], [ACTB])
        for fo in range(KC):
            P = pz[pi % 6]; pi += 1
            for h in range(2):
                W = w2b[wi % 2]; wi += 1
                k.dma('pool', W[:], w2v[:, h * 32:(h + 1) * 32, fo * 128:(fo + 1) * 128], wbuf=W)
                for f in range(32):
                    fc = h * 32 + f
                    k.op('pe', lambda e: e.matmul(P[:], W[:, f, :], ACTB[:, fc, :], start=(fc == 0), stop=(fc == FC - 1)), [W, ACTB], [P])
            k.op('dve', lambda e: e.tensor_tensor(out=X[:, fo, :], in0=X[:, fo, :], in1=P[:], op=ALU.add), [X, P], [X])
        if last:
            rmsn(fnt, None)
            for kc in range(KC):
                k.op('dve', lambda e: e.scalar_tensor_tensor(out=X[:, kc, :], in0=X[:, kc, :], scalar=fnt[:, kc:kc + 1], in1=rstd[:], op0=ALU.mult, op1=ALU.mult), [X, fnt, rstd], [X])
        k.dma('sp', xov[:, :, ts], X[:], rbuf=X)
    print("C n_inst", k.n_inst)
    return k.finish()

PI = math.pi
PI = math.pi
def emit_S5(k, ntok, banks, TB=512):
    NP = 3
    zs = k.dram("s_zs", [96, ntok], F32, "ExternalInput")
    lamre = k.dram("s_lamre", [128, NP], F32, "ExternalInput")
    lamim = k.dram("s_lamim", [128, NP], F32, "ExternalInput")
    lstep = k.dram("s_lstep", [128, NP], F32, "ExternalInput")
    bre = k.dram("s_bre", [96, NP, 128], F32, "ExternalInput")
    bim = k.dram("s_bim", [96, NP, 128], F32, "ExternalInput")
    cre = k.dram("s_cre", [128, NP, 96], F32, "ExternalInput")
    cim = k.dram("s_cim", [128, NP, 96], F32, "ExternalInput")
    dvec = k.dram("s_dvec", [96, 1], F32, "ExternalInput")
    tau = k.dram("s_tau", [128, 128], F32, "ExternalInput")
    yo = k.dram("s_yo", [96, ntok], F32, "ExternalOutput")
    nb = ntok // TB; NCH = TB // 128
    sb = lambda n, shp, dt=F32: k.sb("s_" + n, shp, dt)
    lre = sb("lre", [128, NP]); lim = sb("lim", [128, NP]); lst = sb("lst", [128, NP])
    breT = sb("breT", [96, NP, 128]); bimT = sb("bimT", [96, NP, 128])
    creT = sb("creT", [128, NP, 96]); cimT = sb("cimT", [128, NP, 96]); ncreT = sb("ncreT", [128, NP, 96]); ncimT = sb("ncimT", [128, NP, 96])
    dT = sb("dT", [96, 1]); tauT = sb("tauT", [128, 128])
    gcon = k.group("s_con"); gz = [k.group(f"s_z{i}") for i in range(2)]; gy = [k.group(f"s_y{i}") for i in range(2)]
    for t, d in ((lre, lamre), (lim, lamim), (lst, lstep), (breT, bre), (bimT, bim), (creT, cre), (cimT, cim), (dT, dvec), (tauT, tau)):
        k.dma('sp', t[:], d, wbuf=t, grp=gcon)
    k.op('dve', lambda e: e.tensor_scalar(out=ncreT[:], in0=creT[:], scalar1=-1.0, scalar2=None, op0=ALU.mult), [creT], [ncreT])
    k.op('dve', lambda e: e.tensor_scalar(out=ncimT[:], in0=cimT[:], scalar1=-1.0, scalar2=None, op0=ALU.mult), [cimT], [ncimT])
    col = lambda n: sb(n, [128, NP])
    dt = col("dt"); lr = col("lr"); al = col("al"); om = col("om"); ea = col("ea"); cw = col("cw"); sw = col("sw")
    lbr = col("lbr"); lbi = col("lbi"); den = col("den"); t1 = col("t1"); t2 = col("t2"); qre = col("qre"); qim = col("qim")
    Rre = col("Rre"); Rim = col("Rim"); ang = col("ang")
    V = lambda fn, r, w: k.op('dve', fn, r, w)
    A_ = lambda fn, r, w: k.op('act', fn, r, w)
    A_(lambda e: e.activation(out=dt[:], in_=lst[:], func=AF.Exp), [lst], [dt])
    V(lambda e: e.tensor_scalar(out=lr[:], in0=lre[:], scalar1=-1e-4, scalar2=None, op0=ALU.min), [lre], [lr])
    V(lambda e: e.tensor_tensor(out=al[:], in0=lr[:], in1=dt[:], op=ALU.mult), [lr, dt], [al])
    V(lambda e: e.tensor_tensor(out=om[:], in0=lim[:], in1=dt[:], op=ALU.mult), [lim, dt], [om])
    A_(lambda e: e.activation(out=ea[:], in_=al[:], func=AF.Exp), [al], [ea])
    def sincos(dst_s, dst_c, src_ap, srcbufs, shape_ap_fn, scale):
        pass
    angi = sb("angi", [128, NP], mybir.dt.int32); angf = sb("angf", [128, NP])
    def sin_of(dst, src, tmpb, mult, phase):
        (db, da), (sbf, sa), (tb_, ta) = dst, src, tmpb
        V(lambda e: e.tensor_scalar(out=ta, in0=sa, scalar1=mult / (2 * PI), scalar2=phase / (2 * PI), op0=ALU.mult, op1=ALU.add), [sbf], [tb_])
        V(lambda e: e.tensor_copy(out=angi[:], in_=ta), [tb_], [angi])
        V(lambda e: e.tensor_copy(out=angf[:], in_=angi[:]), [angi], [angf])
        V(lambda e: e.tensor_tensor(out=ta, in0=ta, in1=angf[:], op=ALU.subtract), [tb_, angf], [tb_])
        A_(lambda e: e.activation(out=da, in_=ta, func=AF.Sin, scale=2 * PI), [tb_], [db])
    sin_of((sw, sw[:]), (om, om[:]), (ang, ang[:]), 1.0, 0.0)
    sin_of((cw, cw[:]), (om, om[:]), (ang, ang[:]), 1.0, PI / 2)
    sin_of((Rim, Rim[:]), (om, om[:]), (ang, ang[:]), 128.0, 0.0)
    sin_of((Rre, Rre[:]), (om, om[:]), (ang, ang[:]), 128.0, PI / 2)
    V(lambda e: e.tensor_tensor(out=lbr[:], in0=ea[:], in1=cw[:], op=ALU.mult), [ea, cw], [lbr])
    V(lambda e: e.tensor_tensor(out=lbi[:], in0=ea[:], in1=sw[:], op=ALU.mult), [ea, sw], [lbi])
    V(lambda e: e.tensor_scalar(out=lbr[:], in0=lbr[:], scalar1=-1.0, scalar2=None, op0=ALU.add), [lbr], [lbr])
    V(lambda e: e.tensor_tensor(out=den[:], in0=lr[:], in1=lr[:], op=ALU.mult), [lr], [den])
    V(lambda e: e.tensor_tensor(out=t1[:], in0=lim[:], in1=lim[:], op=ALU.mult), [lim], [t1])
    V(lambda e: e.tensor_tensor(out=den[:], in0=den[:], in1=t1[:], op=ALU.add), [den, t1], [den])
    V(lambda e: e.reciprocal(out=den[:], in_=den[:]), [den], [den])
    V(lambda e: e.tensor_tensor(out=t1[:], in0=lbr[:], in1=lr[:], op=ALU.mult), [lbr, lr], [t1])
    V(lambda e: e.tensor_tensor(out=t2[:], in0=lbi[:], in1=lim[:], op=ALU.mult), [lbi, lim], [t2])
    V(lambda e: e.tensor_tensor(out=t1[:], in0=t1[:], in1=t2[:], op=ALU.add), [t1, t2], [t1])
    V(lambda e: e.tensor_tensor(out=qre[:], in0=t1[:], in1=den[:], op=ALU.mult), [t1, den], [qre])
    V(lambda e: e.tensor_tensor(out=t1[:], in0=lbi[:], in1=lr[:], op=ALU.mult), [lbi, lr], [t1])
    V(lambda e: e.tensor_tensor(out=t2[:], in0=lbr[:], in1=lim[:], op=ALU.mult), [lbr, lim], [t2])
    V(lambda e: e.tensor_tensor(out=t1[:], in0=t1[:], in1=t2[:], op=ALU.subtract), [t1, t2], [t1])
    V(lambda e: e.tensor_tensor(out=qim[:], in0=t1[:], in1=den[:], op=ALU.mult), [t1, den], [qim])
    COS = [sb(f"COS{p}", [128, 128]) for p in range(NP)]; SIN = [sb(f"SIN{p}", [128, 128]) for p in range(NP)]
    ERE = [sb(f"ERE{p}", [128, 128]) for p in range(NP)]; EIM = [sb(f"EIM{p}", [128, 128]) for p in range(NP)]
    DEC = [sb(f"DEC{p}", [128, 128]) for p in range(NP)]
    th = sb("th", [128, 128]); thi = sb("thi", [128, 128], mybir.dt.int32); thf = sb("thf", [128, 128])
    for p in range(NP):
        for dstT, ph in ((SIN[p], 0.0), (COS[p], PI / 2)):
            V(lambda e: e.tensor_scalar(out=th[:], in0=tauT[:], scalar1=om[:, p:p + 1], scalar2=1.0 / (2 * PI), op0=ALU.mult, op1=ALU.mult), [tauT, om], [th])
            V(lambda e: e.tensor_scalar(out=th[:], in0=th[:], scalar1=ph / (2 * PI), scalar2=None, op0=ALU.add), [th], [th])
            V(lambda e: e.tensor_copy(out=thi[:], in_=th[:]), [th], [thi])
            V(lambda e: e.tensor_copy(out=thf[:], in_=thi[:]), [thi], [thf])
            V(lambda e: e.tensor_tensor(out=th[:], in0=th[:], in1=thf[:], op=ALU.subtract), [th, thf], [th])
            A_(lambda e: e.activation(out=dstT[:], in_=th[:], func=AF.Sin, scale=2 * PI), [th], [dstT])
        V(lambda e: e.tensor_scalar(out=th[:], in0=SIN[p][:], scalar1=qim[:, p:p + 1], scalar2=None, op0=ALU.mult), [SIN[p], qim], [th])
        V(lambda e: e.scalar_tensor_tensor(out=ERE[p][:], in0=COS[p][:], scalar=qre[:, p:p + 1], in1=th[:], op0=ALU.mult, op1=ALU.add), [COS[p], qre, th], [ERE[p]])
        V(lambda e: e.tensor_scalar(out=th[:], in0=SIN[p][:], scalar1=qre[:, p:p + 1], scalar2=None, op0=ALU.mult), [SIN[p], qre], [th])
        V(lambda e: e.scalar_tensor_tensor(out=EIM[p][:], in0=COS[p][:], scalar=qim[:, p:p + 1], in1=th[:], op0=ALU.mult, op1=ALU.subtract), [COS[p], qim, th], [EIM[p]])
        V(lambda e: e.tensor_scalar(out=DEC[p][:], in0=tauT[:], scalar1=0.0, scalar2=ea[:, p:p + 1], op0=ALU.mult, op1=ALU.add), [tauT, ea], [DEC[p]])
    zb = [sb(f"zb{i}", [96, TB]) for i in range(2)]
    PX = [[banks[0], banks[1]], [banks[0], banks[1]]]
    PY = [k.view(banks[2], banks[2][0:96, :]), k.view(banks[3], banks[3][0:96, :])]
    rr = [[sb(f"rr{p}_{c}", [128, TB]) for c in range(2)] for p in range(NP)]
    tm = [sb(f"tm{i}", [128, TB]) for i in range(2)]
    PP = [[sb(f"PP{i}_{j}", [128, TB]) for j in range(4)] for i in range(2)]
    ini = [[sb(f"ini{p}_{c}", [128, 1]) for c in range(2)] for p in range(NP)]
    it = [sb(f"it{p}", [128, 1]) for p in range(NP)]
    yout = [sb(f"yout{i}", [96, TB]) for i in range(2)]
    for p in range(NP):
        for c in range(2):
            k.op('pool', lambda e: e.memset(ini[p][c][:], 0.0), [], [ini[p][c]]); yield
    bc = lambda T: T[:].unsqueeze(1).to_broadcast([128, NCH, 128])
    v3 = lambda ap: ap.rearrange("p (c t) -> p c t", t=128)
    k.dma('sp', zb[0][:], zs[:, 0:TB], wbuf=zb[0], grp=gz[0]); yield
    for b in range(nb):
        Z = zb[b % 2]; ts = slice(b * TB, (b + 1) * TB)
        if b + 1 < nb:
            k.dma('sp', zb[(b + 1) % 2][:], zs[:, (b + 1) * TB:(b + 2) * TB], wbuf=zb[(b + 1) % 2], grp=gz[(b + 1) % 2]); yield
        PYb = PY[b % 2]
        for p in range(NP):
            Xr, Xi = PX[p % 2]
            k.op('pe', lambda e: e.matmul(Xr[:], breT[:, p, :], Z[:], start=True, stop=True), [breT, Z], [Xr]); yield
            k.op('pe', lambda e: e.matmul(Xi[:], bimT[:, p, :], Z[:], start=True, stop=True), [bimT, Z], [Xi]); yield
            Rr, Ri = rr[p]
            T0, T1 = tm
            V(lambda e: e.tensor_tensor(out=v3(T0[:]), in0=v3(Xr[:]), in1=bc(ERE[p]), op=ALU.mult), [Xr, ERE[p]], [T0]); yield
            V(lambda e: e.tensor_tensor(out=v3(T1[:]), in0=v3(Xi[:]), in1=bc(EIM[p]), op=ALU.mult), [Xi, EIM[p]], [T1]); yield
            k.op('pool', lambda e: e.tensor_tensor(out=Rr[:], in0=T0[:], in1=T1[:], op=ALU.subtract), [T0, T1], [Rr]); yield
            V(lambda e: e.tensor_tensor(out=v3(T0[:]), in0=v3(Xi[:]), in1=bc(ERE[p]), op=ALU.mult), [Xi, ERE[p]], [T0]); yield
            V(lambda e: e.tensor_tensor(out=v3(T1[:]), in0=v3(Xr[:]), in1=bc(EIM[p]), op=ALU.mult), [Xr, EIM[p]], [T1]); yield
            k.op('pool', lambda e: e.tensor_tensor(out=Ri[:], in0=T0[:], in1=T1[:], op=ALU.add), [T0, T1], [Ri]); yield
            for c in range(NCH):
                cs = slice(c * 128, (c + 1) * 128)
                Ir, Ii = ini[p]
                V(lambda e: e.tensor_tensor_scan(out=Rr[:, cs], data0=DEC[p][:], data1=Rr[:, cs], initial=Ir[:], op0=ALU.mult, op1=ALU.add), [DEC[p], Rr, Ir], [Rr]); yield
                V(lambda e: e.tensor_tensor_scan(out=Ri[:, cs], data0=DEC[p][:], data1=Ri[:, cs], initial=Ii[:], op0=ALU.mult, op1=ALU.add), [DEC[p], Ri, Ii], [Ri]); yield
                lastr = Rr[:, c * 128 + 127:c * 128 + 128]; lasti = Ri[:, c * 128 + 127:c * 128 + 128]
                V(lambda e: e.tensor_tensor(out=it[p][:], in0=lasti, in1=Rim[:, p:p + 1], op=ALU.mult), [Ri, Rim], [it[p]]); yield
                V(lambda e: e.scalar_tensor_tensor(out=Ir[:], in0=lastr, scalar=Rre[:, p:p + 1], in1=it[p][:], op0=ALU.mult, op1=ALU.subtract), [Rr, Rre, it[p]], [Ir]); yield
                V(lambda e: e.tensor_tensor(out=it[p][:], in0=lastr, in1=Rim[:, p:p + 1], op=ALU.mult), [Rr, Rim], [it[p]]); yield
                V(lambda e: e.scalar_tensor_tensor(out=Ii[:], in0=lasti, scalar=Rre[:, p:p + 1], in1=it[p][:], op0=ALU.mult, op1=ALU.add), [Ri, Rre, it[p]], [Ii]); yield
            P1, P2, P3, P4 = PP[p % 2]
            k.op('pool', lambda e: e.tensor_tensor(out=v3(P1[:]), in0=v3(Rr[:]), in1=bc(COS[p]), op=ALU.mult), [Rr, COS[p]], [P1]); yield
            k.op('pool', lambda e: e.tensor_tensor(out=v3(P2[:]), in0=v3(Ri[:]), in1=bc(SIN[p]), op=ALU.mult), [Ri, SIN[p]], [P2]); yield
            k.op('pool', lambda e: e.tensor_tensor(out=v3(P3[:]), in0=v3(Rr[:]), in1=bc(SIN[p]), op=ALU.mult), [Rr, SIN[p]], [P3]); yield
            k.op('pool', lambda e: e.tensor_tensor(out=v3(P4[:]), in0=v3(Ri[:]), in1=bc(COS[p]), op=ALU.mult), [Ri, COS[p]], [P4]); yield
            k.op('pe', lambda e: e.matmul(PYb[:], creT[:, p, :], P1[:], start=(p == 0), stop=False), [creT, P1], [PYb]); yield
            k.op('pe', lambda e: e.matmul(PYb[:], ncreT[:, p, :], P2[:], start=False, stop=False), [ncreT, P2], [PYb]); yield
            k.op('pe', lambda e: e.matmul(PYb[:], ncimT[:, p, :], P3[:], start=False, stop=False), [ncimT, P3], [PYb]); yield
            k.op('pe', lambda e: e.matmul(PYb[:], ncimT[:, p, :], P4[:], start=False, stop=(p == NP - 1)), [ncimT, P4], [PYb]); yield
        YO = yout[b % 2]
        V(lambda e: e.scalar_tensor_tensor(out=YO[:], in0=Z[:], scalar=dT[:, 0:1], in1=PYb[:], op0=ALU.mult, op1=ALU.add), [Z, dT, PYb], [YO]); yield
        k.dma('sp', yo[:, ts], YO[:], rbuf=YO, grp=gy[b % 2]); yield

def s5_layout(core, zT_s5, lam_re, lam_im, log_step, b_re, b_im, c_re, c_im, d):
    import numpy as np
    g0 = 6 * core
    f = np.float32
    lamre = np.zeros((128, 3), f); lamim = np.zeros((128, 3), f); lstep = np.zeros((128, 3), f)
    bre = np.zeros((96, 3, 128), f); bim = np.zeros((96, 3, 128), f); cre = np.zeros((128, 3, 96), f); cim = np.zeros((128, 3, 96), f)
    for p in range(3):
        for gl in range(2):
            g = g0 + 2 * p + gl; gi = 2 * p + gl
            lamre[gl * 64:(gl + 1) * 64, p] = lam_re[g]; lamim[gl * 64:(gl + 1) * 64, p] = lam_im[g]; lstep[gl * 64:(gl + 1) * 64, p] = log_step[g]
            bre[gi * 16:(gi + 1) * 16, p, gl * 64:(gl + 1) * 64] = b_re[g].T
            bim[gi * 16:(gi + 1) * 16, p, gl * 64:(gl + 1) * 64] = b_im[g].T
            cre[gl * 64:(gl + 1) * 64, p, gi * 16:(gi + 1) * 16] = c_re[g].T
            cim[gl * 64:(gl + 1) * 64, p, gi * 16:(gi + 1) * 16] = c_im[g].T
    return {"s_" + n_: v_ for n_, v_ in {"zs": np.ascontiguousarray(zT_s5[g0 * 16:(g0 + 6) * 16]), "lamre": lamre, "lamim": lamim, "lstep": lstep, "bre": bre, "bim": bim,
            "cre": cre, "cim": cim, "dvec": np.ascontiguousarray(d[g0 * 16:(g0 + 6) * 16].reshape(96, 1)),
            "tau": np.broadcast_to(np.arange(128, dtype=f), (128, 128)).copy()}.items()}

def gla_consts():
    import numpy as np
    s = np.arange(128)[:, None]; t = np.arange(128)[None, :]
    same = (s // 64) == (t // 64)
    f = np.float32
    return {"tri_le": (same & (s <= t)).astype(f), "tri_gt": (same & (s > t)).astype(f),
            "chind": np.stack([(np.arange(128) < 64), (np.arange(128) >= 64)], 1).astype(f)}
def emit_GLA(k, ntok, banks, pfx="g_"):
    D_ = lambda n, shp, kind="ExternalInput": k.dram(pfx + n, shp, F32, kind)
    qT = D_("qT", [64, ntok]); kT = D_("kT", [64, ntok]); ktok = D_("ktok", [ntok, 64]); vtok = D_("vtok", [ntok, 128]); gtok = D_("gtok", [ntok, 128])
    ainT = D_("ainT", [16, ntok]); alora = D_("alora", [16, 64]); abias = D_("abias", [128, 64]); normg = D_("normg", [128, 128])
    tri_le = D_("tri_le", [128, 128]); tri_gt = D_("tri_gt", [128, 128]); chind = D_("chind", [128, 2])
    otok = D_("otok", [ntok, 128], "ExternalOutput")
    sb = lambda n, shp, dt=F32: k.sb(pfx + n, shp, dt)
    aloraT = sb("aloraT", [16, 64]); abiasT = sb("abiasT", [128, 64]); normgT = sb("normgT", [128, 128])
    TLE = sb("TLE", [128, 128]); TGT = sb("TGT", [128, 128]); CHI = sb("CHI", [128, 2])
    gcon = k.group(pfx + "con"); gl = [k.group(pfx + f"l{i}") for i in range(2)]; go = [k.group(pfx + f"o{i}") for i in range(2)]
    for t, d in ((aloraT, alora), (abiasT, abias), (normgT, normg), (TLE, tri_le), (TGT, tri_gt), (CHI, chind)):
        k.dma('sp', t[:], d[:, :], wbuf=t, grp=gcon)
    NB = 2
    QT = [sb(f"QT{i}", [64, 128]) for i in range(NB)]; KT = [sb(f"KT{i}", [64, 128]) for i in range(NB)]
    KK = [sb(f"KK{i}", [128, 64]) for i in range(NB)]; VV = [sb(f"VV{i}", [128, 128]) for i in range(NB)]; GG = [sb(f"GG{i}", [128, 128]) for i in range(NB)]
    AIN = [sb(f"AIN{i}", [16, 128]) for i in range(NB)]
    xb = sb("xb", [128, 64]); ax = sb("ax", [128, 64]); la = sb("la", [128, 64]); mn = sb("mn", [128, 64])
    EP = sb("EP", [64, 128]); EN = sb("EN", [64, 128]); ER = sb("ER", [128, 64])
    QF = sb("QF", [64, 128]); KF = sb("KF", [64, 128]); KH = sb("KH", [128, 64])
    Q0 = sb("Q0", [64, 128]); Q1 = sb("Q1", [64, 128])
    NM = sb("NM", [128, 128]); EB = sb("EB", [64, 2])
    S = [sb(f"S{i}", [64, 128]) for i in range(3)]
    ss = sb("ss", [128, 1]); junk = sb("junk", [128, 128]); SG = sb("SG", [128, 128]); O1 = sb("O1", [128, 128])
    OO = [sb(f"OO{i}", [128, 128]) for i in range(2)]
    bankA = [banks[0]]; bankB = [banks[1]]; bankC = [banks[2]]
    def views(i):
        A, B, C = bankA[i], bankB[i], bankC[i]
        return dict(px=k.view(A, A[:, 0:64]), pc=k.view(A, A[:, 64:128]), pr=k.view(A, A[:, 128:192]),
                    pbl=k.view(A, A[0:64, 192:194]), pcT=k.view(A, A[0:64, 256:384]),
                    pN=k.view(B, B[:, 0:128]), pO=k.view(B, B[:, 128:256]), pS0=k.view(C, C[0:64, 0:128]), pS1=k.view(C, C[0:64, 128:256]))
    PV = [views(0), views(0)]
    V = lambda fn, r, w: k.op('dve', fn, r, w)
    A_ = lambda fn, r, w: k.op('act', fn, r, w)
    P_ = lambda fn, r, w: k.op('pool', fn, r, w)
    T_ = lambda fn, r, w: k.op('pe', fn, r, w)
    P_(lambda e: e.memset(Q0[:], 0.0), [], [Q0]); P_(lambda e: e.memset(Q1[:], 0.0), [], [Q1]); P_(lambda e: e.memset(S[0][:], 0.0), [], [S[0]])
    si = 0
    def load(ti):
        b = ti % NB; r0 = ti * 128
        k.dma('sp', QT[b][:], qT[:, r0:r0 + 128], wbuf=QT[b], grp=gl[b]); k.dma('sp', KT[b][:], kT[:, r0:r0 + 128], wbuf=KT[b], grp=gl[b])
        k.dma('sp', KK[b][:], ktok[r0:r0 + 128, :], wbuf=KK[b], grp=gl[b]); k.dma('sp', VV[b][:], vtok[r0:r0 + 128, :], wbuf=VV[b], grp=gl[b])
        k.dma('sp', GG[b][:], gtok[r0:r0 + 128, :], wbuf=GG[b], grp=gl[b]); k.dma('sp', AIN[b][:], ainT[:, r0:r0 + 128], wbuf=AIN[b], grp=gl[b])
    load(0)
    for ti in range(ntok // 128):
        b = ti % NB; r0 = ti * 128; pv = PV[ti % 2]
        if ti + 1 < ntok // 128: load(ti + 1)
        px, pc, pr, pbl, pcT, pN, pO, pS0, pS1 = (pv[n] for n in ("px", "pc", "pr", "pbl", "pcT", "pN", "pO", "pS0", "pS1"))
        T_(lambda e: e.matmul(px[:], AIN[b][:], aloraT[:], start=True, stop=True), [AIN[b], aloraT], [px]); yield
        V(lambda e: e.tensor_tensor(out=xb[:], in0=px[:], in1=abiasT[:], op=ALU.add), [px, abiasT], [xb]); yield
        A_(lambda e: e.activation(out=ax[:], in_=xb[:], func=AF.Abs), [xb], [ax]); yield
        A_(lambda e: e.activation(out=ax[:], in_=ax[:], func=AF.Exp, scale=-1.0), [ax], [ax]); yield
        A_(lambda e: e.activation(out=ax[:], in_=ax[:], func=AF.Ln, bias=1.0), [ax], [ax]); yield
        V(lambda e: e.tensor_scalar(out=mn[:], in0=xb[:], scalar1=0.0, scalar2=None, op0=ALU.min), [xb], [mn]); yield
        V(lambda e: e.tensor_tensor(out=la[:], in0=mn[:], in1=ax[:], op=ALU.subtract), [mn, ax], [la]); yield
        T_(lambda e: e.matmul(pcT[:], la[:], TLE[:], start=True, stop=True), [la, TLE], [pcT]); yield
        T_(lambda e: e.matmul(pr[:], TGT[:], la[:], start=True, stop=True), [la, TGT], [pr]); yield
        T_(lambda e: e.matmul(pbl[:], la[:], CHI[:], start=True, stop=True), [la, CHI], [pbl]); yield
        A_(lambda e: e.activation(out=EP[:], in_=pcT[:], func=AF.Exp, scale=1.0 / 16, bias=math.log(0.125)), [pcT], [EP]); yield
        A_(lambda e: e.activation(out=EN[:], in_=pcT[:], func=AF.Exp, scale=-1.0 / 16), [pcT], [EN]); yield
        A_(lambda e: e.activation(out=ER[:], in_=pr[:], func=AF.Exp, scale=1.0 / 16), [pr], [ER]); yield
        A_(lambda e: e.activation(out=EB[:], in_=pbl[:], func=AF.Exp, scale=1.0 / 16), [pbl], [EB]); yield
        V(lambda e: e.tensor_tensor(out=QF[:], in0=QT[b][:], in1=EP[:], op=ALU.mult), [QT[b], EP], [QF]); yield
        V(lambda e: e.tensor_tensor(out=KF[:], in0=KT[b][:], in1=EN[:], op=ALU.mult), [KT[b], EN], [KF]); yield
        V(lambda e: e.tensor_tensor(out=KH[:], in0=KK[b][:], in1=ER[:], op=ALU.mult), [KK[b], ER], [KH]); yield
        P_(lambda e: e.tensor_copy(out=Q0[:, 0:64], in_=QF[:, 0:64]), [QF], [Q0]); yield
        P_(lambda e: e.tensor_copy(out=Q1[:, 64:128], in_=QF[:, 64:128]), [QF], [Q1]); yield
        T_(lambda e: e.matmul(pN[:], KF[:], QF[:], start=True, stop=True), [KF, QF], [pN]); yield
        V(lambda e: e.tensor_tensor(out=NM[:], in0=pN[:], in1=TLE[:], op=ALU.mult), [pN, TLE], [NM]); yield
        S0 = S[si % 3]; S1 = S[(si + 1) % 3]; S2 = S[(si + 2) % 3]; si += 2
        T_(lambda e: e.matmul(pO[:], NM[:], VV[b][:], start=True, stop=False), [NM, VV[b]], [pO]); yield
        T_(lambda e: e.matmul(pO[:], Q0[:], S0[:], start=False, stop=False), [Q0, S0], [pO]); yield
        T_(lambda e: e.matmul(pS0[:], KH[0:64, :], VV[b][0:64, :], start=True, stop=True), [KH, VV[b]], [pS0]); yield
        V(lambda e: e.scalar_tensor_tensor(out=S1[:], in0=S0[:], scalar=EB[:, 0:1], in1=pS0[:], op0=ALU.mult, op1=ALU.add), [S0, EB, pS0], [S1]); yield
        T_(lambda e: e.matmul(pO[:], Q1[:], S1[:], start=False, stop=True), [Q1, S1], [pO]); yield
        T_(lambda e: e.matmul(pS1[:], KH[64:128, :], VV[b][64:128, :], start=True, stop=True), [KH, VV[b]], [pS1]); yield
        V(lambda e: e.scalar_tensor_tensor(out=S2[:], in0=S1[:], scalar=EB[:, 1:2], in1=pS1[:], op0=ALU.mult, op1=ALU.add), [S1, EB, pS1], [S2]); yield
        A_(lambda e: e.activation(out=junk[:], in_=pO[:], func=AF.Square, accum_out=ss[:]), [pO], [junk, ss]); yield
        A_(lambda e: e.activation(out=ss[:], in_=ss[:], func=AF.Sqrt, scale=1.0 / 128, bias=1e-6), [ss], [ss]); yield
        V(lambda e: e.reciprocal(out=ss[:], in_=ss[:]), [ss], [ss]); yield
        A_(lambda e: e.activation(out=SG[:], in_=GG[b][:], func=AF.Silu), [GG[b]], [SG]); yield
        V(lambda e: e.scalar_tensor_tensor(out=O1[:], in0=pO[:], scalar=ss[:, 0:1], in1=normgT[:], op0=ALU.mult, op1=ALU.mult), [pO, ss, normgT], [O1]); yield
        OB = OO[ti % 2]
        V(lambda e: e.tensor_tensor(out=OB[:], in0=O1[:], in1=SG[:], op=ALU.mult), [O1, SG], [OB]); yield
        k.dma('sp', otok[r0:r0 + 128, :], OB[:], rbuf=OB, grp=go[ti % 2]); yield

def gla_layout(h, z_gla, alpha_lora, alpha_bias, norm_g):
    import numpy as np
    f = np.float32; ntok = z_gla.shape[0]
    d = dict(gla_consts())
    if h is None:
        z = lambda *s: np.zeros(s, f)
        d.update(qT=z(64, ntok), kT=z(64, ntok), ktok=z(ntok, 64), vtok=z(ntok, 128), gtok=z(ntok, 128), ainT=z(16, ntok), alora=z(16, 64), abias=z(128, 64), normg=z(128, 128))
    else:
        q = z_gla[:, h * 64:(h + 1) * 64]; kk = z_gla[:, 320 + h * 64:320 + (h + 1) * 64]
        v = z_gla[:, 640 + h * 128:640 + (h + 1) * 128]; g = z_gla[:, 1280 + h * 128:1280 + (h + 1) * 128]; a = z_gla[:, 1920:1936]
        c = np.ascontiguousarray
        d.update(qT=c(q.T), kT=c(kk.T), ktok=c(kk), vtok=c(v), gtok=c(g), ainT=c(a.T), alora=c(alpha_lora[:, h * 64:(h + 1) * 64]),
                 abias=c(np.broadcast_to(alpha_bias[h * 64:(h + 1) * 64], (128, 64))), normg=c(np.broadcast_to(norm_g[h * 128:(h + 1) * 128], (128, 128))))
    return {"g_" + n: v for n, v in d.items()}

RW_SHARED = ("loraT", "mu_w", "mu_a", "mu_g", "tri_le", "tri_gt", "mask4", "ident", "chind", "vresT", "mu_v")
def rw_consts():
    import numpy as np
    s = np.arange(128)[:, None]; t = np.arange(128)[None, :]
    same = (s // 64) == (t // 64)
    f = np.float32
    tle = (same & (s <= t)).astype(f); tlt = (same & (s < t)).astype(f); tgt = (same & (s > t)).astype(f)
    return {"tri_le": tle, "tri_gt": tgt, "mask4": np.concatenate([tlt, tle, tlt, tle], 1),
            "ident": np.eye(128, dtype=f), "chind": np.stack([(np.arange(128) < 64), (np.arange(128) >= 64)], 1).astype(f)}

NLEV = 5
def emit_RW(k, ntok, pfx, has_vres, banks, shared):
    def D_(n, shp, kind="ExternalInput"):
        if n in RW_SHARED:
            if n not in shared: shared[n] = k.dram("rs_" + n, shp, F32, kind)
            return shared[n]
        return k.dram(pfx + n, shp, F32, kind)
    rkv = D_("rkv", [ntok + 1, 192]); mu_rkv = D_("mu_rkv", [128, 192])
    loraT = D_("loraT", [480, ntok + 1]); mu_w = D_("mu_w", [96, 1]); mu_a = D_("mu_a", [128, 1]); mu_g = D_("mu_g", [128, 2])
    w_lora = D_("w_lora", [96, 64]); a_lora = D_("a_lora", [128, 64]); g_lora = D_("g_lora", [128, 2, 64])
    bc5 = D_("bc5", [128, 7, 64])
    tri_le = D_("tri_le", [128, 128]); tri_gt = D_("tri_gt", [128, 128]); mask4 = D_("mask4", [128, 512]); ident = D_("ident", [128, 128]); chind = D_("chind", [128, 2])
    if has_vres:
        vfirst = D_("vfirst", [ntok, 64]); vresT = D_("vresT", [64, ntok + 1]); mu_v = D_("mu_v", [64, 1]); vres_b = D_("vres_b", [64, 64]); vbias = D_("vbias", [128, 64])
    ytok = D_("ytok", [ntok, 64], "ExternalOutput")
    if not has_vres:
        vout = D_("vout", [ntok, 64], "ExternalOutput")
    sb = lambda n, shp, dt=F32: k.sb(pfx + n, shp, dt)
    MU = sb("MU", [128, 192]); MUW = sb("MUW", [96, 1]); MUA = sb("MUA", [128, 1]); MUG = sb("MUG", [128, 2])
    WL = sb("WL", [96, 64]); AL = sb("AL", [128, 64]); GL = sb("GL", [128, 2, 64]); BC = sb("BC", [128, 7, 64])
    TLE = sb("TLE", [128, 128]); TGT = sb("TGT", [128, 128]); M4 = sb("M4", [128, 512]); ID = sb("ID", [128, 128]); CHI = sb("CHI", [128, 2])
    loads = [(MU, mu_rkv), (MUW, mu_w), (MUA, mu_a), (MUG, mu_g), (WL, w_lora), (AL, a_lora), (GL, g_lora), (BC, bc5), (TLE, tri_le), (TGT, tri_gt), (M4, mask4), (ID, ident), (CHI, chind)]
    if has_vres:
        MUV = sb("MUV", [64, 1]); VB = sb("VB", [64, 64]); VBI = sb("VBI", [128, 64])
        loads += [(MUV, mu_v), (VB, vres_b), (VBI, vbias)]
    gcon = k.group(pfx + "con"); gl = [k.group(pfx + f"l{i}") for i in range(2)]; go = [k.group(pfx + f"o{i}") for i in range(2)]
    for t, d in loads:
        k.dma('sp', t[:], d, wbuf=t, grp=gcon)
    W0, A0, KKW, KA, RK, LNW, LNB = (BC[:, i, :] for i in range(7))
    NB = 2
    CUR = [sb(f"CUR{i}", [128, 192]) for i in range(NB)]; PRV = [sb(f"PRV{i}", [128, 192]) for i in range(NB)]
    LWc = [sb(f"LWc{i}", [96, 129]) for i in range(NB)]; LAc = [sb(f"LAc{i}", [128, 129]) for i in range(NB)]; LGc = [sb(f"LGc{i}", [128, 2, 129]) for i in range(NB)]
    if has_vres:
        VF = [sb(f"VF{i}", [128, 64]) for i in range(NB)]; VRc = [sb(f"VRc{i}", [64, 129]) for i in range(NB)]
        vS = sb("vS", [64, 128]); vD = sb("vD", [64, 128]); xv = sb("xv", [128, 64])
    Z = sb("Z", [128, 192]); Dz = sb("Dz", [128, 192])
    wS = sb("wS", [96, 128]); wD = sb("wD", [96, 128]); aS = sb("aS", [128, 128]); aD = sb("aD", [128, 128]); gS = sb("gS", [128, 2, 128]); gD = sb("gD", [128, 2, 128])
    xw = sb("xw", [128, 64]); t64 = [sb(f"t64_{i}", [128, 64]) for i in range(6)]
    ew = sb("ew", [128, 64]); asg = sb("asg", [128, 64]); gg = sb("gg", [128, 64]); VP = sb("VP", [128, 64])
    kk = sb("kk", [128, 64]); kkn = sb("kkn", [128, 64]); bv = sb("bv", [128, 64]); k2 = sb("k2", [128, 64])
    col = [sb(f"col{i}", [128, 1]) for i in range(6)]; junk = sb("junk", [128, 64])
    Em = sb("Em", [128, 64]); Ep = sb("Ep", [128, 64]); Eme = sb("Eme", [128, 64]); Er = sb("Er", [128, 64]); cex = sb("cex", [128, 64])
    T4 = sb("T4", [128, 4, 64])
    KH = sb("KH", [128, 64]); BH = sb("BH", [128, 64]); PC = sb("PC", [64, 2])
    FT = sb("FT", [64, 512])
    AT0 = sb("AT0", [64, 128]); AT1 = sb("AT1", [64, 128]); RT0 = sb("RT0", [64, 128]); RT1 = sb("RT1", [64, 128])
    NM = sb("NM", [128, 512]); Ncur = [sb(f"Ncur{i}", [128, 128]) for i in range(2)]; Lcur = [sb(f"Lcur{i}", [128, 128]) for i in range(2)]
    X = sb("X", [128, 128]); Y = sb("Y", [128, 128])
    RHS = sb("RHS", [128, 64]); U = sb("U", [128, 64])
    S = [sb(f"S{i}", [64, 64]) for i in range(3)]
    yc = sb("yc", [128, 64]); yn = sb("yn", [128, 64]); YO = [sb(f"YO{i}", [128, 64]) for i in range(2)]; VO = [sb(f"VO{i}", [128, 64]) for i in range(2)]
    B0, B1, B2, B3 = banks
    vw = k.view
    pw = vw(B0, B0[:, 0:64]); pa = vw(B0, B0[:, 64:128]); pg = vw(B0, B0[:, 128:192]); pvg = vw(B0, B0[:, 192:256])
    pce = vw(B0, B0[:, 256:320]); prem = vw(B0, B0[:, 320:384]); pPC = vw(B0, B0[0:64, 384:386])
    pT = vw(B1, B1[0:64, :]); pNM = B1
    pL = vw(B2, B2[:, 0:128]); pN2 = vw(B2, B2[:, 128:256]); pL2 = vw(B2, B2[:, 256:384]); pXu = vw(B2, B2[:, 384:512])
    pYu = vw(B3, B3[:, 0:128]); pR = vw(B3, B3[:, 128:192]); pU = vw(B3, B3[:, 192:256]); pS = vw(B3, B3[0:64, 256:320]); pYo = vw(B3, B3[:, 320:384])
    V = lambda fn, r, w: k.op('dve', fn, r, w)
    A_ = lambda fn, r, w: k.op('act', fn, r, w)
    P_ = lambda fn, r, w: k.op('pool', fn, r, w)
    T_ = lambda fn, r, w: k.op('pe', fn, r, w)
    for t in (AT0, AT1, RT0, RT1, S[0]):
        P_(lambda e: e.memset(t[:], 0.0), [], [t])
    si = 0
    def load(ti):
        b = ti % NB; r0 = ti * 128
        k.dma('sp', CUR[b][:], rkv[r0 + 1:r0 + 129, :], wbuf=CUR[b], grp=gl[b]); k.dma('sp', PRV[b][:], rkv[r0:r0 + 128, :], wbuf=PRV[b], grp=gl[b])
        k.dma('sp', LWc[b][:], loraT[0:96, r0:r0 + 129], wbuf=LWc[b], grp=gl[b]); k.dma('sp', LAc[b][:], loraT[96:224, r0:r0 + 129], wbuf=LAc[b], grp=gl[b])
        k.dma('sp', LGc[b][:], loraT[224:480, r0:r0 + 129].rearrange("(c p) t -> p c t", p=128), wbuf=LGc[b], grp=gl[b])
        if has_vres:
            k.dma('sp', VF[b][:], vfirst[r0:r0 + 128, :], wbuf=VF[b], grp=gl[b]); k.dma('sp', VRc[b][:], vresT[:, r0:r0 + 129], wbuf=VRc[b], grp=gl[b])
    load(0)
    for ti in range(ntok // 128):
        b = ti % NB; r0 = ti * 128
        if ti + 1 < ntok // 128: load(ti + 1)
        V(lambda e: e.tensor_tensor(out=Dz[:], in0=PRV[b][:], in1=CUR[b][:], op=ALU.subtract), [PRV[b], CUR[b]], [Dz]); yield
        V(lambda e: e.tensor_tensor(out=Dz[:], in0=Dz[:], in1=MU[:], op=ALU.mult), [Dz, MU], [Dz]); yield
        V(lambda e: e.tensor_tensor(out=Z[:], in0=Dz[:], in1=CUR[b][:], op=ALU.add), [Dz, CUR[b]], [Z]); yield
        r_ = Z[:, 0:64]; k_ = Z[:, 64:128]; v_ = Z[:, 128:192]
        P_(lambda e: e.tensor_tensor(out=wD[:], in0=LWc[b][:, 0:128], in1=LWc[b][:, 1:129], op=ALU.subtract), [LWc[b]], [wD]); yield
        V(lambda e: e.scalar_tensor_tensor(out=wS[:], in0=wD[:], scalar=MUW[:, 0:1], in1=LWc[b][:, 1:129], op0=ALU.mult, op1=ALU.add), [wD, MUW, LWc[b]], [wS]); yield
        P_(lambda e: e.tensor_tensor(out=aD[:], in0=LAc[b][:, 0:128], in1=LAc[b][:, 1:129], op=ALU.subtract), [LAc[b]], [aD]); yield
        V(lambda e: e.scalar_tensor_tensor(out=aS[:], in0=aD[:], scalar=MUA[:, 0:1], in1=LAc[b][:, 1:129], op0=ALU.mult, op1=ALU.add), [aD, MUA, LAc[b]], [aS]); yield
        for c in range(2):
            P_(lambda e: e.tensor_tensor(out=gD[:, c, :], in0=LGc[b][:, c, 0:128], in1=LGc[b][:, c, 1:129], op=ALU.subtract), [LGc[b]], [gD]); yield
            V(lambda e: e.scalar_tensor_tensor(out=gS[:, c, :], in0=gD[:, c, :], scalar=MUG[:, c:c + 1], in1=LGc[b][:, c, 1:129], op0=ALU.mult, op1=ALU.add), [gD, MUG, LGc[b]], [gS]); yield
        A_(lambda e: e.activation(out=wS[:], in_=wS[:], func=AF.Tanh), [wS], [wS]); yield
        A_(lambda e: e.activation(out=gS[:], in_=gS[:], func=AF.Sigmoid), [gS], [gS]); yield
        T_(lambda e: e.matmul(pw[:], wS[:], WL[:], start=True, stop=True), [wS, WL], [pw]); yield
        T_(lambda e: e.matmul(pa[:], aS[:], AL[:], start=True, stop=True), [aS, AL], [pa]); yield
        T_(lambda e: e.matmul(pg[:], gS[:, 0, :], GL[:, 0, :], start=True, stop=False), [gS, GL], [pg]); yield
        T_(lambda e: e.matmul(pg[:], gS[:, 1, :], GL[:, 1, :], start=False, stop=True), [gS, GL], [pg]); yield
        if has_vres:
            P_(lambda e: e.tensor_tensor(out=vD[:], in0=VRc[b][:, 0:128], in1=VRc[b][:, 1:129], op=ALU.subtract), [VRc[b]], [vD]); yield
            V(lambda e: e.scalar_tensor_tensor(out=vS[:], in0=vD[:], scalar=MUV[:, 0:1], in1=VRc[b][:, 1:129], op0=ALU.mult, op1=ALU.add), [vD, MUV, VRc[b]], [vS]); yield
            T_(lambda e: e.matmul(pvg[:], vS[:], VB[:], start=True, stop=True), [vS, VB], [pvg]); yield
        V(lambda e: e.tensor_tensor(out=xw[:], in0=pw[:], in1=W0, op=ALU.add), [pw, BC], [xw]); yield
        ax, mn, ta = t64[0], t64[1], t64[2]
        A_(lambda e: e.activation(out=ax[:], in_=xw[:], func=AF.Abs), [xw], [ax]); yield
        A_(lambda e: e.activation(out=ax[:], in_=ax[:], func=AF.Exp, scale=-1.0), [ax], [ax]); yield
        A_(lambda e: e.activation(out=ax[:], in_=ax[:], func=AF.Ln, bias=1.0), [ax], [ax]); yield
        V(lambda e: e.tensor_scalar(out=mn[:], in0=xw[:], scalar1=0.0, scalar2=None, op0=ALU.min), [xw], [mn]); yield
        V(lambda e: e.tensor_tensor(out=mn[:], in0=mn[:], in1=ax[:], op=ALU.subtract), [mn, ax], [mn]); yield
        A_(lambda e: e.activation(out=ew[:], in_=mn[:], func=AF.Exp, bias=-0.5), [mn], [ew]); yield
        V(lambda e: e.tensor_tensor(out=ta[:], in0=pa[:], in1=A0, op=ALU.add), [pa, BC], [ta]); yield
        A_(lambda e: e.activation(out=asg[:], in_=ta[:], func=AF.Sigmoid), [ta], [asg]); yield
        A_(lambda e: e.copy(out=gg[:], in_=pg[:]), [pg], [gg]); yield
        if has_vres:
            V(lambda e: e.tensor_tensor(out=xv[:], in0=pvg[:], in1=VBI[:], op=ALU.add), [pvg, VBI], [xv]); yield
            A_(lambda e: e.activation(out=xv[:], in_=xv[:], func=AF.Sigmoid), [xv], [xv]); yield
            V(lambda e: e.tensor_tensor(out=VP[:], in0=VF[b][:], in1=v_, op=ALU.subtract), [VF[b], Z], [VP]); yield
            V(lambda e: e.tensor_tensor(out=VP[:], in0=VP[:], in1=xv[:], op=ALU.mult), [VP, xv], [VP]); yield
            V(lambda e: e.tensor_tensor(out=VP[:], in0=VP[:], in1=v_, op=ALU.add), [VP, Z], [VP]); yield
        else:
            P_(lambda e: e.tensor_copy(out=VP[:], in_=v_), [Z], [VP]); yield
        ssq, rn, bsum, s1, s2, rstd = col
        V(lambda e: e.tensor_tensor(out=kk[:], in0=k_, in1=KKW, op=ALU.mult), [Z, BC], [kk]); yield
        A_(lambda e: e.activation(out=junk[:], in_=kk[:], func=AF.Square, accum_out=ssq[:]), [kk], [junk, ssq]); yield
        V(lambda e: e.tensor_scalar(out=rn[:], in0=ssq[:], scalar1=1e-24, scalar2=None, op0=ALU.max), [ssq], [rn]); yield
        A_(lambda e: e.activation(out=rn[:], in_=rn[:], func=AF.Sqrt), [rn], [rn]); yield
        V(lambda e: e.reciprocal(out=rn[:], in_=rn[:]), [rn], [rn]); yield
        V(lambda e: e.tensor_scalar(out=kkn[:], in0=kk[:], scalar1=rn[:, 0:1], scalar2=None, op0=ALU.mult), [kk, rn], [kkn]); yield
        V(lambda e: e.tensor_tensor(out=bv[:], in0=kkn[:], in1=asg[:], op=ALU.mult), [kkn, asg], [bv]); yield
        V(lambda e: e.scalar_tensor_tensor(out=k2[:], in0=asg[:], scalar=-1.0, in1=KA, op0=ALU.add, op1=ALU.mult), [asg, BC], [k2]); yield
        V(lambda e: e.scalar_tensor_tensor(out=k2[:], in0=k2[:], scalar=1.0, in1=k_, op0=ALU.add, op1=ALU.mult), [k2, Z], [k2]); yield
        tb = t64[3]
        V(lambda e: e.tensor_tensor(out=tb[:], in0=r_, in1=k2[:], op=ALU.mult), [Z, k2], [tb]); yield
        V(lambda e: e.scalar_tensor_tensor(out=junk[:], in0=tb[:], scalar=1.0, in1=RK, op0=ALU.mult, op1=ALU.mult, accum_out=bsum[:]), [tb, BC], [junk, bsum]); yield
        T_(lambda e: e.matmul(pce[:], TLE[:], ew[:], start=True, stop=True), [TLE, ew], [pce]); yield
        T_(lambda e: e.matmul(prem[:], TGT[:], ew[:], start=True, stop=True), [TGT, ew], [prem]); yield
        T_(lambda e: e.matmul(pPC[:], ew[:], CHI[:], start=True, stop=True), [ew, CHI], [pPC]); yield
        A_(lambda e: e.activation(out=Em[:], in_=pce[:], func=AF.Exp, scale=-1.0), [pce], [Em]); yield
        A_(lambda e: e.activation(out=Ep[:], in_=pce[:], func=AF.Exp), [pce], [Ep]); yield
        V(lambda e: e.tensor_tensor(out=cex[:], in0=pce[:], in1=ew[:], op=ALU.subtract), [pce, ew], [cex]); yield
        A_(lambda e: e.activation(out=Eme[:], in_=cex[:], func=AF.Exp, scale=-1.0), [cex], [Eme]); yield
        A_(lambda e: e.activation(out=Er[:], in_=prem[:], func=AF.Exp, scale=-1.0), [prem], [Er]); yield
        A_(lambda e: e.activation(out=PC[:], in_=pPC[:], func=AF.Exp, scale=-1.0), [pPC], [PC]); yield
        V(lambda e: e.scalar_tensor_tensor(out=T4[:, 0, :], in0=kkn[:], scalar=-1.0, in1=Eme[:], op0=ALU.mult, op1=ALU.mult), [kkn, Eme], [T4]); yield
        V(lambda e: e.tensor_tensor(out=T4[:, 1, :], in0=r_, in1=Em[:], op=ALU.mult), [Z, Em], [T4]); yield
        V(lambda e: e.tensor_tensor(out=T4[:, 2, :], in0=bv[:], in1=Ep[:], op=ALU.mult), [bv, Ep], [T4]); yield
        V(lambda e: e.tensor_tensor(out=T4[:, 3, :], in0=k2[:], in1=Ep[:], op=ALU.mult), [k2, Ep], [T4]); yield
        P_(lambda e: e.tensor_tensor(out=KH[:], in0=k2[:], in1=Er[:], op=ALU.mult), [k2, Er], [KH]); yield
        P_(lambda e: e.tensor_tensor(out=BH[:], in0=bv[:], in1=Er[:], op=ALU.mult), [bv, Er], [BH]); yield
        for j in range(4):
            T_(lambda e: e.matmul(pT[:, j * 128:(j + 1) * 128], T4[:, j, :], ID[:], start=True, stop=True), [T4, ID], [pT]); yield
        A_(lambda e: e.copy(out=FT[:], in_=pT[:]), [pT], [FT]); yield
        aT = FT[:, 0:128]; rT = FT[:, 128:256]; bT = FT[:, 256:384]; kT = FT[:, 384:512]
        P_(lambda e: e.tensor_copy(out=AT0[:, 0:64], in_=FT[:, 0:64]), [FT], [AT0]); yield
        P_(lambda e: e.tensor_copy(out=AT1[:, 64:128], in_=FT[:, 64:128]), [FT], [AT1]); yield
        P_(lambda e: e.tensor_copy(out=RT0[:, 0:64], in_=FT[:, 128:192]), [FT], [RT0]); yield
        P_(lambda e: e.tensor_copy(out=RT1[:, 64:128], in_=FT[:, 192:256]), [FT], [RT1]); yield
        T_(lambda e: e.matmul(pNM[:, 0:256], bT, FT[:, 0:256], start=True, stop=True), [FT], [pNM]); yield
        T_(lambda e: e.matmul(pNM[:, 256:512], kT, FT[:, 0:256], start=True, stop=True), [FT], [pNM]); yield
        T_(lambda e: e.matmul(pL[:], aT, bT, start=True, stop=True), [FT], [pL]); yield
        V(lambda e: e.tensor_tensor(out=NM[:], in0=pNM[:], in1=M4[:], op=ALU.mult), [pNM, M4], [NM]); yield
        NC_, LC_ = Ncur[0], Lcur[0]
        P_(lambda e: e.tensor_copy(out=NC_[:], in_=NM[:, 0:128]), [NM], [NC_]); yield
        V(lambda e: e.tensor_tensor(out=LC_[:], in0=pL[:], in1=TGT[:], op=ALU.mult), [pL, TGT], [LC_]); yield
        P_(lambda e: e.tensor_tensor(out=X[:], in0=NC_[:], in1=ID[:], op=ALU.add), [NC_, ID], [X]); yield
        P_(lambda e: e.tensor_tensor(out=Y[:], in0=LC_[:], in1=ID[:], op=ALU.add), [LC_, ID], [Y]); yield
        for lev in range(NLEV):
            last = lev == NLEV - 1
            Nn, Ln = Ncur[(lev + 1) % 2], Lcur[(lev + 1) % 2]
            T_(lambda e: e.matmul(pN2[:], LC_[:], NC_[:], start=True, stop=True), [LC_, NC_], [pN2]); yield
            if not last:
                T_(lambda e: e.matmul(pL2[:], NC_[:], LC_[:], start=True, stop=True), [LC_, NC_], [pL2]); yield
            A_(lambda e: e.copy(out=Nn[:], in_=pN2[:]), [pN2], [Nn]); yield
            if not last:
                V(lambda e: e.tensor_copy(out=Ln[:], in_=pL2[:]), [pL2], [Ln]); yield
            T_(lambda e: e.matmul(pXu[:], Y[:], Nn[:], start=True, stop=True), [Y, Nn], [pXu]); yield
            if not last:
                T_(lambda e: e.matmul(pYu[:], Nn[:], Y[:], start=True, stop=True), [Y, Nn], [pYu]); yield
            V(lambda e: e.tensor_tensor(out=X[:], in0=X[:], in1=pXu[:], op=ALU.add), [X, pXu], [X]); yield
            if not last:
                V(lambda e: e.tensor_tensor(out=Y[:], in0=Y[:], in1=pYu[:], op=ALU.add), [Y, pYu], [Y]); yield
            NC_, LC_ = Nn, Ln
        Ss = [S[si % 3], S[(si + 1) % 3], S[(si + 2) % 3]]; si += 2
        ATc = (AT0, AT1); RTc = (RT0, RT1)
        for c in range(2):
            ps_ = slice(c * 64, (c + 1) * 64)
            T_(lambda e: e.matmul(pR[:], NM[:, 256:384], VP[:], start=True, stop=False), [NM, VP], [pR]); yield
            T_(lambda e: e.matmul(pR[:], ATc[c][:], Ss[c][:], start=False, stop=True), [ATc[c], Ss[c]], [pR]); yield
            A_(lambda e: e.copy(out=RHS[ps_, :], in_=pR[ps_, :]), [pR], [RHS]); yield
            T_(lambda e: e.matmul(pU[:], X[ps_, :], RHS[ps_, :], start=True, stop=True), [X, RHS], [pU]); yield
            A_(lambda e: e.copy(out=U[ps_, :], in_=pU[ps_, :]), [pU], [U]); yield
            T_(lambda e: e.matmul(pS[:], BH[ps_, :], U[ps_, :], start=True, stop=False), [BH, U], [pS]); yield
            T_(lambda e: e.matmul(pS[:], KH[ps_, :], VP[ps_, :], start=False, stop=True), [KH, VP], [pS]); yield
            V(lambda e: e.scalar_tensor_tensor(out=Ss[c + 1][:], in0=Ss[c][:], scalar=PC[:, c:c + 1], in1=pS[:], op0=ALU.mult, op1=ALU.add), [Ss[c], PC, pS], [Ss[c + 1]]); yield
        T_(lambda e: e.matmul(pYo[:], NM[:, 128:256], U[:], start=True, stop=False), [NM, U], [pYo]); yield
        T_(lambda e: e.matmul(pYo[:], NM[:, 384:512], VP[:], start=False, stop=False), [NM, VP], [pYo]); yield
        T_(lambda e: e.matmul(pYo[:], RT0[:], Ss[0][:], start=False, stop=False), [RT0, Ss[0]], [pYo]); yield
        T_(lambda e: e.matmul(pYo[:], RT1[:], Ss[1][:], start=False, stop=True), [RT1, Ss[1]], [pYo]); yield
        A_(lambda e: e.activation(out=junk[:], in_=pYo[:], func=AF.Copy, accum_out=s1[:]), [pYo], [junk, s1]); yield
        V(lambda e: e.tensor_scalar(out=s1[:], in0=s1[:], scalar1=1.0 / 64, scalar2=None, op0=ALU.mult), [s1], [s1]); yield
        V(lambda e: e.tensor_scalar(out=yc[:], in0=pYo[:], scalar1=s1[:, 0:1], scalar2=None, op0=ALU.subtract), [pYo, s1], [yc]); yield
        A_(lambda e: e.activation(out=junk[:], in_=yc[:], func=AF.Square, accum_out=s2[:]), [yc], [junk, s2]); yield
        A_(lambda e: e.activation(out=rstd[:], in_=s2[:], func=AF.Sqrt, scale=1.0 / 64, bias=64e-5), [s2], [rstd]); yield
        V(lambda e: e.reciprocal(out=rstd[:], in_=rstd[:]), [rstd], [rstd]); yield
        V(lambda e: e.scalar_tensor_tensor(out=yn[:], in0=yc[:], scalar=rstd[:, 0:1], in1=LNW, op0=ALU.mult, op1=ALU.mult), [yc, rstd, BC], [yn]); yield
        V(lambda e: e.tensor_tensor(out=yn[:], in0=yn[:], in1=LNB, op=ALU.add), [yn, BC], [yn]); yield
        V(lambda e: e.scalar_tensor_tensor(out=yn[:], in0=VP[:], scalar=bsum[:, 0:1], in1=yn[:], op0=ALU.mult, op1=ALU.add), [VP, bsum, yn], [yn]); yield
        OB = YO[ti % 2]
        V(lambda e: e.tensor_tensor(out=OB[:], in0=yn[:], in1=gg[:], op=ALU.mult), [yn, gg], [OB]); yield
        k.dma('sp', ytok[r0:r0 + 128, :], OB[:], rbuf=OB, grp=go[ti % 2]); yield
        if not has_vres:
            OV = VO[ti % 2]
            P_(lambda e: e.tensor_copy(out=OV[:], in_=VP[:]), [VP], [OV]); yield
            k.dma('sp', vout[r0:r0 + 128, :], OV[:], rbuf=OV, grp=go[ti % 2]); yield

def rw_layout(pfx, h, z_rw, P, vfirst=None, vres_u=None):
    import numpy as np
    f = np.float32; ntok = z_rw.shape[0]; c = np.ascontiguousarray
    has_vres = vres_u is not None
    d = dict(rw_consts())
    zpad = lambda a: np.concatenate([np.zeros((1, a.shape[1]), f), a], 0)
    bc = lambda v: np.broadcast_to(v, (128, v.shape[-1]))
    if h is None:
        z = lambda *s: np.zeros(s, f)
        d.update(rkv=z(ntok + 1, 192), mu_rkv=z(128, 192), loraT=z(480, ntok + 1), mu_w=z(96, 1), mu_a=z(128, 1), mu_g=z(128, 2), w_lora=z(96, 64), a_lora=z(128, 64),
                 g_lora=z(128, 2, 64), bc5=z(128, 7, 64))
        if has_vres: d.update(vfirst=z(ntok, 64), vresT=z(64, ntok + 1), mu_v=z(64, 1), vres_b=z(64, 64), vbias=z(128, 64))
    else:
        hs = slice(h * 64, (h + 1) * 64)
        mu = P["rwkv_mu"]
        rkv = np.concatenate([z_rw[:, hs], z_rw[:, 640 + h * 64:640 + (h + 1) * 64], z_rw[:, 1280 + h * 64:1280 + (h + 1) * 64]], 1)
        mu_rkv = np.concatenate([mu[hs], mu[640 + h * 64:640 + (h + 1) * 64], mu[1280 + h * 64:1280 + (h + 1) * 64]])
        d.update(rkv=c(zpad(rkv)), mu_rkv=c(bc(mu_rkv)), loraT=c(zpad(z_rw[:, 1920:2400]).T),
                 mu_w=c(mu[1920:2016].reshape(96, 1)), mu_a=c(mu[2016:2144].reshape(128, 1)), mu_g=c(mu[2144:2400].reshape(2, 128).T),
                 w_lora=c(P["rwkv_w_lora"][:, hs]), a_lora=c(P["rwkv_a_lora"][:, hs]), g_lora=c(P["rwkv_g_lora"][:, hs].reshape(2, 128, 64).transpose(1, 0, 2)),
                 bc5=c(np.stack([bc(P[n][hs]) for n in ("rwkv_w0", "rwkv_a0", "rwkv_k_k", "rwkv_k_a", "rwkv_r_k", "rwkv_lnx_w", "rwkv_lnx_b")], 1)))
        if has_vres:
            d.update(vfirst=c(vfirst[:, hs]), vresT=c(zpad(vres_u).T), mu_v=c(P["rwkv_vres_mu"].reshape(64, 1)), vres_b=c(P["rwkv_vres_b"][:, hs]), vbias=c(bc(P["rwkv_vres_bias"][hs])))
    return {("rs_" if n in RW_SHARED else pfx) + n: v.astype(f) for n, v in d.items()}


def build_B(ntok, has_vres):
    k = KB()
    banks = [k.ps(f"bank{i}", [128, 512]) for i in range(8)]
    shared = {}
    ga = emit_RW(k, ntok, "r0_", has_vres, banks[0:4], shared)
    gb = emit_RW(k, ntok, "r1_", has_vres, banks[4:8], shared)
    alive = [ga, gb]
    while alive:
        for g_ in list(alive):
            try:
                next(g_)
            except StopIteration:
                alive.remove(g_)
    gg = emit_GLA(k, ntok, banks[0:3])
    gs = emit_S5(k, ntok, banks[3:7])
    alive = [(gg, 3), (gs, 2)]
    while alive:
        for it in list(alive):
            g_, n_ = it
            try:
                for _ in range(n_): next(g_)
            except StopIteration:
                alive.remove(it)
    return k.finish()

_PROGS = {}
def _prog(key, fn):
    if key not in _PROGS:
        _PROGS[key] = fn()
    return _PROGS[key]

def _tile16(v):
    return np.ascontiguousarray(np.asarray(v, np.float32).reshape(-1, 128).T)

def kernel(**inp):
    from concourse.bass_utils import run_bass_kernel_spmd
    f32 = np.float32
    inp = {n: np.asarray(v, f32) for n, v in inp.items()}
    x = inp["x"]; T = x.shape[1]; NCORE = 8; tpc = T // NCORE; depth = inp["w_in"].shape[0]
    cores = list(range(NCORE))
    c_ = np.ascontiguousarray
    xT = [c_(x[0, c * tpc:(c + 1) * tpc].T) for c in range(NCORE)]
    vfirst = None
    NCOLS = 11312
    for l in range(depth):
        extra = inp["rwkv_vres_a"][l - 1] if l > 0 else np.zeros((2048, 64), f32)
        wA = c_(np.concatenate([inp["w_in"][l], extra], 1))
        gA = _tile16(inp["norm_mix"][l])
        ncA = _prog(("A", tpc), lambda: build_A(tpc, NCOLS))
        res = run_bass_kernel_spmd(ncA, [{"xT": xT[c], "w": wA, "g": gA} for c in cores], core_ids=cores).results
        z = np.concatenate([res[c]["zT"].T for c in cores], 0)
        del res
        P = {n: inp[n][l] for n in ("rwkv_mu", "rwkv_w_lora", "rwkv_w0", "rwkv_a_lora", "rwkv_a0", "rwkv_g_lora", "rwkv_k_k", "rwkv_k_a", "rwkv_r_k", "rwkv_lnx_w", "rwkv_lnx_b")}
        has_vres = l > 0
        if has_vres:
            P.update(rwkv_vres_mu=inp["rwkv_vres_mu"][l - 1], rwkv_vres_b=inp["rwkv_vres_b"][l - 1], rwkv_vres_bias=inp["rwkv_vres_bias"][l - 1])
        zT_s5 = c_(z[:, :768].T); z_rw = z[:, 768:3168]; z_gla = z[:, 3168:5104]
        vres_u = c_(z[:, 11248:11312]) if has_vres else None
        ims = []
        for c in cores:
            m = {}
            m.update(s5_layout(c, zT_s5, inp["s5_lambda_re"][l], inp["s5_lambda_im"][l], inp["s5_log_step"][l], inp["s5_b_re"][l], inp["s5_b_im"][l],
                               inp["s5_c_re"][l], inp["s5_c_im"][l], inp["s5_d"][l]))
            for s in range(2):
                h = 2 * c + s
                m.update(rw_layout(f"r{s}_", h if h < 10 else None, z_rw, P, vfirst, vres_u))
            m.update(gla_layout(c if c < 5 else None, z_gla, inp["gla_alpha_lora"][l], inp["gla_alpha_bias"][l], inp["gla_norm_g"][l]))
            ims.append(m)
        ncB = _prog(("B", T, has_vres), lambda: build_B(T, has_vres))
        res = run_bass_kernel_spmd(ncB, ims, core_ids=cores).results
        del ims
        y = np.empty((T, 2048), f32)
        y[:, :768] = np.concatenate([res[c]["s_yo"] for c in cores], 0).T
        for h in range(10):
            y[:, 768 + h * 64:768 + (h + 1) * 64] = res[h // 2][f"r{h % 2}_ytok"]
        for h in range(5):
            y[:, 1408 + h * 128:1408 + (h + 1) * 128] = res[h]["g_otok"]
        if l == 0:
            vfirst = np.concatenate([res[h // 2][f"r{h % 2}_vout"] for h in range(10)], 1)
        del res
        last = l == depth - 1
        ncC = _prog(("C", tpc, last), lambda: build_C(tpc, last))
        gbt = _tile16(inp["gate_bias"][l])
        ims = []
        for c in cores:
            ts = slice(c * tpc, (c + 1) * tpc)
            ims.append({"xT": xT[c], "zgT": c_(z[ts, 5104:11248].T), "gb": gbt, "yT": c_(y[ts].T), "w_up": inp["w_up"][l], "w_out": inp["w_out"][l],
                        "nm": _tile16(inp["norm_mlp"][l]), "w1": inp["mlp_w1"][l], "w2": inp["mlp_w2"][l], "fn": _tile16(inp["final_norm"]),
                        "gluw": inp["s5_glu_w"][l], "glub": _tile16(inp["s5_glu_b"][l])})
        del z, y
        res = run_bass_kernel_spmd(ncC, ims, core_ids=cores).results
        del ims
        xT = [res[c]["xoT"] for c in cores]
        del res
    out = np.concatenate([xT[c].T for c in cores], 0)[None]
    return np.ascontiguousarray(out.astype(f32))
```

```python
import math
import numpy as np
from contextlib import ExitStack
import concourse.bass as bass
import concourse.mybir as mybir
F32 = mybir.dt.float32; BF16 = mybir.dt.bfloat16
AF = mybir.ActivationFunctionType; ALU = mybir.AluOpType; AX = mybir.AxisListType

class Buf:
    def __init__(self, t, name):
        self.t = t; self.name = name; self.lw = {}; self.rd = {}; self.sem = None; self.dcount = 0; self.excl = False
    def __getitem__(self, k):
        return self.t[k]

class Group:
    def __init__(self, name, sem):
        self.name = name; self.sem = sem; self.dcount = 0

class View:
    def __init__(self, parent, ap):
        object.__setattr__(self, 'parent', parent); object.__setattr__(self, 't', ap)
    def __getitem__(self, k): return self.t[k]
    def __getattr__(self, n): return getattr(self.parent, n)
    def __setattr__(self, n, v): setattr(self.parent, n, v)
    def __eq__(self, o): return (o.parent if isinstance(o, View) else o) is self.parent
    def __hash__(self): return id(self.parent)

class KB:
    SAME_ENGINE_SYNC = ('dve', 'act', 'pool')
    def view(self, parent, ap, name=None):
        return View(parent, ap)
    def __init__(self):
        self.nc = bass.Bass("TRN2", target_bir_lowering=False)
        self.es = ExitStack()
        nc = self.nc
        self.eng = {'pe': nc.tensor, 'dve': nc.vector, 'act': nc.scalar, 'pool': nc.gpsimd, 'sp': nc.sync}
        self.sem = {k: self.es.enter_context(nc.semaphore("s_" + k)) for k in self.eng}
        self.cnt = {k: 0 for k in self.eng}
        self.waited = {}
        self.bufs = []
        self.n_inst = 0
        self.dma_sems = []
    def dram(self, name, shape, dt, kind):
        return self.nc.dram_tensor(name, list(shape), dt, kind=kind).ap()
    def sb(self, name, shape, dt=F32):
        b = Buf(self.es.enter_context(self.nc.sbuf_tensor(name, list(shape), dt)), name); self.bufs.append(b); return b
    def ps(self, name, shape, dt=F32):
        b = Buf(self.es.enter_context(self.nc.psum_tensor(name, list(shape), dt)), name); b.excl = True; self.bufs.append(b); return b
    def group(self, name):
        g_ = Group(name, self.es.enter_context(self.nc.semaphore("g_" + name))); self.dma_sems.append(g_); return g_
    def _wait(self, e, key, semh, count):
        if isinstance(semh, Group):
            count = semh.dcount; semh = semh.sem
        if self.waited.get((e, key), 0) >= count: return
        self.eng[e].wait_ge(semh, count); self.waited[(e, key)] = count
    def _deps(self, e, reads, writes):
        for b in reads:
            for key, (semh, c) in b.lw.items():
                if key == e and e not in self.SAME_ENGINE_SYNC: continue
                self._wait(e, key, semh, c)
            if b.excl:
                for key, (semh, c) in b.rd.items():
                    if key != e: self._wait(e, key, semh, c)
        for b in writes:
            for d in (b.lw, b.rd):
                for key, (semh, c) in d.items():
                    if key == e and e not in self.SAME_ENGINE_SYNC: continue
                    self._wait(e, key, semh, c)
    def op(self, e, fn, reads=(), writes=()):
        self._deps(e, reads, writes)
        ins = fn(self.eng[e])
        self.cnt[e] += 1; c = self.cnt[e]
        ins.then_inc(self.sem[e], 1)
        for b in writes:
            b.lw = {e: (self.sem[e], c)}; b.rd = {}
        for b in reads:
            if b not in writes: b.rd[e] = (self.sem[e], c)
        self.n_inst += 1
        return ins
    def dma(self, q, out, in_, rbuf=None, wbuf=None, grp=None, **kw):
        b = wbuf if wbuf is not None else rbuf
        if grp is None:
            if b.sem is None:
                b.sem = self.es.enter_context(self.nc.semaphore("d_" + b.name)); self.dma_sems.append(b)
            holder = b
        else:
            holder = grp
        self._deps(q, [rbuf] if rbuf is not None else [], [wbuf] if wbuf is not None else [])
        ins = self.eng[q].dma_start(out=out, in_=in_, **kw)
        holder.dcount += 16
        ins.then_inc(holder.sem, 16)
        key = 'dma_' + holder.name
        ent = (grp, None) if grp is not None else (b.sem, b.dcount)
        if wbuf is not None:
            wbuf.lw = {key: ent}; wbuf.rd = {}
        else:
            rbuf.rd[key] = ent
        self.n_inst += 1
        return ins
    def finish(self, e='sp'):
        for b in self.dma_sems:
            self._wait(e, 'dma_' + b.name, b.sem, b.dcount)
        self.es.close()
        return self.nc

D = 2048; KC = 16
def build_A(ntok, ncols, TB=512):
    k = KB()
    xT = k.dram("xT", [D, ntok], F32, "ExternalInput")
    w = k.dram("w", [D, ncols], F32, "ExternalInput")
    g = k.dram("g", [128, KC], F32, "ExternalInput")
    zT = k.dram("zT", [ncols, ntok], F32, "ExternalOutput")
    ntb = ntok // TB
    xk = [k.sb(f"xk{i}", [128, KC, TB]) for i in range(2)]
    u = k.sb("u", [128, KC, ntok], BF16)
    sq = [k.sb(f"sq{i}", [128, TB], BF16) for i in range(2)]
    ones = k.sb("ones", [128, 128], BF16)
    gt = k.sb("gt", [128, KC])
    rstd = k.sb("rstd", [128, TB])
    psn = k.ps("psn", [128, TB])
    pz = [k.ps(f"pz{i}", [128, TB]) for i in range(4)]
    zo = [k.sb(f"zo{i}", [128, TB]) for i in range(3)]
    wb = [k.sb(f"wb{i}", [128, KC, 128], BF16) for i in range(2)]
    k.op('pool', lambda e: e.memset(ones[:], 1.0), [], [ones])
    k.dma('sp', gt[:], g[:, :], wbuf=gt)
    xv = xT.rearrange("(kc p) t -> p kc t", p=128)
    for tb in range(ntb):
        X = xk[tb % 2]
        k.dma('sp', X[:], xv[:, :, tb * TB:(tb + 1) * TB], wbuf=X)
        for kc in range(KC):
            S = sq[kc % 2]
            k.op('act', lambda e: e.activation(out=S[:], in_=X[:, kc, :], func=AF.Square), [X], [S])
            k.op('pe', lambda e: e.matmul(psn[:], ones[:], S[:], start=(kc == 0), stop=(kc == KC - 1)), [ones, S], [psn])
        k.op('act', lambda e: e.activation(out=rstd[:], in_=psn[:], func=AF.Sqrt, scale=1.0 / D, bias=1e-6), [psn], [rstd])
        k.op('dve', lambda e: e.reciprocal(out=rstd[:], in_=rstd[:]), [rstd], [rstd])
        for kc in range(KC):
            k.op('dve', lambda e: e.scalar_tensor_tensor(out=u[:, kc, tb * TB:(tb + 1) * TB], in0=X[:, kc, :], scalar=gt[:, kc:kc + 1],
                                                        in1=rstd[:], op0=ALU.mult, op1=ALU.mult), [X, gt, rstd], [u])
    wv = w.rearrange("(kc p) c -> p kc c", p=128)
    ncc = (ncols + 127) // 128
    i = 0
    for cc in range(ncc):
        cw = min(128, ncols - cc * 128)
        W = wb[cc % 2]
        k.dma('pool', W[:, :, :cw], wv[:, :, cc * 128:cc * 128 + cw], wbuf=W)
        for tb in range(ntb):
            P = pz[i % 4]; Z = zo[i % 3]
            for kc in range(KC):
                k.op('pe', lambda e: e.matmul(P[:cw, :], W[:, kc, :cw], u[:, kc, tb * TB:(tb + 1) * TB], start=(kc == 0), stop=(kc == KC - 1)), [W, u], [P])
            if i % 2 == 0:
                k.op('act', lambda e: e.copy(out=Z[:cw, :], in_=P[:cw, :]), [P], [Z])
            else:
                k.op('dve', lambda e: e.tensor_copy(out=Z[:cw, :], in_=P[:cw, :]), [P], [Z])
            k.dma('sp', zT[cc * 128:cc * 128 + cw, tb * TB:(tb + 1) * TB], Z[:cw, :], rbuf=Z)
            i += 1
    print("A n_inst", k.n_inst)
    return k.finish()

D = 2048; KC = 16; FF = 8192; FC = 64
def build_C(ntok, last, TB=512):
    k = KB()
    xT = k.dram("xT", [D, ntok], F32, "ExternalInput")
    zgT = k.dram("zgT", [3 * D, ntok], F32, "ExternalInput")
    gb = k.dram("gb", [128, 48], F32, "ExternalInput")
    yT = k.dram("yT", [D, ntok], F32, "ExternalInput")
    w_up = k.dram("w_up", [D, D], F32, "ExternalInput")
    w_out = k.dram("w_out", [D, D], F32, "ExternalInput")
    nm = k.dram("nm", [128, KC], F32, "ExternalInput")
    w1 = k.dram("w1", [D, FF], F32, "ExternalInput")
    w2 = k.dram("w2", [FF, D], F32, "ExternalInput")
    fn = k.dram("fn", [128, KC], F32, "ExternalInput")
    gluw = k.dram("gluw", [768, 768], F32, "ExternalInput")
    glub = k.dram("glub", [128, 6], F32, "ExternalInput")
    xoT = k.dram("xoT", [D, ntok], F32, "ExternalOutput")
    ntb = ntok // TB
    X = k.sb("X", [128, KC, TB])
    YH = k.sb("YH", [128, KC, TB], BF16)
    M = k.sb("M", [128, KC, TB], BF16)
    ACTB = k.sb("ACTB", [128, FC, TB], BF16)
    NWB = 4
    wb = [k.sb(f"wb{i}", [128, KC, 128], BF16) for i in range(NWB)]
    stg = [k.sb(f"stg{i}", [128, KC, 128]) for i in range(2)]
    G = [[k.sb(f"G{i}_{j}", [128, TB]) for j in range(3)] for i in range(2)]
    sq = [k.sb(f"sq{i}", [128, TB], BF16) for i in range(2)]
    tmp = [k.sb(f"tmp{i}", [128, TB]) for i in range(2)]
    mm = k.sb("mm", [128, TB])
    ones = k.sb("ones", [128, 128], BF16)
    gbt = k.sb("gbt", [128, 48]); nmt = k.sb("nmt", [128, KC]); fnt = k.sb("fnt", [128, KC]); glubt = k.sb("glubt", [128, 6])
    k.dma('sp', glubt[:], glub[:, :], wbuf=glubt)
    AF32 = ACTB[:].rearrange("p a b -> p (a b)").bitcast(F32)
    S5N = 6 * TB
    YA = k.view(ACTB, AF32[:, 0:S5N]); YG = k.view(ACTB, AF32[:, S5N:2 * S5N]); SQ = k.view(ACTB, AF32[:, 2 * S5N:3 * S5N])
    YGb = k.view(ACTB, ACTB[:].rearrange("p a b -> p (a b)")[:, 6 * S5N:7 * S5N])
    gluv = gluw.rearrange("(kc p) c -> p kc c", p=128)
    C1 = 0.7978845608028654; C2 = C1 * 0.044715
    rstd = k.sb("rstd", [128, TB])
    psn = k.ps("psn", [128, TB])
    pz = [k.ps(f"pz{i}", [128, TB]) for i in range(6)]
    k.op('pool', lambda e: e.memset(ones[:], 1.0), [], [ones])
    k.dma('sp', gbt[:], gb[:, :], wbuf=gbt); k.dma('sp', nmt[:], nm[:, :], wbuf=nmt); k.dma('sp', fnt[:], fn[:, :], wbuf=fnt)
    xv = xT.rearrange("(kc p) t -> p kc t", p=128); yv = yT.rearrange("(kc p) t -> p kc t", p=128)
    xov = xoT.rearrange("(kc p) t -> p kc t", p=128)
    wupv = w_up.rearrange("(kc p) c -> p kc c", p=128); woutv = w_out.rearrange("(kc p) c -> p kc c", p=128)
    w1v = w1.rearrange("(kc p) c -> p kc c", p=128); w2v = w2.rearrange("(fc p) c -> p fc c", p=128)
    wlist = []
    for tb_ in range(ntb):
        for fo in range(6): wlist.append((gluv[:, :, fo * 128:(fo + 1) * 128], 6))
        for fo in range(KC): wlist.append((wupv[:, :, fo * 128:(fo + 1) * 128], KC))
        for fo in range(KC): wlist.append((woutv[:, :, fo * 128:(fo + 1) * 128], KC))
        for fc in range(FC): wlist.append((w1v[:, :, fc * 128:(fc + 1) * 128], KC))
        for fo in range(KC):
            for q in range(4): wlist.append((w2v[:, q * 16:(q + 1) * 16, fo * 128:(fo + 1) * 128], KC))
    wst = {"dma": 0, "cast": 0, "taken": 0}
    def w_next():
        i = wst["taken"]
        while wst["dma"] < min(len(wlist), i + 4):
            j = wst["dma"]; src, nk = wlist[j]
            if j % 2 == 0:
                k.dma('pool', wb[j % NWB][:, 0:nk, :], src, wbuf=wb[j % NWB])
            else:
                S_ = stg[(j // 2) % 2]
                k.dma('sp', S_[:, 0:nk, :], src, wbuf=S_)
            wst["dma"] += 1
        while wst["cast"] < min(wst["dma"], i + 2):
            j = wst["cast"]; src, nk = wlist[j]
            if j % 2 == 1:
                S_ = stg[(j // 2) % 2]; W_ = wb[j % NWB]
                k.op('act', lambda e: e.copy(out=W_[:, 0:nk, :], in_=S_[:, 0:nk, :]), [S_], [W_])
            wst["cast"] += 1
        wst["taken"] += 1
        return wb[i % NWB]
    BR = [(0, 6), (6, 11), (11, 16)]
    wi = 0; pi = 0
    def rmsn(gt_tile, dst_fn):
        for kc in range(KC):
            S = sq[kc % 2]
            k.op('act', lambda e: e.activation(out=S[:], in_=X[:, kc, :], func=AF.Square), [X], [S])
            k.op('pe', lambda e: e.matmul(psn[:], ones[:], S[:], start=(kc == 0), stop=(kc == KC - 1)), [ones, S], [psn])
        k.op('act', lambda e: e.activation(out=rstd[:], in_=psn[:], func=AF.Sqrt, scale=1.0 / D, bias=1e-6), [psn], [rstd])
        k.op('dve', lambda e: e.reciprocal(out=rstd[:], in_=rstd[:]), [rstd], [rstd])
    for tb in range(ntb):
        ts = slice(tb * TB, (tb + 1) * TB)
        k.dma('sp', X[:], xv[:, :, ts], wbuf=X)
        k.dma('pool', YH[:, 6:16, :], yv[:, 6:16, ts], wbuf=YH)
        k.dma('sp', YA[:].rearrange("p (a b) -> p a b", a=6), yv[:, 0:6, ts], wbuf=YA)
        k.op('pool', lambda e: e.tensor_tensor(out=SQ[:], in0=YA[:], in1=YA[:], op=ALU.mult), [YA], [SQ])
        k.op('dve', lambda e: e.tensor_scalar(out=SQ[:], in0=SQ[:], scalar1=C2, scalar2=C1, op0=ALU.mult, op1=ALU.add), [SQ], [SQ])
        k.op('dve', lambda e: e.tensor_tensor(out=SQ[:], in0=SQ[:], in1=YA[:], op=ALU.mult), [SQ, YA], [SQ])
        k.op('act', lambda e: e.activation(out=SQ[:], in_=SQ[:], func=AF.Tanh), [SQ], [SQ])
        k.op('dve', lambda e: e.tensor_scalar(out=SQ[:], in0=SQ[:], scalar1=0.5, scalar2=0.5, op0=ALU.mult, op1=ALU.add), [SQ], [SQ])
        k.op('dve', lambda e: e.tensor_tensor(out=YG[:], in0=SQ[:], in1=YA[:], op=ALU.mult), [SQ, YA], [YG])
        k.op('pool', lambda e: e.tensor_copy(out=YGb[:], in_=YG[:]), [YG], [YGb])
        for fo in range(6):
            W = w_next()
            P = pz[pi % 6]; pi += 1
            for kc in range(6):
                k.op('pe', lambda e: e.matmul(P[:], W[:, kc, :], YGb[:, kc * TB:(kc + 1) * TB], start=(kc == 0), stop=(kc == 5)), [W, YGb], [P])
            T = tmp[fo % 2]
            k.op('act', lambda e: e.activation(out=T[:], in_=P[:], func=AF.Sigmoid, bias=glubt[:, fo:fo + 1]), [P, glubt], [T])
            k.op('dve', lambda e: e.tensor_tensor(out=YH[:, fo, :], in0=YG[:, fo * TB:(fo + 1) * TB], in1=T[:], op=ALU.mult), [YG, T], [YH])
        for fo in range(KC):
            W = w_next()
            Gs = G[fo % 2]
            for br in range(3):
                r0 = br * D + fo * 128
                k.dma('sp', Gs[br][:], zgT[r0:r0 + 128, ts], wbuf=Gs[br])
                k.op('act', lambda e: e.activation(out=Gs[br][:], in_=Gs[br][:], func=AF.Sigmoid, bias=gbt[:, br * 16 + fo:br * 16 + fo + 1]), [Gs[br], gbt], [Gs[br]])
            Ps = []
            for br in range(3):
                P = pz[pi % 6]; pi += 1; Ps.append(P)
                a, b = BR[br]
                for kc in range(a, b):
                    k.op('pe', lambda e: e.matmul(P[:], W[:, kc, :], YH[:, kc, :], start=(kc == a), stop=(kc == b - 1)), [W, YH], [P])
            k.op('dve', lambda e: e.tensor_tensor(out=mm[:], in0=Ps[0][:], in1=Gs[0][:], op=ALU.mult), [Ps[0], Gs[0]], [mm])
            T = tmp[0]
            k.op('dve', lambda e: e.tensor_tensor(out=T[:], in0=Ps[1][:], in1=Gs[1][:], op=ALU.mult), [Ps[1], Gs[1]], [T])
            k.op('dve', lambda e: e.tensor_tensor(out=mm[:], in0=mm[:], in1=T[:], op=ALU.add), [mm, T], [mm])
            T = tmp[1]
            k.op('dve', lambda e: e.tensor_tensor(out=T[:], in0=Ps[2][:], in1=Gs[2][:], op=ALU.mult), [Ps[2], Gs[2]], [T])
            k.op('dve', lambda e: e.tensor_tensor(out=M[:, fo, :], in0=mm[:], in1=T[:], op=ALU.add), [mm, T], [M])
        for fo in range(KC):
            W = w_next()
            P = pz[pi % 6]; pi += 1
            for kc in range(KC):
                k.op('pe', lambda e: e.matmul(P[:], W[:, kc, :], M[:, kc, :], start=(kc == 0), stop=(kc == KC - 1)), [W, M], [P])
            k.op('dve', lambda e: e.tensor_tensor(out=X[:, fo, :], in0=X[:, fo, :], in1=P[:], op=ALU.add), [X, P], [X])
        rmsn(nmt, None)
        for kc in range(KC):
            k.op('dve', lambda e: e.scalar_tensor_tensor(out=YH[:, kc, :], in0=X[:, kc, :], scalar=nmt[:, kc:kc + 1], in1=rstd[:], op0=ALU.mult, op1=ALU.mult), [X, nmt, rstd], [YH])
        for fc in range(FC):
            W = w_next()
            P = pz[pi % 6]; pi += 1
            for kc in range(KC):
                k.op('pe', lambda e: e.matmul(P[:], W[:, kc, :], YH[:, kc, :], start=(kc == 0), stop=(kc == KC - 1)), [W, YH], [P])
            T = tmp[fc % 2]
            k.op('act', lambda e: e.activation(out=T[:], in_=P[:], func=AF.Relu), [P], [T])
            k.op('pool', lambda e: e.tensor_tensor(out=ACTB[:, fc, :], in0=T[:], in1=T[:], op=ALU.mult), [T], [ACTB])
        for fo in range(KC):
            P = pz[pi % 6]; pi += 1
            for h in range(4):
                W = w_next()
                for f in range(16):
                    fc = h * 16 + f
                    k.op('pe', lambda e: e.matmul(P[:], W[:, f, :], ACTB[:, fc, :], start=(fc == 0), stop=(fc == FC - 1)), [W, ACTB], [P])
            k.op('dve', lambda e: e.tensor_tensor(out=X[:, fo, :], in0=X[:, fo, :], in1=P[:], op=ALU.add), [X, P], [X])
        if last:
            rmsn(fnt, None)
            for kc in range(KC):
                k.op('dve', lambda e: e.scalar_tensor_tensor(out=X[:, kc, :], in0=X[:, kc, :], scalar=fnt[:, kc:kc + 1], in1=rstd[:], op0=ALU.mult, op1=ALU.mult), [X, fnt, rstd], [X])
        k.dma('sp', xov[:, :, ts], X[:], rbuf=X)
    print("C n_inst", k.n_inst)
    return k.finish()

PI = math.pi
PI = math.pi
def emit_S5(k, ntok, banks, TB=512):
    NP = 3
    zs = k.dram("s_zs", [96, ntok], F32, "ExternalInput")
    lamre = k.dram("s_lamre", [128, NP], F32, "ExternalInput")
    lamim = k.dram("s_lamim", [128, NP], F32, "ExternalInput")
    lstep = k.dram("s_lstep", [128, NP], F32, "ExternalInput")
    bre = k.dram("s_bre", [96, NP, 128], F32, "ExternalInput")
    bim = k.dram("s_bim", [96, NP, 128], F32, "ExternalInput")
    cre = k.dram("s_cre", [128, NP, 96], F32, "ExternalInput")
    cim = k.dram("s_cim", [128, NP, 96], F32, "ExternalInput")
    dvec = k.dram("s_dvec", [96, 1], F32, "ExternalInput")
    tau = k.dram("s_tau", [128, 128], F32, "ExternalInput")
    yo = k.dram("s_yo", [96, ntok], F32, "ExternalOutput")
    nb = ntok // TB; NCH = TB // 128
    sb = lambda n, shp, dt=F32: k.sb("s_" + n, shp, dt)
    lre = sb("lre", [128, NP]); lim = sb("lim", [128, NP]); lst = sb("lst", [128, NP])
    breT = sb("breT", [96, NP, 128]); bimT = sb("bimT", [96, NP, 128])
    creT = sb("creT", [128, NP, 96]); cimT = sb("cimT", [128, NP, 96]); ncreT = sb("ncreT", [128, NP, 96]); ncimT = sb("ncimT", [128, NP, 96])
    dT = sb("dT", [96, 1]); tauT = sb("tauT", [128, 128])
    gcon = k.group("s_con"); gz = [k.group(f"s_z{i}") for i in range(2)]; gy = [k.group(f"s_y{i}") for i in range(2)]
    for t, d in ((lre, lamre), (lim, lamim), (lst, lstep), (breT, bre), (bimT, bim), (creT, cre), (cimT, cim), (dT, dvec), (tauT, tau)):
        k.dma('sp', t[:], d, wbuf=t, grp=gcon)
    k.op('dve', lambda e: e.tensor_scalar(out=ncreT[:], in0=creT[:], scalar1=-1.0, scalar2=None, op0=ALU.mult), [creT], [ncreT])
    k.op('dve', lambda e: e.tensor_scalar(out=ncimT[:], in0=cimT[:], scalar1=-1.0, scalar2=None, op0=ALU.mult), [cimT], [ncimT])
    col = lambda n: sb(n, [128, NP])
    dt = col("dt"); lr = col("lr"); al = col("al"); om = col("om"); ea = col("ea"); cw = col("cw"); sw = col("sw")
    lbr = col("lbr"); lbi = col("lbi"); den = col("den"); t1 = col("t1"); t2 = col("t2"); qre = col("qre"); qim = col("qim")
    Rre = col("Rre"); Rim = col("Rim"); ang = col("ang")
    V = lambda fn, r, w: k.op('dve', fn, r, w)
    A_ = lambda fn, r, w: k.op('act', fn, r, w)
    A_(lambda e: e.activation(out=dt[:], in_=lst[:], func=AF.Exp), [lst], [dt])
    V(lambda e: e.tensor_scalar(out=lr[:], in0=lre[:], scalar1=-1e-4, scalar2=None, op0=ALU.min), [lre], [lr])
    V(lambda e: e.tensor_tensor(out=al[:], in0=lr[:], in1=dt[:], op=ALU.mult), [lr, dt], [al])
    V(lambda e: e.tensor_tensor(out=om[:], in0=lim[:], in1=dt[:], op=ALU.mult), [lim, dt], [om])
    A_(lambda e: e.activation(out=ea[:], in_=al[:], func=AF.Exp), [al], [ea])
    def sincos(dst_s, dst_c, src_ap, srcbufs, shape_ap_fn, scale):
        pass
    angi = sb("angi", [128, NP], mybir.dt.int32); angf = sb("angf", [128, NP])
    def sin_of(dst, src, tmpb, mult, phase):
        (db, da), (sbf, sa), (tb_, ta) = dst, src, tmpb
        V(lambda e: e.tensor_scalar(out=ta, in0=sa, scalar1=mult / (2 * PI), scalar2=phase / (2 * PI), op0=ALU.mult, op1=ALU.add), [sbf], [tb_])
        V(lambda e: e.tensor_copy(out=angi[:], in_=ta), [tb_], [angi])
        V(lambda e: e.tensor_copy(out=angf[:], in_=angi[:]), [angi], [angf])
        V(lambda e: e.tensor_tensor(out=ta, in0=ta, in1=angf[:], op=ALU.subtract), [tb_, angf], [tb_])
        A_(lambda e: e.activation(out=da, in_=ta, func=AF.Sin, scale=2 * PI), [tb_], [db])
    sin_of((sw, sw[:]), (om, om[:]), (ang, ang[:]), 1.0, 0.0)
    sin_of((cw, cw[:]), (om, om[:]), (ang, ang[:]), 1.0, PI / 2)
    sin_of((Rim, Rim[:]), (om, om[:]), (ang, ang[:]), 128.0, 0.0)
    sin_of((Rre, Rre[:]), (om, om[:]), (ang, ang[:]), 128.0, PI / 2)
    V(lambda e: e.tensor_tensor(out=lbr[:], in0=ea[:], in1=cw[:], op=ALU.mult), [ea, cw], [lbr])
    V(lambda e: e.tensor_tensor(out=lbi[:], in0=ea[:], in1=sw[:], op=ALU.mult), [ea, sw], [lbi])
    V(lambda e: e.tensor_scalar(out=lbr[:], in0=lbr[:], scalar1=-1.0, scalar2=None, op0=ALU.add), [lbr], [lbr])
    V(lambda e: e.tensor_tensor(out=den[:], in0=lr[:], in1=lr[:], op=ALU.mult), [lr], [den])
    V(lambda e: e.tensor_tensor(out=t1[:], in0=lim[:], in1=lim[:], op=ALU.mult), [lim], [t1])
    V(lambda e: e.tensor_tensor(out=den[:], in0=den[:], in1=t1[:], op=ALU.add), [den, t1], [den])
    V(lambda e: e.reciprocal(out=den[:], in_=den[:]), [den], [den])
    V(lambda e: e.tensor_tensor(out=t1[:], in0=lbr[:], in1=lr[:], op=ALU.mult), [lbr, lr], [t1])
    V(lambda e: e.tensor_tensor(out=t2[:], in0=lbi[:], in1=lim[:], op=ALU.mult), [lbi, lim], [t2])
    V(lambda e: e.tensor_tensor(out=t1[:], in0=t1[:], in1=t2[:], op=ALU.add), [t1, t2], [t1])
    V(lambda e: e.tensor_tensor(out=qre[:], in0=t1[:], in1=den[:], op=ALU.mult), [t1, den], [qre])
    V(lambda e: e.tensor_tensor(out=t1[:], in0=lbi[:], in1=lr[:], op=ALU.mult), [lbi, lr], [t1])
    V(lambda e: e.tensor_tensor(out=t2[:], in0=lbr[:], in1=lim[:], op=ALU.mult), [lbr, lim], [t2])
    V(lambda e: e.tensor_tensor(out=t1[:], in0=t1[:], in1=t2[:], op=ALU.subtract), [t1, t2], [t1])
    V(lambda e: e.tensor_tensor(out=qim[:], in0=t1[:], in1=den[:], op=ALU.mult), [t1, den], [qim])
    COS = [sb(f"COS{p}", [128, 128]) for p in range(NP)]; SIN = [sb(f"SIN{p}", [128, 128]) for p in range(NP)]
    ERE = [sb(f"ERE{p}", [128, 128]) for p in range(NP)]; EIM = [sb(f"EIM{p}", [128, 128]) for p in range(NP)]
    DEC = [sb(f"DEC{p}", [128, 128]) for p in range(NP)]
    th = sb("th", [128, 128]); thi = sb("thi", [128, 128], mybir.dt.int32); thf = sb("thf", [128, 128])
    for p in range(NP):
        for dstT, ph in ((SIN[p], 0.0), (COS[p], PI / 2)):
            V(lambda e: e.tensor_scalar(out=th[:], in0=tauT[:], scalar1=om[:, p:p + 1], scalar2=1.0 / (2 * PI), op0=ALU.mult, op1=ALU.mult), [tauT, om], [th])
            V(lambda e: e.tensor_scalar(out=th[:], in0=th[:], scalar1=ph / (2 * PI), scalar2=None, op0=ALU.add), [th], [th])
            V(lambda e: e.tensor_copy(out=thi[:], in_=th[:]), [th], [thi])
            V(lambda e: e.tensor_copy(out=thf[:], in_=thi[:]), [thi], [thf])
            V(lambda e: e.tensor_tensor(out=th[:], in0=th[:], in1=thf[:], op=ALU.subtract), [th, thf], [th])
            A_(lambda e: e.activation(out=dstT[:], in_=th[:], func=AF.Sin, scale=2 * PI), [th], [dstT])
        V(lambda e: e.tensor_scalar(out=th[:], in0=SIN[p][:], scalar1=qim[:, p:p + 1], scalar2=None, op0=ALU.mult), [SIN[p], qim], [th])
        V(lambda e: e.scalar_tensor_tensor(out=ERE[p][:], in0=COS[p][:], scalar=qre[:, p:p + 1], in1=th[:], op0=ALU.mult, op1=ALU.add), [COS[p], qre, th], [ERE[p]])
        V(lambda e: e.tensor_scalar(out=th[:], in0=SIN[p][:], scalar1=qre[:, p:p + 1], scalar2=None, op0=ALU.mult), [SIN[p], qre], [th])
        V(lambda e: e.scalar_tensor_tensor(out=EIM[p][:], in0=COS[p][:], scalar=qim[:, p:p + 1], in1=th[:], op0=ALU.mult, op1=ALU.subtract), [COS[p], qim, th], [EIM[p]])
        V(lambda e: e.tensor_scalar(out=DEC[p][:], in0=tauT[:], scalar1=0.0, scalar2=ea[:, p:p + 1], op0=ALU.mult, op1=ALU.add), [tauT, ea], [DEC[p]])
    zb = [sb(f"zb{i}", [96, TB]) for i in range(2)]
    PX = [[banks[0], banks[1]], [banks[0], banks[1]]]
    PY = [k.view(banks[2], banks[2][0:96, :]), k.view(banks[3], banks[3][0:96, :])]
    rr = [[sb(f"rr{p}_{c}", [128, TB]) for c in range(2)] for p in range(NP)]
    tm = [sb(f"tm{i}", [128, TB]) for i in range(2)]
    PP = [[sb(f"PP{i}_{j}", [128, TB]) for j in range(4)] for i in range(2)]
    ini = [[sb(f"ini{p}_{c}", [128, 1]) for c in range(2)] for p in range(NP)]
    it = [sb(f"it{p}", [128, 1]) for p in range(NP)]
    yout = [sb(f"yout{i}", [96, TB]) for i in range(2)]
    for p in range(NP):
        for c in range(2):
            k.op('pool', lambda e: e.memset(ini[p][c][:], 0.0), [], [ini[p][c]]); yield
    bc = lambda T: T[:].unsqueeze(1).to_broadcast([128, NCH, 128])
    v3 = lambda ap: ap.rearrange("p (c t) -> p c t", t=128)
    k.dma('sp', zb[0][:], zs[:, 0:TB], wbuf=zb[0], grp=gz[0]); yield
    for b in range(nb):
        Z = zb[b % 2]; ts = slice(b * TB, (b + 1) * TB)
        if b + 1 < nb:
            k.dma('sp', zb[(b + 1) % 2][:], zs[:, (b + 1) * TB:(b + 2) * TB], wbuf=zb[(b + 1) % 2], grp=gz[(b + 1) % 2]); yield
        PYb = PY[b % 2]
        for p in range(NP):
            Xr, Xi = PX[p % 2]
            k.op('pe', lambda e: e.matmul(Xr[:], breT[:, p, :], Z[:], start=True, stop=True), [breT, Z], [Xr]); yield
            k.op('pe', lambda e: e.matmul(Xi[:], bimT[:, p, :], Z[:], start=True, stop=True), [bimT, Z], [Xi]); yield
            Rr, Ri = rr[p]
            T0, T1 = tm
            V(lambda e: e.tensor_tensor(out=v3(T0[:]), in0=v3(Xr[:]), in1=bc(ERE[p]), op=ALU.mult), [Xr, ERE[p]], [T0]); yield
            V(lambda e: e.tensor_tensor(out=v3(T1[:]), in0=v3(Xi[:]), in1=bc(EIM[p]), op=ALU.mult), [Xi, EIM[p]], [T1]); yield
            k.op('pool', lambda e: e.tensor_tensor(out=Rr[:], in0=T0[:], in1=T1[:], op=ALU.subtract), [T0, T1], [Rr]); yield
            V(lambda e: e.tensor_tensor(out=v3(T0[:]), in0=v3(Xi[:]), in1=bc(ERE[p]), op=ALU.mult), [Xi, ERE[p]], [T0]); yield
            V(lambda e: e.tensor_tensor(out=v3(T1[:]), in0=v3(Xr[:]), in1=bc(EIM[p]), op=ALU.mult), [Xr, EIM[p]], [T1]); yield
            k.op('pool', lambda e: e.tensor_tensor(out=Ri[:], in0=T0[:], in1=T1[:], op=ALU.add), [T0, T1], [Ri]); yield
            for c in range(NCH):
                cs = slice(c * 128, (c + 1) * 128)
                Ir, Ii = ini[p]
                V(lambda e: e.tensor_tensor_scan(out=Rr[:, cs], data0=DEC[p][:], data1=Rr[:, cs], initial=Ir[:], op0=ALU.mult, op1=ALU.add), [DEC[p], Rr, Ir], [Rr]); yield
                V(lambda e: e.tensor_tensor_scan(out=Ri[:, cs], data0=DEC[p][:], data1=Ri[:, cs], initial=Ii[:], op0=ALU.mult, op1=ALU.add), [DEC[p], Ri, Ii], [Ri]); yield
                lastr = Rr[:, c * 128 + 127:c * 128 + 128]; lasti = Ri[:, c * 128 + 127:c * 128 + 128]
                V(lambda e: e.tensor_tensor(out=it[p][:], in0=lasti, in1=Rim[:, p:p + 1], op=ALU.mult), [Ri, Rim], [it[p]]); yield
                V(lambda e: e.scalar_tensor_tensor(out=Ir[:], in0=lastr, scalar=Rre[:, p:p + 1], in1=it[p][:], op0=ALU.mult, op1=ALU.subtract), [Rr, Rre, it[p]], [Ir]); yield
                V(lambda e: e.tensor_tensor(out=it[p][:], in0=lastr, in1=Rim[:, p:p + 1], op=ALU.mult), [Rr, Rim], [it[p]]); yield
                V(lambda e: e.scalar_tensor_tensor(out=Ii[:], in0=lasti, scalar=Rre[:, p:p + 1], in1=it[p][:], op0=ALU.mult, op1=ALU.add), [Ri, Rre, it[p]], [Ii]); yield
            P1, P2, P3, P4 = PP[p % 2]
            k.op('pool', lambda e: e.tensor_tensor(out=v3(P1[:]), in0=v3(Rr[:]), in1=bc(COS[p]), op=ALU.mult), [Rr, COS[p]], [P1]); yield
            k.op('pool', lambda e: e.tensor_tensor(out=v3(P2[:]), in0=v3(Ri[:]), in1=bc(SIN[p]), op=ALU.mult), [Ri, SIN[p]], [P2]); yield
            k.op('pool', lambda e: e.tensor_tensor(out=v3(P3[:]), in0=v3(Rr[:]), in1=bc(SIN[p]), op=ALU.mult), [Rr, SIN[p]], [P3]); yield
            k.op('pool', lambda e: e.tensor_tensor(out=v3(P4[:]), in0=v3(Ri[:]), in1=bc(COS[p]), op=ALU.mult), [Ri, COS[p]], [P4]); yield
            k.op('pe', lambda e: e.matmul(PYb[:], creT[:, p, :], P1[:], start=(p == 0), stop=False), [creT, P1], [PYb]); yield
            k.op('pe', lambda e: e.matmul(PYb[:], ncreT[:, p, :], P2[:], start=False, stop=False), [ncreT, P2], [PYb]); yield
            k.op('pe', lambda e: e.matmul(PYb[:], ncimT[:, p, :], P3[:], start=False, stop=False), [ncimT, P3], [PYb]); yield
            k.op('pe', lambda e: e.matmul(PYb[:], ncimT[:, p, :], P4[:], start=False, stop=(p == NP - 1)), [ncimT, P4], [PYb]); yield
        YO = yout[b % 2]
        V(lambda e: e.scalar_tensor_tensor(out=YO[:], in0=Z[:], scalar=dT[:, 0:1], in1=PYb[:], op0=ALU.mult, op1=ALU.add), [Z, dT, PYb], [YO]); yield
        k.dma('sp', yo[:, ts], YO[:], rbuf=YO, grp=gy[b % 2]); yield

def s5_layout(core, zT_s5, lam_re, lam_im, log_step, b_re, b_im, c_re, c_im, d):
    import numpy as np
    g0 = 6 * core
    f = np.float32
    lamre = np.zeros((128, 3), f); lamim = np.zeros((128, 3), f); lstep = np.zeros((128, 3), f)
    bre = np.zeros((96, 3, 128), f); bim = np.zeros((96, 3, 128), f); cre = np.zeros((128, 3, 96), f); cim = np.zeros((128, 3, 96), f)
    for p in range(3):
        for gl in range(2):
            g = g0 + 2 * p + gl; gi = 2 * p + gl
            lamre[gl * 64:(gl + 1) * 64, p] = lam_re[g]; lamim[gl * 64:(gl + 1) * 64, p] = lam_im[g]; lstep[gl * 64:(gl + 1) * 64, p] = log_step[g]
            bre[gi * 16:(gi + 1) * 16, p, gl * 64:(gl + 1) * 64] = b_re[g].T
            bim[gi * 16:(gi + 1) * 16, p, gl * 64:(gl + 1) * 64] = b_im[g].T
            cre[gl * 64:(gl + 1) * 64, p, gi * 16:(gi + 1) * 16] = c_re[g].T
            cim[gl * 64:(gl + 1) * 64, p, gi * 16:(gi + 1) * 16] = c_im[g].T
    return {"s_" + n_: v_ for n_, v_ in {"zs": np.ascontiguousarray(zT_s5[g0 * 16:(g0 + 6) * 16]), "lamre": lamre, "lamim": lamim, "lstep": lstep, "bre": bre, "bim": bim,
            "cre": cre, "cim": cim, "dvec": np.ascontiguousarray(d[g0 * 16:(g0 + 6) * 16].reshape(96, 1)),
            "tau": np.broadcast_to(np.arange(128, dtype=f), (128, 128)).copy()}.items()}

def gla_consts():
    import numpy as np
    s = np.arange(128)[:, None]; t = np.arange(128)[None, :]
    same = (s // 64) == (t // 64)
    f = np.float32
    return {"tri_le": (same & (s <= t)).astype(f), "tri_gt": (same & (s > t)).astype(f),
            "chind": np.stack([(np.arange(128) < 64), (np.arange(128) >= 64)], 1).astype(f)}
def emit_GLA(k, ntok, banks, pfx="g_"):
    D_ = lambda n, shp, kind="ExternalInput": k.dram(pfx + n, shp, F32, kind)
    qT = D_("qT", [64, ntok]); kT = D_("kT", [64, ntok]); ktok = D_("ktok", [ntok, 64]); vtok = D_("vtok", [ntok, 128]); gtok = D_("gtok", [ntok, 128])
    ainT = D_("ainT", [16, ntok]); alora = D_("alora", [16, 64]); abias = D_("abias", [128, 64]); normg = D_("normg", [128, 128])
    tri_le = D_("tri_le", [128, 128]); tri_gt = D_("tri_gt", [128, 128]); chind = D_("chind", [128, 2])
    otok = D_("otok", [ntok, 128], "ExternalOutput")
    sb = lambda n, shp, dt=F32: k.sb(pfx + n, shp, dt)
    aloraT = sb("aloraT", [16, 64]); abiasT = sb("abiasT", [128, 64]); normgT = sb("normgT", [128, 128])
    TLE = sb("TLE", [128, 128]); TGT = sb("TGT", [128, 128]); CHI = sb("CHI", [128, 2])
    gcon = k.group(pfx + "con"); gl = [k.group(pfx + f"l{i}") for i in range(2)]; go = [k.group(pfx + f"o{i}") for i in range(2)]
    for t, d in ((aloraT, alora), (abiasT, abias), (normgT, normg), (TLE, tri_le), (TGT, tri_gt), (CHI, chind)):
        k.dma('sp', t[:], d[:, :], wbuf=t, grp=gcon)
    NB = 2
    QT = [sb(f"QT{i}", [64, 128]) for i in range(NB)]; KT = [sb(f"KT{i}", [64, 128]) for i in range(NB)]
    KK = [sb(f"KK{i}", [128, 64]) for i in range(NB)]; VV = [sb(f"VV{i}", [128, 128]) for i in range(NB)]; GG = [sb(f"GG{i}", [128, 128]) for i in range(NB)]
    AIN = [sb(f"AIN{i}", [16, 128]) for i in range(NB)]
    xb = sb("xb", [128, 64]); ax = sb("ax", [128, 64]); la = sb("la", [128, 64]); mn = sb("mn", [128, 64])
    EP = sb("EP", [64, 128]); EN = sb("EN", [64, 128]); ER = sb("ER", [128, 64])
    QF = sb("QF", [64, 128]); KF = sb("KF", [64, 128]); KH = sb("KH", [128, 64])
    Q0 = sb("Q0", [64, 128]); Q1 = sb("Q1", [64, 128])
    NM = sb("NM", [128, 128]); EB = sb("EB", [64, 2])
    S = [sb(f"S{i}", [64, 128]) for i in range(3)]
    ss = sb("ss", [128, 1]); junk = sb("junk", [128, 128]); SG = sb("SG", [128, 128]); O1 = sb("O1", [128, 128])
    OO = [sb(f"OO{i}", [128, 128]) for i in range(2)]
    bankA = [banks[0]]; bankB = [banks[1]]; bankC = [banks[2]]
    def views(i):
        A, B, C = bankA[i], bankB[i], bankC[i]
        return dict(px=k.view(A, A[:, 0:64]), pc=k.view(A, A[:, 64:128]), pr=k.view(A, A[:, 128:192]),
                    pbl=k.view(A, A[0:64, 192:194]), pcT=k.view(A, A[0:64, 256:384]),
                    pN=k.view(B, B[:, 0:128]), pO=k.view(B, B[:, 128:256]), pS0=k.view(C, C[0:64, 0:128]), pS1=k.view(C, C[0:64, 128:256]))
    PV = [views(0), views(0)]
    V = lambda fn, r, w: k.op('dve', fn, r, w)
    A_ = lambda fn, r, w: k.op('act', fn, r, w)
    P_ = lambda fn, r, w: k.op('pool', fn, r, w)
    T_ = lambda fn, r, w: k.op('pe', fn, r, w)
    P_(lambda e: e.memset(Q0[:], 0.0), [], [Q0]); P_(lambda e: e.memset(Q1[:], 0.0), [], [Q1]); P_(lambda e: e.memset(S[0][:], 0.0), [], [S[0]])
    si = 0
    def load(ti):
        b = ti % NB; r0 = ti * 128
        k.dma('sp', QT[b][:], qT[:, r0:r0 + 128], wbuf=QT[b], grp=gl[b]); k.dma('sp', KT[b][:], kT[:, r0:r0 + 128], wbuf=KT[b], grp=gl[b])
        k.dma('sp', KK[b][:], ktok[r0:r0 + 128, :], wbuf=KK[b], grp=gl[b]); k.dma('sp', VV[b][:], vtok[r0:r0 + 128, :], wbuf=VV[b], grp=gl[b])
        k.dma('sp', GG[b][:], gtok[r0:r0 + 128, :], wbuf=GG[b], grp=gl[b]); k.dma('sp', AIN[b][:], ainT[:, r0:r0 + 128], wbuf=AIN[b], grp=gl[b])
    load(0)
    for ti in range(ntok // 128):
        b = ti % NB; r0 = ti * 128; pv = PV[ti % 2]
        if ti + 1 < ntok // 128: load(ti + 1)
        px, pc, pr, pbl, pcT, pN, pO, pS0, pS1 = (pv[n] for n in ("px", "pc", "pr", "pbl", "pcT", "pN", "pO", "pS0", "pS1"))
        T_(lambda e: e.matmul(px[:], AIN[b][:], aloraT[:], start=True, stop=True), [AIN[b], aloraT], [px]); yield
        V(lambda e: e.tensor_tensor(out=xb[:], in0=px[:], in1=abiasT[:], op=ALU.add), [px, abiasT], [xb]); yield
        A_(lambda e: e.activation(out=ax[:], in_=xb[:], func=AF.Abs), [xb], [ax]); yield
        A_(lambda e: e.activation(out=ax[:], in_=ax[:], func=AF.Exp, scale=-1.0), [ax], [ax]); yield
        A_(lambda e: e.activation(out=ax[:], in_=ax[:], func=AF.Ln, bias=1.0), [ax], [ax]); yield
        V(lambda e: e.tensor_scalar(out=mn[:], in0=xb[:], scalar1=0.0, scalar2=None, op0=ALU.min), [xb], [mn]); yield
        V(lambda e: e.tensor_tensor(out=la[:], in0=mn[:], in1=ax[:], op=ALU.subtract), [mn, ax], [la]); yield
        T_(lambda e: e.matmul(pcT[:], la[:], TLE[:], start=True, stop=True), [la, TLE], [pcT]); yield
        T_(lambda e: e.matmul(pr[:], TGT[:], la[:], start=True, stop=True), [la, TGT], [pr]); yield
        T_(lambda e: e.matmul(pbl[:], la[:], CHI[:], start=True, stop=True), [la, CHI], [pbl]); yield
        A_(lambda e: e.activation(out=EP[:], in_=pcT[:], func=AF.Exp, scale=1.0 / 16, bias=math.log(0.125)), [pcT], [EP]); yield
        A_(lambda e: e.activation(out=EN[:], in_=pcT[:], func=AF.Exp, scale=-1.0 / 16), [pcT], [EN]); yield
        A_(lambda e: e.activation(out=ER[:], in_=pr[:], func=AF.Exp, scale=1.0 / 16), [pr], [ER]); yield
        A_(lambda e: e.activation(out=EB[:], in_=pbl[:], func=AF.Exp, scale=1.0 / 16), [pbl], [EB]); yield
        V(lambda e: e.tensor_tensor(out=QF[:], in0=QT[b][:], in1=EP[:], op=ALU.mult), [QT[b], EP], [QF]); yield
        V(lambda e: e.tensor_tensor(out=KF[:], in0=KT[b][:], in1=EN[:], op=ALU.mult), [KT[b], EN], [KF]); yield
        V(lambda e: e.tensor_tensor(out=KH[:], in0=KK[b][:], in1=ER[:], op=ALU.mult), [KK[b], ER], [KH]); yield
        P_(lambda e: e.tensor_copy(out=Q0[:, 0:64], in_=QF[:, 0:64]), [QF], [Q0]); yield
        P_(lambda e: e.tensor_copy(out=Q1[:, 64:128], in_=QF[:, 64:128]), [QF], [Q1]); yield
        T_(lambda e: e.matmul(pN[:], KF[:], QF[:], start=True, stop=True), [KF, QF], [pN]); yield
        V(lambda e: e.tensor_tensor(out=NM[:], in0=pN[:], in1=TLE[:], op=ALU.mult), [pN, TLE], [NM]); yield
        S0 = S[si % 3]; S1 = S[(si + 1) % 3]; S2 = S[(si + 2) % 3]; si += 2
        T_(lambda e: e.matmul(pO[:], NM[:], VV[b][:], start=True, stop=False), [NM, VV[b]], [pO]); yield
        T_(lambda e: e.matmul(pO[:], Q0[:], S0[:], start=False, stop=False), [Q0, S0], [pO]); yield
        T_(lambda e: e.matmul(pS0[:], KH[0:64, :], VV[b][0:64, :], start=True, stop=True), [KH, VV[b]], [pS0]); yield
        V(lambda e: e.scalar_tensor_tensor(out=S1[:], in0=S0[:], scalar=EB[:, 0:1], in1=pS0[:], op0=ALU.mult, op1=ALU.add), [S0, EB, pS0], [S1]); yield
        T_(lambda e: e.matmul(pO[:], Q1[:], S1[:], start=False, stop=True), [Q1, S1], [pO]); yield
        T_(lambda e: e.matmul(pS1[:], KH[64:128, :], VV[b][64:128, :], start=True, stop=True), [KH, VV[b]], [pS1]); yield
        V(lambda e: e.scalar_tensor_tensor(out=S2[:], in0=S1[:], scalar=EB[:, 1:2], in1=pS1[:], op0=ALU.mult, op1=ALU.add), [S1, EB, pS1], [S2]); yield
        A_(lambda e: e.activation(out=junk[:], in_=pO[:], func=AF.Square, accum_out=ss[:]), [pO], [junk, ss]); yield
        A_(lambda e: e.activation(out=ss[:], in_=ss[:], func=AF.Sqrt, scale=1.0 / 128, bias=1e-6), [ss], [ss]); yield
        V(lambda e: e.reciprocal(out=ss[:], in_=ss[:]), [ss], [ss]); yield
        A_(lambda e: e.activation(out=SG[:], in_=GG[b][:], func=AF.Silu), [GG[b]], [SG]); yield
        V(lambda e: e.scalar_tensor_tensor(out=O1[:], in0=pO[:], scalar=ss[:, 0:1], in1=normgT[:], op0=ALU.mult, op1=ALU.mult), [pO, ss, normgT], [O1]); yield
        OB = OO[ti % 2]
        V(lambda e: e.tensor_tensor(out=OB[:], in0=O1[:], in1=SG[:], op=ALU.mult), [O1, SG], [OB]); yield
        k.dma('sp', otok[r0:r0 + 128, :], OB[:], rbuf=OB, grp=go[ti % 2]); yield

def gla_layout(h, z_gla, alpha_lora, alpha_bias, norm_g):
    import numpy as np
    f = np.float32; ntok = z_gla.shape[0]
    d = dict(gla_consts())
    if h is None:
        z = lambda *s: np.zeros(s, f)
        d.update(qT=z(64, ntok), kT=z(64, ntok), ktok=z(ntok, 64), vtok=z(ntok, 128), gtok=z(ntok, 128), ainT=z(16, ntok), alora=z(16, 64), abias=z(128, 64), normg=z(128, 128))
    else:
        q = z_gla[:, h * 64:(h + 1) * 64]; kk = z_gla[:, 320 + h * 64:320 + (h + 1) * 64]
        v = z_gla[:, 640 + h * 128:640 + (h + 1) * 128]; g = z_gla[:, 1280 + h * 128:1280 + (h + 1) * 128]; a = z_gla[:, 1920:1936]
        c = np.ascontiguousarray
        d.update(qT=c(q.T), kT=c(kk.T), ktok=c(kk), vtok=c(v), gtok=c(g), ainT=c(a.T), alora=c(alpha_lora[:, h * 64:(h + 1) * 64]),
                 abias=c(np.broadcast_to(alpha_bias[h * 64:(h + 1) * 64], (128, 64))), normg=c(np.broadcast_to(norm_g[h * 128:(h + 1) * 128], (128, 128))))
    return {"g_" + n: v for n, v in d.items()}

RW_SHARED = ("loraT", "mu_w", "mu_a", "mu_g", "tri_le", "tri_gt", "mask4", "ident", "chind", "vresT", "mu_v")
def rw_consts():
    import numpy as np
    s = np.arange(128)[:, None]; t = np.arange(128)[None, :]
    same = (s // 64) == (t // 64)
    f = np.float32
    tle = (same & (s <= t)).astype(f); tlt = (same & (s < t)).astype(f); tgt = (same & (s > t)).astype(f)
    return {"tri_le": tle, "tri_gt": tgt, "mask4": np.concatenate([tlt, tle, tlt, tle], 1),
            "ident": np.eye(128, dtype=f), "chind": np.stack([(np.arange(128) < 64), (np.arange(128) >= 64)], 1).astype(f)}

NLEV = 5
def emit_RW(k, ntok, pfx, has_vres, banks, shared):
    def D_(n, shp, kind="ExternalInput"):
        if n in RW_SHARED:
            if n not in shared: shared[n] = k.dram("rs_" + n, shp, F32, kind)
            return shared[n]
        return k.dram(pfx + n, shp, F32, kind)
    rkv = D_("rkv", [ntok + 1, 192]); mu_rkv = D_("mu_rkv", [128, 192])
    loraT = D_("loraT", [480, ntok + 1]); mu_w = D_("mu_w", [96, 1]); mu_a = D_("mu_a", [128, 1]); mu_g = D_("mu_g", [128, 2])
    w_lora = D_("w_lora", [96, 64]); a_lora = D_("a_lora", [128, 64]); g_lora = D_("g_lora", [128, 2, 64])
    bc5 = D_("bc5", [128, 7, 64])
    tri_le = D_("tri_le", [128, 128]); tri_gt = D_("tri_gt", [128, 128]); mask4 = D_("mask4", [128, 512]); ident = D_("ident", [128, 128]); chind = D_("chind", [128, 2])
    if has_vres:
        vfirst = D_("vfirst", [ntok, 64]); vresT = D_("vresT", [64, ntok + 1]); mu_v = D_("mu_v", [64, 1]); vres_b = D_("vres_b", [64, 64]); vbias = D_("vbias", [128, 64])
    ytok = D_("ytok", [ntok, 64], "ExternalOutput")
    if not has_vres:
        vout = D_("vout", [ntok, 64], "ExternalOutput")
    sb = lambda n, shp, dt=F32: k.sb(pfx + n, shp, dt)
    MU = sb("MU", [128, 192]); MUW = sb("MUW", [96, 1]); MUA = sb("MUA", [128, 1]); MUG = sb("MUG", [128, 2])
    WL = sb("WL", [96, 64]); AL = sb("AL", [128, 64]); GL = sb("GL", [128, 2, 64]); BC = sb("BC", [128, 7, 64])
    TLE = sb("TLE", [128, 128]); TGT = sb("TGT", [128, 128]); M4 = sb("M4", [128, 512]); ID = sb("ID", [128, 128]); CHI = sb("CHI", [128, 2])
    loads = [(MU, mu_rkv), (MUW, mu_w), (MUA, mu_a), (MUG, mu_g), (WL, w_lora), (AL, a_lora), (GL, g_lora), (BC, bc5), (TLE, tri_le), (TGT, tri_gt), (M4, mask4), (ID, ident), (CHI, chind)]
    if has_vres:
        MUV = sb("MUV", [64, 1]); VB = sb("VB", [64, 64]); VBI = sb("VBI", [128, 64])
        loads += [(MUV, mu_v), (VB, vres_b), (VBI, vbias)]
    gcon = k.group(pfx + "con"); gl = [k.group(pfx + f"l{i}") for i in range(2)]; go = [k.group(pfx + f"o{i}") for i in range(2)]
    for t, d in loads:
        k.dma('sp', t[:], d, wbuf=t, grp=gcon)
    W0, A0, KKW, KA, RK, LNW, LNB = (BC[:, i, :] for i in range(7))
    NB = 2
    CUR = [sb(f"CUR{i}", [128, 192]) for i in range(NB)]; PRV = [sb(f"PRV{i}", [128, 192]) for i in range(NB)]
    LWc = [sb(f"LWc{i}", [96, 129]) for i in range(NB)]; LAc = [sb(f"LAc{i}", [128, 129]) for i in range(NB)]; LGc = [sb(f"LGc{i}", [128, 2, 129]) for i in range(NB)]
    if has_vres:
        VF = [sb(f"VF{i}", [128, 64]) for i in range(NB)]; VRc = [sb(f"VRc{i}", [64, 129]) for i in range(NB)]
        vS = sb("vS", [64, 128]); vD = sb("vD", [64, 128]); xv = sb("xv", [128, 64])
    Z = sb("Z", [128, 192]); Dz = sb("Dz", [128, 192])
    wS = sb("wS", [96, 128]); wD = sb("wD", [96, 128]); aS = sb("aS", [128, 128]); aD = sb("aD", [128, 128]); gS = sb("gS", [128, 2, 128]); gD = sb("gD", [128, 2, 128])
    xw = sb("xw", [128, 64]); t64 = [sb(f"t64_{i}", [128, 64]) for i in range(6)]
    ew = sb("ew", [128, 64]); asg = sb("asg", [128, 64]); gg_ = [sb(f"gg{i}", [128, 64]) for i in range(2)]; VP_ = [sb(f"VP{i}", [128, 64]) for i in range(2)]
    kk = sb("kk", [128, 64]); kkn = sb("kkn", [128, 64]); bv = sb("bv", [128, 64]); k2 = sb("k2", [128, 64])
    col = [sb(f"col{i}", [128, 1]) for i in range(6)]; junk = sb("junk", [128, 64]); junk2 = sb("junk2", [128, 64]); bsum_ = [sb(f"bsum{i}", [128, 1]) for i in range(2)]
    Em = sb("Em", [128, 64]); Ep = sb("Ep", [128, 64]); Eme = sb("Eme", [128, 64]); Er = sb("Er", [128, 64]); cex = sb("cex", [128, 64])
    T4 = sb("T4", [128, 4, 64])
    KH_ = [sb(f"KH{i}", [128, 64]) for i in range(2)]; BH_ = [sb(f"BH{i}", [128, 64]) for i in range(2)]; PC_ = [sb(f"PC{i}", [64, 2]) for i in range(2)]
    FT = sb("FT", [64, 512])
    AT0_ = [sb(f"AT0{i}", [64, 128]) for i in range(2)]; AT1_ = [sb(f"AT1{i}", [64, 128]) for i in range(2)]; RT0_ = [sb(f"RT0{i}", [64, 128]) for i in range(2)]; RT1_ = [sb(f"RT1{i}", [64, 128]) for i in range(2)]
    NM_ = [sb(f"NM{i}", [128, 512]) for i in range(2)]; Ncur = [sb(f"Ncur{i}", [128, 128]) for i in range(2)]; Lcur = [sb(f"Lcur{i}", [128, 128]) for i in range(2)]
    X_ = [sb(f"X{i}", [128, 128]) for i in range(2)]; Y = sb("Y", [128, 128])
    RHS = sb("RHS", [128, 64]); U = sb("U", [128, 64])
    S = [sb(f"S{i}", [64, 64]) for i in range(3)]
    yc = sb("yc", [128, 64]); yn = sb("yn", [128, 64]); YO = [sb(f"YO{i}", [128, 64]) for i in range(2)]; VO = [sb(f"VO{i}", [128, 64]) for i in range(2)]
    B0, B1, B2, B3 = banks
    vw = k.view
    pw = vw(B0, B0[:, 0:64]); pa = vw(B0, B0[:, 64:128]); pg = vw(B0, B0[:, 128:192]); pvg = vw(B0, B0[:, 192:256])
    pce = vw(B0, B0[:, 256:320]); prem = vw(B0, B0[:, 320:384]); pPC = vw(B0, B0[0:64, 0:2]); pYu = vw(B0, B0[:, 384:512])
    pT = vw(B1, B1[0:64, :]); pNM = B1
    pL = vw(B2, B2[:, 0:128]); pN2 = vw(B2, B2[:, 128:256]); pL2 = vw(B2, B2[:, 256:384]); pXu = vw(B2, B2[:, 384:512])
    pR = vw(B3, B3[:, 0:64]); pU = vw(B3, B3[:, 64:128]); pS = vw(B3, B3[0:64, 128:192]); pYo = vw(B3, B3[:, 192:256])
    V = lambda fn, r, w: k.op('dve', fn, r, w)
    A_ = lambda fn, r, w: k.op('act', fn, r, w)
    P_ = lambda fn, r, w: k.op('pool', fn, r, w)
    T_ = lambda fn, r, w: k.op('pe', fn, r, w)
    for t in (*AT0_, *AT1_, *RT0_, *RT1_, S[0]):
        P_(lambda e: e.memset(t[:], 0.0), [], [t])
    si = 0
    def load(ti):
        b = ti % NB; r0 = ti * 128
        k.dma('sp', CUR[b][:], rkv[r0 + 1:r0 + 129, :], wbuf=CUR[b], grp=gl[b]); k.dma('sp', PRV[b][:], rkv[r0:r0 + 128, :], wbuf=PRV[b], grp=gl[b])
        k.dma('sp', LWc[b][:], loraT[0:96, r0:r0 + 129], wbuf=LWc[b], grp=gl[b]); k.dma('sp', LAc[b][:], loraT[96:224, r0:r0 + 129], wbuf=LAc[b], grp=gl[b])
        k.dma('sp', LGc[b][:], loraT[224:480, r0:r0 + 129].rearrange("(c p) t -> p c t", p=128), wbuf=LGc[b], grp=gl[b])
        if has_vres:
            k.dma('sp', VF[b][:], vfirst[r0:r0 + 128, :], wbuf=VF[b], grp=gl[b]); k.dma('sp', VRc[b][:], vresT[:, r0:r0 + 129], wbuf=VRc[b], grp=gl[b])
    ntiles = ntok // 128
    def prep(ti):
        b = ti % NB; r0 = ti * 128
        sl = ti % 2; NM = NM_[sl]; X = X_[sl]; AT0 = AT0_[sl]; AT1 = AT1_[sl]; RT0 = RT0_[sl]; RT1 = RT1_[sl]; KH = KH_[sl]; BH = BH_[sl]; PC = PC_[sl]; VP = VP_[sl]; gg = gg_[sl]; bsum = bsum_[sl]
        if ti + 1 < ntiles: load(ti + 1)
        V(lambda e: e.tensor_tensor(out=Dz[:], in0=PRV[b][:], in1=CUR[b][:], op=ALU.subtract), [PRV[b], CUR[b]], [Dz]); yield
        V(lambda e: e.tensor_tensor(out=Dz[:], in0=Dz[:], in1=MU[:], op=ALU.mult), [Dz, MU], [Dz]); yield
        V(lambda e: e.tensor_tensor(out=Z[:], in0=Dz[:], in1=CUR[b][:], op=ALU.add), [Dz, CUR[b]], [Z]); yield
        r_ = Z[:, 0:64]; k_ = Z[:, 64:128]; v_ = Z[:, 128:192]
        P_(lambda e: e.tensor_tensor(out=wD[:], in0=LWc[b][:, 0:128], in1=LWc[b][:, 1:129], op=ALU.subtract), [LWc[b]], [wD]); yield
        V(lambda e: e.scalar_tensor_tensor(out=wS[:], in0=wD[:], scalar=MUW[:, 0:1], in1=LWc[b][:, 1:129], op0=ALU.mult, op1=ALU.add), [wD, MUW, LWc[b]], [wS]); yield
        P_(lambda e: e.tensor_tensor(out=aD[:], in0=LAc[b][:, 0:128], in1=LAc[b][:, 1:129], op=ALU.subtract), [LAc[b]], [aD]); yield
        V(lambda e: e.scalar_tensor_tensor(out=aS[:], in0=aD[:], scalar=MUA[:, 0:1], in1=LAc[b][:, 1:129], op0=ALU.mult, op1=ALU.add), [aD, MUA, LAc[b]], [aS]); yield
        for c in range(2):
            P_(lambda e: e.tensor_tensor(out=gD[:, c, :], in0=LGc[b][:, c, 0:128], in1=LGc[b][:, c, 1:129], op=ALU.subtract), [LGc[b]], [gD]); yield
            V(lambda e: e.scalar_tensor_tensor(out=gS[:, c, :], in0=gD[:, c, :], scalar=MUG[:, c:c + 1], in1=LGc[b][:, c, 1:129], op0=ALU.mult, op1=ALU.add), [gD, MUG, LGc[b]], [gS]); yield
        A_(lambda e: e.activation(out=wS[:], in_=wS[:], func=AF.Tanh), [wS], [wS]); yield
        A_(lambda e: e.activation(out=gS[:], in_=gS[:], func=AF.Sigmoid), [gS], [gS]); yield
        T_(lambda e: e.matmul(pw[:], wS[:], WL[:], start=True, stop=True), [wS, WL], [pw]); yield
        T_(lambda e: e.matmul(pa[:], aS[:], AL[:], start=True, stop=True), [aS, AL], [pa]); yield
        T_(lambda e: e.matmul(pg[:], gS[:, 0, :], GL[:, 0, :], start=True, stop=False), [gS, GL], [pg]); yield
        T_(lambda e: e.matmul(pg[:], gS[:, 1, :], GL[:, 1, :], start=False, stop=True), [gS, GL], [pg]); yield
        if has_vres:
            P_(lambda e: e.tensor_tensor(out=vD[:], in0=VRc[b][:, 0:128], in1=VRc[b][:, 1:129], op=ALU.subtract), [VRc[b]], [vD]); yield
            V(lambda e: e.scalar_tensor_tensor(out=vS[:], in0=vD[:], scalar=MUV[:, 0:1], in1=VRc[b][:, 1:129], op0=ALU.mult, op1=ALU.add), [vD, MUV, VRc[b]], [vS]); yield
            T_(lambda e: e.matmul(pvg[:], vS[:], VB[:], start=True, stop=True), [vS, VB], [pvg]); yield
        V(lambda e: e.tensor_tensor(out=xw[:], in0=pw[:], in1=W0, op=ALU.add), [pw, BC], [xw]); yield
        ax, mn, ta = t64[0], t64[1], t64[2]
        A_(lambda e: e.activation(out=ax[:], in_=xw[:], func=AF.Abs), [xw], [ax]); yield
        A_(lambda e: e.activation(out=ax[:], in_=ax[:], func=AF.Exp, scale=-1.0), [ax], [ax]); yield
        A_(lambda e: e.activation(out=ax[:], in_=ax[:], func=AF.Ln, bias=1.0), [ax], [ax]); yield
        V(lambda e: e.tensor_scalar(out=mn[:], in0=xw[:], scalar1=0.0, scalar2=None, op0=ALU.min), [xw], [mn]); yield
        V(lambda e: e.tensor_tensor(out=mn[:], in0=mn[:], in1=ax[:], op=ALU.subtract), [mn, ax], [mn]); yield
        A_(lambda e: e.activation(out=ew[:], in_=mn[:], func=AF.Exp, bias=-0.5), [mn], [ew]); yield
        V(lambda e: e.tensor_tensor(out=ta[:], in0=pa[:], in1=A0, op=ALU.add), [pa, BC], [ta]); yield
        A_(lambda e: e.activation(out=asg[:], in_=ta[:], func=AF.Sigmoid), [ta], [asg]); yield
        A_(lambda e: e.copy(out=gg[:], in_=pg[:]), [pg], [gg]); yield
        if has_vres:
            V(lambda e: e.tensor_tensor(out=xv[:], in0=pvg[:], in1=VBI[:], op=ALU.add), [pvg, VBI], [xv]); yield
            A_(lambda e: e.activation(out=xv[:], in_=xv[:], func=AF.Sigmoid), [xv], [xv]); yield
            V(lambda e: e.tensor_tensor(out=VP[:], in0=VF[b][:], in1=v_, op=ALU.subtract), [VF[b], Z], [VP]); yield
            V(lambda e: e.tensor_tensor(out=VP[:], in0=VP[:], in1=xv[:], op=ALU.mult), [VP, xv], [VP]); yield
            V(lambda e: e.tensor_tensor(out=VP[:], in0=VP[:], in1=v_, op=ALU.add), [VP, Z], [VP]); yield
        else:
            P_(lambda e: e.tensor_copy(out=VP[:], in_=v_), [Z], [VP]); yield
        ssq, rn = col[0], col[1]
        V(lambda e: e.tensor_tensor(out=kk[:], in0=k_, in1=KKW, op=ALU.mult), [Z, BC], [kk]); yield
        A_(lambda e: e.activation(out=junk[:], in_=kk[:], func=AF.Square, accum_out=ssq[:]), [kk], [junk, ssq]); yield
        V(lambda e: e.tensor_scalar(out=rn[:], in0=ssq[:], scalar1=1e-24, scalar2=None, op0=ALU.max), [ssq], [rn]); yield
        A_(lambda e: e.activation(out=rn[:], in_=rn[:], func=AF.Sqrt), [rn], [rn]); yield
        V(lambda e: e.reciprocal(out=rn[:], in_=rn[:]), [rn], [rn]); yield
        V(lambda e: e.tensor_scalar(out=kkn[:], in0=kk[:], scalar1=rn[:, 0:1], scalar2=None, op0=ALU.mult), [kk, rn], [kkn]); yield
        V(lambda e: e.tensor_tensor(out=bv[:], in0=kkn[:], in1=asg[:], op=ALU.mult), [kkn, asg], [bv]); yield
        V(lambda e: e.scalar_tensor_tensor(out=k2[:], in0=asg[:], scalar=-1.0, in1=KA, op0=ALU.add, op1=ALU.mult), [asg, BC], [k2]); yield
        V(lambda e: e.scalar_tensor_tensor(out=k2[:], in0=k2[:], scalar=1.0, in1=k_, op0=ALU.add, op1=ALU.mult), [k2, Z], [k2]); yield
        tb = t64[3]
        V(lambda e: e.tensor_tensor(out=tb[:], in0=r_, in1=k2[:], op=ALU.mult), [Z, k2], [tb]); yield
        V(lambda e: e.scalar_tensor_tensor(out=junk[:], in0=tb[:], scalar=1.0, in1=RK, op0=ALU.mult, op1=ALU.mult, accum_out=bsum[:]), [tb, BC], [junk, bsum]); yield
        T_(lambda e: e.matmul(pce[:], TLE[:], ew[:], start=True, stop=True), [TLE, ew], [pce]); yield
        T_(lambda e: e.matmul(prem[:], TGT[:], ew[:], start=True, stop=True), [TGT, ew], [prem]); yield
        T_(lambda e: e.matmul(pPC[:], ew[:], CHI[:], start=True, stop=True), [ew, CHI], [pPC]); yield
        A_(lambda e: e.activation(out=Em[:], in_=pce[:], func=AF.Exp, scale=-1.0), [pce], [Em]); yield
        A_(lambda e: e.activation(out=Ep[:], in_=pce[:], func=AF.Exp), [pce], [Ep]); yield
        V(lambda e: e.tensor_tensor(out=cex[:], in0=pce[:], in1=ew[:], op=ALU.subtract), [pce, ew], [cex]); yield
        A_(lambda e: e.activation(out=Eme[:], in_=cex[:], func=AF.Exp, scale=-1.0), [cex], [Eme]); yield
        A_(lambda e: e.activation(out=Er[:], in_=prem[:], func=AF.Exp, scale=-1.0), [prem], [Er]); yield
        A_(lambda e: e.activation(out=PC[:], in_=pPC[:], func=AF.Exp, scale=-1.0), [pPC], [PC]); yield
        V(lambda e: e.scalar_tensor_tensor(out=T4[:, 0, :], in0=kkn[:], scalar=-1.0, in1=Eme[:], op0=ALU.mult, op1=ALU.mult), [kkn, Eme], [T4]); yield
        V(lambda e: e.tensor_tensor(out=T4[:, 1, :], in0=r_, in1=Em[:], op=ALU.mult), [Z, Em], [T4]); yield
        V(lambda e: e.tensor_tensor(out=T4[:, 2, :], in0=bv[:], in1=Ep[:], op=ALU.mult), [bv, Ep], [T4]); yield
        V(lambda e: e.tensor_tensor(out=T4[:, 3, :], in0=k2[:], in1=Ep[:], op=ALU.mult), [k2, Ep], [T4]); yield
        P_(lambda e: e.tensor_tensor(out=KH[:], in0=k2[:], in1=Er[:], op=ALU.mult), [k2, Er], [KH]); yield
        P_(lambda e: e.tensor_tensor(out=BH[:], in0=bv[:], in1=Er[:], op=ALU.mult), [bv, Er], [BH]); yield
        for j in range(4):
            T_(lambda e: e.matmul(pT[:, j * 128:(j + 1) * 128], T4[:, j, :], ID[:], start=True, stop=True), [T4, ID], [pT]); yield
        A_(lambda e: e.copy(out=FT[:], in_=pT[:]), [pT], [FT]); yield
        aT = FT[:, 0:128]; rT = FT[:, 128:256]; bT = FT[:, 256:384]; kT = FT[:, 384:512]
        P_(lambda e: e.tensor_copy(out=AT0[:, 0:64], in_=FT[:, 0:64]), [FT], [AT0]); yield
        P_(lambda e: e.tensor_copy(out=AT1[:, 64:128], in_=FT[:, 64:128]), [FT], [AT1]); yield
        P_(lambda e: e.tensor_copy(out=RT0[:, 0:64], in_=FT[:, 128:192]), [FT], [RT0]); yield
        P_(lambda e: e.tensor_copy(out=RT1[:, 64:128], in_=FT[:, 192:256]), [FT], [RT1]); yield
        T_(lambda e: e.matmul(pNM[:, 0:256], bT, FT[:, 0:256], start=True, stop=True), [FT], [pNM]); yield
        T_(lambda e: e.matmul(pNM[:, 256:512], kT, FT[:, 0:256], start=True, stop=True), [FT], [pNM]); yield
        T_(lambda e: e.matmul(pL[:], aT, bT, start=True, stop=True), [FT], [pL]); yield
        V(lambda e: e.tensor_tensor(out=NM[:], in0=pNM[:], in1=M4[:], op=ALU.mult), [pNM, M4], [NM]); yield
        NC_, LC_ = Ncur[0], Lcur[0]
        P_(lambda e: e.tensor_copy(out=NC_[:], in_=NM[:, 0:128]), [NM], [NC_]); yield
        V(lambda e: e.tensor_tensor(out=LC_[:], in0=pL[:], in1=TGT[:], op=ALU.mult), [pL, TGT], [LC_]); yield
        P_(lambda e: e.tensor_tensor(out=X[:], in0=NC_[:], in1=ID[:], op=ALU.add), [NC_, ID], [X]); yield
        P_(lambda e: e.tensor_tensor(out=Y[:], in0=LC_[:], in1=ID[:], op=ALU.add), [LC_, ID], [Y]); yield
        for lev in range(NLEV):
            last = lev == NLEV - 1
            Nn, Ln = Ncur[(lev + 1) % 2], Lcur[(lev + 1) % 2]
            T_(lambda e: e.matmul(pN2[:], LC_[:], NC_[:], start=True, stop=True), [LC_, NC_], [pN2]); yield
            if not last:
                T_(lambda e: e.matmul(pL2[:], NC_[:], LC_[:], start=True, stop=True), [LC_, NC_], [pL2]); yield
            A_(lambda e: e.copy(out=Nn[:], in_=pN2[:]), [pN2], [Nn]); yield
            if not last:
                V(lambda e: e.tensor_copy(out=Ln[:], in_=pL2[:]), [pL2], [Ln]); yield
            T_(lambda e: e.matmul(pXu[:], Y[:], Nn[:], start=True, stop=True), [Y, Nn], [pXu]); yield
            if not last:
                T_(lambda e: e.matmul(pYu[:], Nn[:], Y[:], start=True, stop=True), [Y, Nn], [pYu]); yield
            V(lambda e: e.tensor_tensor(out=X[:], in0=X[:], in1=pXu[:], op=ALU.add), [X, pXu], [X]); yield
            if not last:
                V(lambda e: e.tensor_tensor(out=Y[:], in0=Y[:], in1=pYu[:], op=ALU.add), [Y, pYu], [Y]); yield
            NC_, LC_ = Nn, Ln
    def chain(ti):
        r0 = ti * 128; s1, s2, rstd = col[3], col[4], col[5]
        sl = ti % 2; NM = NM_[sl]; X = X_[sl]; AT0 = AT0_[sl]; AT1 = AT1_[sl]; RT0 = RT0_[sl]; RT1 = RT1_[sl]; KH = KH_[sl]; BH = BH_[sl]; PC = PC_[sl]; VP = VP_[sl]; gg = gg_[sl]; bsum = bsum_[sl]
        Ss = [S[(2 * ti) % 3], S[(2 * ti + 1) % 3], S[(2 * ti + 2) % 3]]
        ATc = (AT0, AT1); RTc = (RT0, RT1)
        for c in range(2):
            ps_ = slice(c * 64, (c + 1) * 64)
            T_(lambda e: e.matmul(pR[:], NM[:, 256:384], VP[:], start=True, stop=False), [NM, VP], [pR]); yield
            T_(lambda e: e.matmul(pR[:], ATc[c][:], Ss[c][:], start=False, stop=True), [ATc[c], Ss[c]], [pR]); yield
            A_(lambda e: e.copy(out=RHS[ps_, :], in_=pR[ps_, :]), [pR], [RHS]); yield
            T_(lambda e: e.matmul(pU[:], X[ps_, :], RHS[ps_, :], start=True, stop=True), [X, RHS], [pU]); yield
            A_(lambda e: e.copy(out=U[ps_, :], in_=pU[ps_, :]), [pU], [U]); yield
            T_(lambda e: e.matmul(pS[:], BH[ps_, :], U[ps_, :], start=True, stop=False), [BH, U], [pS]); yield
            T_(lambda e: e.matmul(pS[:], KH[ps_, :], VP[ps_, :], start=False, stop=True), [KH, VP], [pS]); yield
            V(lambda e: e.scalar_tensor_tensor(out=Ss[c + 1][:], in0=Ss[c][:], scalar=PC[:, c:c + 1], in1=pS[:], op0=ALU.mult, op1=ALU.add), [Ss[c], PC, pS], [Ss[c + 1]]); yield
        T_(lambda e: e.matmul(pYo[:], NM[:, 128:256], U[:], start=True, stop=False), [NM, U], [pYo]); yield
        T_(lambda e: e.matmul(pYo[:], NM[:, 384:512], VP[:], start=False, stop=False), [NM, VP], [pYo]); yield
        T_(lambda e: e.matmul(pYo[:], RT0[:], Ss[0][:], start=False, stop=False), [RT0, Ss[0]], [pYo]); yield
        T_(lambda e: e.matmul(pYo[:], RT1[:], Ss[1][:], start=False, stop=True), [RT1, Ss[1]], [pYo]); yield
        A_(lambda e: e.activation(out=junk2[:], in_=pYo[:], func=AF.Copy, accum_out=s1[:]), [pYo], [junk2, s1]); yield
        V(lambda e: e.tensor_scalar(out=s1[:], in0=s1[:], scalar1=1.0 / 64, scalar2=None, op0=ALU.mult), [s1], [s1]); yield
        V(lambda e: e.tensor_scalar(out=yc[:], in0=pYo[:], scalar1=s1[:, 0:1], scalar2=None, op0=ALU.subtract), [pYo, s1], [yc]); yield
        A_(lambda e: e.activation(out=junk2[:], in_=yc[:], func=AF.Square, accum_out=s2[:]), [yc], [junk2, s2]); yield
        A_(lambda e: e.activation(out=rstd[:], in_=s2[:], func=AF.Sqrt, scale=1.0 / 64, bias=64e-5), [s2], [rstd]); yield
        V(lambda e: e.reciprocal(out=rstd[:], in_=rstd[:]), [rstd], [rstd]); yield
        V(lambda e: e.scalar_tensor_tensor(out=yn[:], in0=yc[:], scalar=rstd[:, 0:1], in1=LNW, op0=ALU.mult, op1=ALU.mult), [yc, rstd, BC], [yn]); yield
        V(lambda e: e.tensor_tensor(out=yn[:], in0=yn[:], in1=LNB, op=ALU.add), [yn, BC], [yn]); yield
        V(lambda e: e.scalar_tensor_tensor(out=yn[:], in0=VP[:], scalar=bsum[:, 0:1], in1=yn[:], op0=ALU.mult, op1=ALU.add), [VP, bsum, yn], [yn]); yield
        OB = YO[ti % 2]
        V(lambda e: e.tensor_tensor(out=OB[:], in0=yn[:], in1=gg[:], op=ALU.mult), [yn, gg], [OB]); yield
        k.dma('sp', ytok[r0:r0 + 128, :], OB[:], rbuf=OB, grp=go[ti % 2]); yield
        if not has_vres:
            OV = VO[ti % 2]
            P_(lambda e: e.tensor_copy(out=OV[:], in_=VP[:]), [VP], [OV]); yield
            k.dma('sp', vout[r0:r0 + 128, :], OV[:], rbuf=OV, grp=go[ti % 2]); yield

    load(0)
    yield from prep(0)
    for ti in range(ntiles):
        gc = chain(ti); gp = prep(ti + 1) if ti + 1 < ntiles else None
        while gc is not None or gp is not None:
            if gp is not None:
                try:
                    for _ in range(4):
                        next(gp); yield
                except StopIteration:
                    gp = None
            if gc is not None:
                try:
                    next(gc); yield
                except StopIteration:
                    gc = None
def rw_layout(pfx, h, z_rw, P, vfirst=None, vres_u=None):
    import numpy as np
    f = np.float32; ntok = z_rw.shape[0]; c = np.ascontiguousarray
    has_vres = vres_u is not None
    d = dict(rw_consts())
    zpad = lambda a: np.concatenate([np.zeros((1, a.shape[1]), f), a], 0)
    bc = lambda v: np.broadcast_to(v, (128, v.shape[-1]))
    if h is None:
        z = lambda *s: np.zeros(s, f)
        d.update(rkv=z(ntok + 1, 192), mu_rkv=z(128, 192), loraT=z(480, ntok + 1), mu_w=z(96, 1), mu_a=z(128, 1), mu_g=z(128, 2), w_lora=z(96, 64), a_lora=z(128, 64),
                 g_lora=z(128, 2, 64), bc5=z(128, 7, 64))
        if has_vres: d.update(vfirst=z(ntok, 64), vresT=z(64, ntok + 1), mu_v=z(64, 1), vres_b=z(64, 64), vbias=z(128, 64))
    else:
        hs = slice(h * 64, (h + 1) * 64)
        mu = P["rwkv_mu"]
        rkv = np.concatenate([z_rw[:, hs], z_rw[:, 640 + h * 64:640 + (h + 1) * 64], z_rw[:, 1280 + h * 64:1280 + (h + 1) * 64]], 1)
        mu_rkv = np.concatenate([mu[hs], mu[640 + h * 64:640 + (h + 1) * 64], mu[1280 + h * 64:1280 + (h + 1) * 64]])
        d.update(rkv=c(zpad(rkv)), mu_rkv=c(bc(mu_rkv)), loraT=c(zpad(z_rw[:, 1920:2400]).T),
                 mu_w=c(mu[1920:2016].reshape(96, 1)), mu_a=c(mu[2016:2144].reshape(128, 1)), mu_g=c(mu[2144:2400].reshape(2, 128).T),
                 w_lora=c(P["rwkv_w_lora"][:, hs]), a_lora=c(P["rwkv_a_lora"][:, hs]), g_lora=c(P["rwkv_g_lora"][:, hs].reshape(2, 128, 64).transpose(1, 0, 2)),
                 bc5=c(np.stack([bc(P[n][hs]) for n in ("rwkv_w0", "rwkv_a0", "rwkv_k_k", "rwkv_k_a", "rwkv_r_k", "rwkv_lnx_w", "rwkv_lnx_b")], 1)))
        if has_vres:
            d.update(vfirst=c(vfirst[:, hs]), vresT=c(zpad(vres_u).T), mu_v=c(P["rwkv_vres_mu"].reshape(64, 1)), vres_b=c(P["rwkv_vres_b"][:, hs]), vbias=c(bc(P["rwkv_vres_bias"][hs])))
    return {("rs_" if n in RW_SHARED else pfx) + n: v.astype(f) for n, v in d.items()}


def build_B(ntok, has_vres):
    k = KB()
    banks = [k.ps(f"bank{i}", [128, 512]) for i in range(8)]
    shared = {}
    ga = emit_RW(k, ntok, "r0_", has_vres, banks[0:4], shared)
    gb = emit_RW(k, ntok, "r1_", has_vres, banks[4:8], shared)
    alive = [ga, gb]
    while alive:
        for g_ in list(alive):
            try:
                next(g_)
            except StopIteration:
                alive.remove(g_)
    gg = emit_GLA(k, ntok, banks[0:3])
    gs = emit_S5(k, ntok, banks[3:7])
    alive = [(gg, 3), (gs, 2)]
    while alive:
        for it in list(alive):
            g_, n_ = it
            try:
                for _ in range(n_): next(g_)
            except StopIteration:
                alive.remove(it)
    return k.finish()

_PROGS = {}
def _prog(key, fn):
    if key not in _PROGS:
        _PROGS[key] = fn()
    return _PROGS[key]

def _tile16(v):
    return np.ascontiguousarray(np.asarray(v, np.float32).reshape(-1, 128).T)

def kernel(**inp):
    from concourse.bass_utils import run_bass_kernel_spmd
    f32 = np.float32
    inp = {n: np.asarray(v, f32) for n, v in inp.items()}
    x = inp["x"]; T = x.shape[1]; NCORE = 8; tpc = T // NCORE; depth = inp["w_in"].shape[0]
    cores = list(range(NCORE))
    c_ = np.ascontiguousarray
    xT = [c_(x[0, c * tpc:(c + 1) * tpc].T) for c in range(NCORE)]
    vfirst = None
    NCOLS = 11312
    for l in range(depth):
        extra = inp["rwkv_vres_a"][l - 1] if l > 0 else np.zeros((2048, 64), f32)
        wA = c_(np.concatenate([inp["w_in"][l], extra], 1))
        gA = _tile16(inp["norm_mix"][l])
        ncA = _prog(("A", tpc), lambda: build_A(tpc, NCOLS))
        res = run_bass_kernel_spmd(ncA, [{"xT": xT[c], "w": wA, "g": gA} for c in cores], core_ids=cores).results
        z = np.concatenate([res[c]["zT"].T for c in cores], 0)
        del res
        P = {n: inp[n][l] for n in ("rwkv_mu", "rwkv_w_lora", "rwkv_w0", "rwkv_a_lora", "rwkv_a0", "rwkv_g_lora", "rwkv_k_k", "rwkv_k_a", "rwkv_r_k", "rwkv_lnx_w", "rwkv_lnx_b")}
        has_vres = l > 0
        if has_vres:
            P.update(rwkv_vres_mu=inp["rwkv_vres_mu"][l - 1], rwkv_vres_b=inp["rwkv_vres_b"][l - 1], rwkv_vres_bias=inp["rwkv_vres_bias"][l - 1])
        zT_s5 = c_(z[:, :768].T); z_rw = z[:, 768:3168]; z_gla = z[:, 3168:5104]
        vres_u = c_(z[:, 11248:11312]) if has_vres else None
        ims = []
        for c in cores:
            m = {}
            m.update(s5_layout(c, zT_s5, inp["s5_lambda_re"][l], inp["s5_lambda_im"][l], inp["s5_log_step"][l], inp["s5_b_re"][l], inp["s5_b_im"][l],
                               inp["s5_c_re"][l], inp["s5_c_im"][l], inp["s5_d"][l]))
            for s in range(2):
                h = 2 * c + s
                m.update(rw_layout(f"r{s}_", h if h < 10 else None, z_rw, P, vfirst, vres_u))
            m.update(gla_layout(c if c < 5 else None, z_gla, inp["gla_alpha_lora"][l], inp["gla_alpha_bias"][l], inp["gla_norm_g"][l]))
            ims.append(m)
        ncB = _prog(("B", T, has_vres), lambda: build_B(T, has_vres))
        res = run_bass_kernel_spmd(ncB, ims, core_ids=cores).results
        del ims
        y = np.empty((T, 2048), f32)
        y[:, :768] = np.concatenate([res[c]["s_yo"] for c in cores], 0).T
        for h in range(10):
            y[:, 768 + h * 64:768 + (h + 1) * 64] = res[h // 2][f"r{h % 2}_ytok"]
        for h in range(5):
            y[:, 1408 + h * 128:1408 + (h + 1) * 128] = res[h]["g_otok"]
        if l == 0:
            vfirst = np.concatenate([res[h // 2][f"r{h % 2}_vout"] for h in range(10)], 1)
        del res
        last = l == depth - 1
        ncC = _prog(("C", tpc, last), lambda: build_C(tpc, last))
        gbt = _tile16(inp["gate_bias"][l])
        ims = []
        for c in cores:
            ts = slice(c * tpc, (c + 1) * tpc)
            ims.append({"xT": xT[c], "zgT": c_(z[ts, 5104:11248].T), "gb": gbt, "yT": c_(y[ts].T), "w_up": inp["w_up"][l], "w_out": inp["w_out"][l],
                        "nm": _tile16(inp["norm_mlp"][l]), "w1": inp["mlp_w1"][l], "w2": inp["mlp_w2"][l], "fn": _tile16(inp["final_norm"]),
                        "gluw": inp["s5_glu_w"][l], "glub": _tile16(inp["s5_glu_b"][l])})
        del z, y
        res = run_bass_kernel_spmd(ncC, ims, core_ids=cores).results
        del ims
        xT = [res[c]["xoT"] for c in cores]
        del res
    out = np.concatenate([xT[c].T for c in cores], 0)[None]
    return np.ascontiguousarray(out.astype(f32))
```

```python
import math
import numpy as np
from contextlib import ExitStack
import concourse.bass as bass
import concourse.mybir as mybir
F32 = mybir.dt.float32; BF16 = mybir.dt.bfloat16
AF = mybir.ActivationFunctionType; ALU = mybir.AluOpType; AX = mybir.AxisListType

class Buf:
    def __init__(self, t, name):
        self.t = t; self.name = name; self.lw = {}; self.rd = {}; self.sem = None; self.dcount = 0; self.excl = False
    def __getitem__(self, k):
        return self.t[k]

class Group:
    def __init__(self, name, sem):
        self.name = name; self.sem = sem; self.dcount = 0

class View:
    def __init__(self, parent, ap):
        object.__setattr__(self, 'parent', parent); object.__setattr__(self, 't', ap)
    def __getitem__(self, k): return self.t[k]
    def __getattr__(self, n): return getattr(self.parent, n)
    def __setattr__(self, n, v): setattr(self.parent, n, v)
    def __eq__(self, o): return (o.parent if isinstance(o, View) else o) is self.parent
    def __hash__(self): return id(self.parent)

F32R = mybir.dt.float32r
class _EngProxy:
    def __init__(self, eng, kb): self._e = eng; self._kb = kb
    def __getattr__(self, n):
        f = getattr(self._e, n)
        if not callable(f) or n in ("wait_ge", "dma_start"): return f
        kb = self._kb
        def w(*a, **kw):
            o = kw.get("out")
            if o is not None and o.dtype == F32 and o.name in kb.rset: kw["out"] = o.bitcast(F32R)
            return f(*a, **kw)
        return w
class _PEProxy:
    def __init__(self, eng, kb): self._e = eng; self._kb = kb
    def matmul(self, out, lhsT, rhs, **kw):
        rs = self._kb.rset
        if lhsT.dtype == F32 and rhs.dtype == F32 and lhsT.name in rs and rhs.name in rs:
            lhsT = lhsT.bitcast(F32R); rhs = rhs.bitcast(F32R)
        return self._e.matmul(out, lhsT, rhs, **kw)
    def __getattr__(self, n): return getattr(self._e, n)

class KB:
    SAME_ENGINE_SYNC = ('dve', 'act', 'pool')
    def view(self, parent, ap, name=None):
        return View(parent, ap)
    def __init__(self):
        self.nc = bass.Bass("TRN2", target_bir_lowering=False)
        self.es = ExitStack()
        nc = self.nc
        self.rset = set()
        self.eng = {'pe': _PEProxy(nc.tensor, self), 'dve': _EngProxy(nc.vector, self), 'act': _EngProxy(nc.scalar, self), 'pool': _EngProxy(nc.gpsimd, self), 'sp': nc.sync}
        self.sem = {k: self.es.enter_context(nc.semaphore("s_" + k)) for k in self.eng}
        self.cnt = {k: 0 for k in self.eng}
        self.waited = {}
        self.bufs = []
        self.n_inst = 0
        self.dma_sems = []
    def dram(self, name, shape, dt, kind):
        return self.nc.dram_tensor(name, list(shape), dt, kind=kind).ap()
    def sb(self, name, shape, dt=F32):
        b = Buf(self.es.enter_context(self.nc.sbuf_tensor(name, list(shape), dt)), name); self.bufs.append(b); return b
    def ps(self, name, shape, dt=F32):
        b = Buf(self.es.enter_context(self.nc.psum_tensor(name, list(shape), dt)), name); b.excl = True; self.bufs.append(b); return b
    def mark_r(self, *bufs):
        for b in bufs: self.rset.add(b.t.name if hasattr(b.t, 'name') else b.name)
    def group(self, name):
        g_ = Group(name, self.es.enter_context(self.nc.semaphore("g_" + name))); self.dma_sems.append(g_); return g_
    def _wait(self, e, key, semh, count):
        if isinstance(semh, Group):
            count = semh.dcount; semh = semh.sem
        if self.waited.get((e, key), 0) >= count: return
        self.eng[e].wait_ge(semh, count); self.waited[(e, key)] = count
    def _deps(self, e, reads, writes):
        for b in reads:
            for key, (semh, c) in b.lw.items():
                if key == e and e not in self.SAME_ENGINE_SYNC: continue
                self._wait(e, key, semh, c)
            if b.excl:
                for key, (semh, c) in b.rd.items():
                    if key != e: self._wait(e, key, semh, c)
        for b in writes:
            for d in (b.lw, b.rd):
                for key, (semh, c) in d.items():
                    if key == e and e not in self.SAME_ENGINE_SYNC: continue
                    self._wait(e, key, semh, c)
    def op(self, e, fn, reads=(), writes=()):
        self._deps(e, reads, writes)
        ins = fn(self.eng[e])
        self.cnt[e] += 1; c = self.cnt[e]
        ins.then_inc(self.sem[e], 1)
        for b in writes:
            b.lw = {e: (self.sem[e], c)}; b.rd = {}
        for b in reads:
            if b not in writes: b.rd[e] = (self.sem[e], c)
        self.n_inst += 1
        return ins
    def dma(self, q, out, in_, rbuf=None, wbuf=None, grp=None, **kw):
        b = wbuf if wbuf is not None else rbuf
        if grp is None:
            if b.sem is None:
                b.sem = self.es.enter_context(self.nc.semaphore("d_" + b.name)); self.dma_sems.append(b)
            holder = b
        else:
            holder = grp
        self._deps(q, [rbuf] if rbuf is not None else [], [wbuf] if wbuf is not None else [])
        ins = self.eng[q].dma_start(out=out, in_=in_, **kw)
        holder.dcount += 16
        ins.then_inc(holder.sem, 16)
        key = 'dma_' + holder.name
        ent = (grp, None) if grp is not None else (b.sem, b.dcount)
        if wbuf is not None:
            wbuf.lw = {key: ent}; wbuf.rd = {}
        else:
            rbuf.rd[key] = ent
        self.n_inst += 1
        return ins
    def finish(self, e='sp'):
        for b in self.dma_sems:
            self._wait(e, 'dma_' + b.name, b.sem, b.dcount)
        self.es.close()
        return self.nc

D = 2048; KC = 16
def build_A(ntok, ncols, TB=512):
    k = KB()
    xT = k.dram("xT", [D, ntok], F32, "ExternalInput")
    w = k.dram("w", [D, ncols], F32, "ExternalInput")
    g = k.dram("g", [128, KC], F32, "ExternalInput")
    zT = k.dram("zT", [ncols, ntok], F32, "ExternalOutput")
    ntb = ntok // TB
    xk = [k.sb(f"xk{i}", [128, KC, TB]) for i in range(2)]
    u = k.sb("u", [128, KC, ntok], BF16)
    sq = [k.sb(f"sq{i}", [128, TB], BF16) for i in range(2)]
    ones = k.sb("ones", [128, 128], BF16)
    gt = k.sb("gt", [128, KC])
    rstd = k.sb("rstd", [128, TB])
    psn = k.ps("psn", [128, TB])
    pz = [k.ps(f"pz{i}", [128, TB]) for i in range(4)]
    zo = [k.sb(f"zo{i}", [128, TB]) for i in range(3)]
    wb = [k.sb(f"wb{i}", [128, KC, 128], BF16) for i in range(2)]
    k.op('pool', lambda e: e.memset(ones[:], 1.0), [], [ones])
    k.dma('sp', gt[:], g[:, :], wbuf=gt)
    xv = xT.rearrange("(kc p) t -> p kc t", p=128)
    for tb in range(ntb):
        X = xk[tb % 2]
        k.dma('sp', X[:], xv[:, :, tb * TB:(tb + 1) * TB], wbuf=X)
        for kc in range(KC):
            S = sq[kc % 2]
            k.op('act', lambda e: e.activation(out=S[:], in_=X[:, kc, :], func=AF.Square), [X], [S])
            k.op('pe', lambda e: e.matmul(psn[:], ones[:], S[:], start=(kc == 0), stop=(kc == KC - 1)), [ones, S], [psn])
        k.op('act', lambda e: e.activation(out=rstd[:], in_=psn[:], func=AF.Sqrt, scale=1.0 / D, bias=1e-6), [psn], [rstd])
        k.op('dve', lambda e: e.reciprocal(out=rstd[:], in_=rstd[:]), [rstd], [rstd])
        for kc in range(KC):
            k.op('dve', lambda e: e.scalar_tensor_tensor(out=u[:, kc, tb * TB:(tb + 1) * TB], in0=X[:, kc, :], scalar=gt[:, kc:kc + 1],
                                                        in1=rstd[:], op0=ALU.mult, op1=ALU.mult), [X, gt, rstd], [u])
    wv = w.rearrange("(kc p) c -> p kc c", p=128)
    ncc = (ncols + 127) // 128
    i = 0
    for cc in range(ncc):
        cw = min(128, ncols - cc * 128)
        W = wb[cc % 2]
        k.dma('pool', W[:, :, :cw], wv[:, :, cc * 128:cc * 128 + cw], wbuf=W)
        for tb in range(ntb):
            P = pz[i % 4]; Z = zo[i % 3]
            for kc in range(KC):
                k.op('pe', lambda e: e.matmul(P[:cw, :], W[:, kc, :cw], u[:, kc, tb * TB:(tb + 1) * TB], start=(kc == 0), stop=(kc == KC - 1)), [W, u], [P])
            if i % 2 == 0:
                k.op('act', lambda e: e.copy(out=Z[:cw, :], in_=P[:cw, :]), [P], [Z])
            else:
                k.op('dve', lambda e: e.tensor_copy(out=Z[:cw, :], in_=P[:cw, :]), [P], [Z])
            k.dma('sp', zT[cc * 128:cc * 128 + cw, tb * TB:(tb + 1) * TB], Z[:cw, :], rbuf=Z)
            i += 1
    print("A n_inst", k.n_inst)
    return k.finish()

D = 2048; KC = 16; FF = 8192; FC = 64
def build_C(ntok, last, TB=512):
    k = KB()
    xT = k.dram("xT", [D, ntok], F32, "ExternalInput")
    zgT = k.dram("zgT", [3 * D, ntok], F32, "ExternalInput")
    gb = k.dram("gb", [128, 48], F32, "ExternalInput")
    yT = k.dram("yT", [D, ntok], F32, "ExternalInput")
    w_up = k.dram("w_up", [D, D], F32, "ExternalInput")
    w_out = k.dram("w_out", [D, D], F32, "ExternalInput")
    nm = k.dram("nm", [128, KC], F32, "ExternalInput")
    w1 = k.dram("w1", [D, FF], F32, "ExternalInput")
    w2 = k.dram("w2", [FF, D], F32, "ExternalInput")
    fn = k.dram("fn", [128, KC], F32, "ExternalInput")
    gluw = k.dram("gluw", [768, 768], F32, "ExternalInput")
    glub = k.dram("glub", [128, 6], F32, "ExternalInput")
    xoT = k.dram("xoT", [D, ntok], F32, "ExternalOutput")
    ntb = ntok // TB
    X = k.sb("X", [128, KC, TB])
    YH = k.sb("YH", [128, KC, TB], BF16)
    M = k.sb("M", [128, KC, TB], BF16)
    ACTB = k.sb("ACTB", [128, FC, TB], BF16)
    NWB = 4
    wb = [k.sb(f"wb{i}", [128, KC, 128], BF16) for i in range(NWB)]
    stg = [k.sb(f"stg{i}", [128, KC, 128]) for i in range(2)]
    G = [[k.sb(f"G{i}_{j}", [128, TB]) for j in range(3)] for i in range(2)]
    sq = [k.sb(f"sq{i}", [128, TB], BF16) for i in range(2)]
    tmp = [k.sb(f"tmp{i}", [128, TB]) for i in range(2)]
    mm = k.sb("mm", [128, TB])
    ones = k.sb("ones", [128, 128], BF16)
    gbt = k.sb("gbt", [128, 48]); nmt = k.sb("nmt", [128, KC]); fnt = k.sb("fnt", [128, KC]); glubt = k.sb("glubt", [128, 6])
    k.dma('sp', glubt[:], glub[:, :], wbuf=glubt)
    AF32 = ACTB[:].rearrange("p a b -> p (a b)").bitcast(F32)
    S5N = 6 * TB
    YA = k.view(ACTB, AF32[:, 0:S5N]); YG = k.view(ACTB, AF32[:, S5N:2 * S5N]); SQ = k.view(ACTB, AF32[:, 2 * S5N:3 * S5N])
    YGb = k.view(ACTB, ACTB[:].rearrange("p a b -> p (a b)")[:, 6 * S5N:7 * S5N])
    gluv = gluw.rearrange("(kc p) c -> p kc c", p=128)
    C1 = 0.7978845608028654; C2 = C1 * 0.044715
    rstd = k.sb("rstd", [128, TB])
    psn = k.ps("psn", [128, TB])
    pz = [k.ps(f"pz{i}", [128, TB]) for i in range(6)]
    k.op('pool', lambda e: e.memset(ones[:], 1.0), [], [ones])
    k.dma('sp', gbt[:], gb[:, :], wbuf=gbt); k.dma('sp', nmt[:], nm[:, :], wbuf=nmt); k.dma('sp', fnt[:], fn[:, :], wbuf=fnt)
    xv = xT.rearrange("(kc p) t -> p kc t", p=128); yv = yT.rearrange("(kc p) t -> p kc t", p=128)
    xov = xoT.rearrange("(kc p) t -> p kc t", p=128)
    wupv = w_up.rearrange("(kc p) c -> p kc c", p=128); woutv = w_out.rearrange("(kc p) c -> p kc c", p=128)
    w1v = w1.rearrange("(kc p) c -> p kc c", p=128); w2v = w2.rearrange("(fc p) c -> p fc c", p=128)
    wlist = []
    for tb_ in range(ntb):
        for fo in range(6): wlist.append((gluv[:, :, fo * 128:(fo + 1) * 128], 6))
        for fo in range(KC): wlist.append((wupv[:, :, fo * 128:(fo + 1) * 128], KC))
        for fo in range(KC): wlist.append((woutv[:, :, fo * 128:(fo + 1) * 128], KC))
        for fc in range(FC): wlist.append((w1v[:, :, fc * 128:(fc + 1) * 128], KC))
        for fo in range(KC):
            for q in range(4): wlist.append((w2v[:, q * 16:(q + 1) * 16, fo * 128:(fo + 1) * 128], KC))
    wst = {"dma": 0, "cast": 0, "taken": 0}
    def w_next():
        i = wst["taken"]
        while wst["dma"] < min(len(wlist), i + 4):
            j = wst["dma"]; src, nk = wlist[j]
            if j % 2 == 0:
                k.dma('pool', wb[j % NWB][:, 0:nk, :], src, wbuf=wb[j % NWB])
            else:
                S_ = stg[(j // 2) % 2]
                k.dma('sp', S_[:, 0:nk, :], src, wbuf=S_)
            wst["dma"] += 1
        while wst["cast"] < min(wst["dma"], i + 2):
            j = wst["cast"]; src, nk = wlist[j]
            if j % 2 == 1:
                S_ = stg[(j // 2) % 2]; W_ = wb[j % NWB]
                k.op('act', lambda e: e.copy(out=W_[:, 0:nk, :], in_=S_[:, 0:nk, :]), [S_], [W_])
            wst["cast"] += 1
        wst["taken"] += 1
        return wb[i % NWB]
    BR = [(0, 6), (6, 11), (11, 16)]
    wi = 0; pi = 0
    def rmsn(gt_tile, dst_fn):
        for kc in range(KC):
            S = sq[kc % 2]
            k.op('act', lambda e: e.activation(out=S[:], in_=X[:, kc, :], func=AF.Square), [X], [S])
            k.op('pe', lambda e: e.matmul(psn[:], ones[:], S[:], start=(kc == 0), stop=(kc == KC - 1)), [ones, S], [psn])
        k.op('act', lambda e: e.activation(out=rstd[:], in_=psn[:], func=AF.Sqrt, scale=1.0 / D, bias=1e-6), [psn], [rstd])
        k.op('dve', lambda e: e.reciprocal(out=rstd[:], in_=rstd[:]), [rstd], [rstd])
    for tb in range(ntb):
        ts = slice(tb * TB, (tb + 1) * TB)
        k.dma('sp', X[:], xv[:, :, ts], wbuf=X)
        k.dma('pool', YH[:, 6:16, :], yv[:, 6:16, ts], wbuf=YH)
        k.dma('sp', YA[:].rearrange("p (a b) -> p a b", a=6), yv[:, 0:6, ts], wbuf=YA)
        k.op('pool', lambda e: e.tensor_tensor(out=SQ[:], in0=YA[:], in1=YA[:], op=ALU.mult), [YA], [SQ])
        k.op('dve', lambda e: e.tensor_scalar(out=SQ[:], in0=SQ[:], scalar1=C2, scalar2=C1, op0=ALU.mult, op1=ALU.add), [SQ], [SQ])
        k.op('dve', lambda e: e.tensor_tensor(out=SQ[:], in0=SQ[:], in1=YA[:], op=ALU.mult), [SQ, YA], [SQ])
        k.op('act', lambda e: e.activation(out=SQ[:], in_=SQ[:], func=AF.Tanh), [SQ], [SQ])
        k.op('dve', lambda e: e.tensor_scalar(out=SQ[:], in0=SQ[:], scalar1=0.5, scalar2=0.5, op0=ALU.mult, op1=ALU.add), [SQ], [SQ])
        k.op('dve', lambda e: e.tensor_tensor(out=YG[:], in0=SQ[:], in1=YA[:], op=ALU.mult), [SQ, YA], [YG])
        k.op('pool', lambda e: e.tensor_copy(out=YGb[:], in_=YG[:]), [YG], [YGb])
        for fo in range(6):
            W = w_next()
            P = pz[pi % 6]; pi += 1
            for kc in range(6):
                k.op('pe', lambda e: e.matmul(P[:], W[:, kc, :], YGb[:, kc * TB:(kc + 1) * TB], start=(kc == 0), stop=(kc == 5)), [W, YGb], [P])
            T = tmp[fo % 2]
            k.op('act', lambda e: e.activation(out=T[:], in_=P[:], func=AF.Sigmoid, bias=glubt[:, fo:fo + 1]), [P, glubt], [T])
            k.op('dve', lambda e: e.tensor_tensor(out=YH[:, fo, :], in0=YG[:, fo * TB:(fo + 1) * TB], in1=T[:], op=ALU.mult), [YG, T], [YH])
        for fo in range(KC):
            W = w_next()
            Gs = G[fo % 2]
            for br in range(3):
                r0 = br * D + fo * 128
                k.dma('sp', Gs[br][:], zgT[r0:r0 + 128, ts], wbuf=Gs[br])
                k.op('act', lambda e: e.activation(out=Gs[br][:], in_=Gs[br][:], func=AF.Sigmoid, bias=gbt[:, br * 16 + fo:br * 16 + fo + 1]), [Gs[br], gbt], [Gs[br]])
            Ps = []
            for br in range(3):
                P = pz[pi % 6]; pi += 1; Ps.append(P)
                a, b = BR[br]
                for kc in range(a, b):
                    k.op('pe', lambda e: e.matmul(P[:], W[:, kc, :], YH[:, kc, :], start=(kc == a), stop=(kc == b - 1)), [W, YH], [P])
            k.op('dve', lambda e: e.tensor_tensor(out=mm[:], in0=Ps[0][:], in1=Gs[0][:], op=ALU.mult), [Ps[0], Gs[0]], [mm])
            T = tmp[0]
            k.op('dve', lambda e: e.tensor_tensor(out=T[:], in0=Ps[1][:], in1=Gs[1][:], op=ALU.mult), [Ps[1], Gs[1]], [T])
            k.op('dve', lambda e: e.tensor_tensor(out=mm[:], in0=mm[:], in1=T[:], op=ALU.add), [mm, T], [mm])
            T = tmp[1]
            k.op('dve', lambda e: e.tensor_tensor(out=T[:], in0=Ps[2][:], in1=Gs[2][:], op=ALU.mult), [Ps[2], Gs[2]], [T])
            k.op('dve', lambda e: e.tensor_tensor(out=M[:, fo, :], in0=mm[:], in1=T[:], op=ALU.add), [mm, T], [M])
        for fo in range(KC):
            W = w_next()
            P = pz[pi % 6]; pi += 1
            for kc in range(KC):
                k.op('pe', lambda e: e.matmul(P[:], W[:, kc, :], M[:, kc, :], start=(kc == 0), stop=(kc == KC - 1)), [W, M], [P])
            k.op('dve', lambda e: e.tensor_tensor(out=X[:, fo, :], in0=X[:, fo, :], in1=P[:], op=ALU.add), [X, P], [X])
        rmsn(nmt, None)
        for kc in range(KC):
            k.op('dve', lambda e: e.scalar_tensor_tensor(out=YH[:, kc, :], in0=X[:, kc, :], scalar=nmt[:, kc:kc + 1], in1=rstd[:], op0=ALU.mult, op1=ALU.mult), [X, nmt, rstd], [YH])
        for fc in range(FC):
            W = w_next()
            P = pz[pi % 6]; pi += 1
            for kc in range(KC):
                k.op('pe', lambda e: e.matmul(P[:], W[:, kc, :], YH[:, kc, :], start=(kc == 0), stop=(kc == KC - 1)), [W, YH], [P])
            T = tmp[fc % 2]
            k.op('act', lambda e: e.activation(out=T[:], in_=P[:], func=AF.Relu), [P], [T])
            k.op('pool', lambda e: e.tensor_tensor(out=ACTB[:, fc, :], in0=T[:], in1=T[:], op=ALU.mult), [T], [ACTB])
        for fo in range(KC):
            P = pz[pi % 6]; pi += 1
            for h in range(4):
                W = w_next()
                for f in range(16):
                    fc = h * 16 + f
                    k.op('pe', lambda e: e.matmul(P[:], W[:, f, :], ACTB[:, fc, :], start=(fc == 0), stop=(fc == FC - 1)), [W, ACTB], [P])
            k.op('dve', lambda e: e.tensor_tensor(out=X[:, fo, :], in0=X[:, fo, :], in1=P[:], op=ALU.add), [X, P], [X])
        if last:
            rmsn(fnt, None)
            for kc in range(KC):
                k.op('dve', lambda e: e.scalar_tensor_tensor(out=X[:, kc, :], in0=X[:, kc, :], scalar=fnt[:, kc:kc + 1], in1=rstd[:], op0=ALU.mult, op1=ALU.mult), [X, fnt, rstd], [X])
        k.dma('sp', xov[:, :, ts], X[:], rbuf=X)
    print("C n_inst", k.n_inst)
    return k.finish()

PI = math.pi
PI = math.pi
def emit_S5(k, ntok, banks, TB=512):
    NP = 3
    zs = k.dram("s_zs", [96, ntok], F32, "ExternalInput")
    lamre = k.dram("s_lamre", [128, NP], F32, "ExternalInput")
    lamim = k.dram("s_lamim", [128, NP], F32, "ExternalInput")
    lstep = k.dram("s_lstep", [128, NP], F32, "ExternalInput")
    bre = k.dram("s_bre", [96, NP, 128], F32, "ExternalInput")
    bim = k.dram("s_bim", [96, NP, 128], F32, "ExternalInput")
    cre = k.dram("s_cre", [128, NP, 96], F32, "ExternalInput")
    cim = k.dram("s_cim", [128, NP, 96], F32, "ExternalInput")
    dvec = k.dram("s_dvec", [96, 1], F32, "ExternalInput")
    tau = k.dram("s_tau", [128, 128], F32, "ExternalInput")
    yo = k.dram("s_yo", [96, ntok], F32, "ExternalOutput")
    nb = ntok // TB; NCH = TB // 128
    sb = lambda n, shp, dt=F32: k.sb("s_" + n, shp, dt)
    lre = sb("lre", [128, NP]); lim = sb("lim", [128, NP]); lst = sb("lst", [128, NP])
    breT = sb("breT", [96, NP, 128]); bimT = sb("bimT", [96, NP, 128])
    creT = sb("creT", [128, NP, 96]); cimT = sb("cimT", [128, NP, 96]); ncreT = sb("ncreT", [128, NP, 96]); ncimT = sb("ncimT", [128, NP, 96])
    dT = sb("dT", [96, 1]); tauT = sb("tauT", [128, 128])
    gcon = k.group("s_con"); gz = [k.group(f"s_z{i}") for i in range(2)]; gy = [k.group(f"s_y{i}") for i in range(2)]
    for t, d in ((lre, lamre), (lim, lamim), (lst, lstep), (breT, bre), (bimT, bim), (creT, cre), (cimT, cim), (dT, dvec), (tauT, tau)):
        k.dma('sp', t[:], d, wbuf=t, grp=gcon)
    k.op('dve', lambda e: e.tensor_scalar(out=ncreT[:], in0=creT[:], scalar1=-1.0, scalar2=None, op0=ALU.mult), [creT], [ncreT])
    k.op('dve', lambda e: e.tensor_scalar(out=ncimT[:], in0=cimT[:], scalar1=-1.0, scalar2=None, op0=ALU.mult), [cimT], [ncimT])
    col = lambda n: sb(n, [128, NP])
    dt = col("dt"); lr = col("lr"); al = col("al"); om = col("om"); ea = col("ea"); cw = col("cw"); sw = col("sw")
    lbr = col("lbr"); lbi = col("lbi"); den = col("den"); t1 = col("t1"); t2 = col("t2"); qre = col("qre"); qim = col("qim")
    Rre = col("Rre"); Rim = col("Rim"); ang = col("ang")
    V = lambda fn, r, w: k.op('dve', fn, r, w)
    A_ = lambda fn, r, w: k.op('act', fn, r, w)
    A_(lambda e: e.activation(out=dt[:], in_=lst[:], func=AF.Exp), [lst], [dt])
    V(lambda e: e.tensor_scalar(out=lr[:], in0=lre[:], scalar1=-1e-4, scalar2=None, op0=ALU.min), [lre], [lr])
    V(lambda e: e.tensor_tensor(out=al[:], in0=lr[:], in1=dt[:], op=ALU.mult), [lr, dt], [al])
    V(lambda e: e.tensor_tensor(out=om[:], in0=lim[:], in1=dt[:], op=ALU.mult), [lim, dt], [om])
    A_(lambda e: e.activation(out=ea[:], in_=al[:], func=AF.Exp), [al], [ea])
    def sincos(dst_s, dst_c, src_ap, srcbufs, shape_ap_fn, scale):
        pass
    angi = sb("angi", [128, NP], mybir.dt.int32); angf = sb("angf", [128, NP])
    def sin_of(dst, src, tmpb, mult, phase):
        (db, da), (sbf, sa), (tb_, ta) = dst, src, tmpb
        V(lambda e: e.tensor_scalar(out=ta, in0=sa, scalar1=mult / (2 * PI), scalar2=phase / (2 * PI), op0=ALU.mult, op1=ALU.add), [sbf], [tb_])
        V(lambda e: e.tensor_copy(out=angi[:], in_=ta), [tb_], [angi])
        V(lambda e: e.tensor_copy(out=angf[:], in_=angi[:]), [angi], [angf])
        V(lambda e: e.tensor_tensor(out=ta, in0=ta, in1=angf[:], op=ALU.subtract), [tb_, angf], [tb_])
        A_(lambda e: e.activation(out=da, in_=ta, func=AF.Sin, scale=2 * PI), [tb_], [db])
    sin_of((sw, sw[:]), (om, om[:]), (ang, ang[:]), 1.0, 0.0)
    sin_of((cw, cw[:]), (om, om[:]), (ang, ang[:]), 1.0, PI / 2)
    sin_of((Rim, Rim[:]), (om, om[:]), (ang, ang[:]), 128.0, 0.0)
    sin_of((Rre, Rre[:]), (om, om[:]), (ang, ang[:]), 128.0, PI / 2)
    V(lambda e: e.tensor_tensor(out=lbr[:], in0=ea[:], in1=cw[:], op=ALU.mult), [ea, cw], [lbr])
    V(lambda e: e.tensor_tensor(out=lbi[:], in0=ea[:], in1=sw[:], op=ALU.mult), [ea, sw], [lbi])
    V(lambda e: e.tensor_scalar(out=lbr[:], in0=lbr[:], scalar1=-1.0, scalar2=None, op0=ALU.add), [lbr], [lbr])
    V(lambda e: e.tensor_tensor(out=den[:], in0=lr[:], in1=lr[:], op=ALU.mult), [lr], [den])
    V(lambda e: e.tensor_tensor(out=t1[:], in0=lim[:], in1=lim[:], op=ALU.mult), [lim], [t1])
    V(lambda e: e.tensor_tensor(out=den[:], in0=den[:], in1=t1[:], op=ALU.add), [den, t1], [den])
    V(lambda e: e.reciprocal(out=den[:], in_=den[:]), [den], [den])
    V(lambda e: e.tensor_tensor(out=t1[:], in0=lbr[:], in1=lr[:], op=ALU.mult), [lbr, lr], [t1])
    V(lambda e: e.tensor_tensor(out=t2[:], in0=lbi[:], in1=lim[:], op=ALU.mult), [lbi, lim], [t2])
    V(lambda e: e.tensor_tensor(out=t1[:], in0=t1[:], in1=t2[:], op=ALU.add), [t1, t2], [t1])
    V(lambda e: e.tensor_tensor(out=qre[:], in0=t1[:], in1=den[:], op=ALU.mult), [t1, den], [qre])
    V(lambda e: e.tensor_tensor(out=t1[:], in0=lbi[:], in1=lr[:], op=ALU.mult), [lbi, lr], [t1])
    V(lambda e: e.tensor_tensor(out=t2[:], in0=lbr[:], in1=lim[:], op=ALU.mult), [lbr, lim], [t2])
    V(lambda e: e.tensor_tensor(out=t1[:], in0=t1[:], in1=t2[:], op=ALU.subtract), [t1, t2], [t1])
    V(lambda e: e.tensor_tensor(out=qim[:], in0=t1[:], in1=den[:], op=ALU.mult), [t1, den], [qim])
    COS = [sb(f"COS{p}", [128, 128]) for p in range(NP)]; SIN = [sb(f"SIN{p}", [128, 128]) for p in range(NP)]
    ERE = [sb(f"ERE{p}", [128, 128]) for p in range(NP)]; EIM = [sb(f"EIM{p}", [128, 128]) for p in range(NP)]
    DEC = [sb(f"DEC{p}", [128, 128]) for p in range(NP)]
    th = sb("th", [128, 128]); thi = sb("thi", [128, 128], mybir.dt.int32); thf = sb("thf", [128, 128])
    for p in range(NP):
        for dstT, ph in ((SIN[p], 0.0), (COS[p], PI / 2)):
            V(lambda e: e.tensor_scalar(out=th[:], in0=tauT[:], scalar1=om[:, p:p + 1], scalar2=1.0 / (2 * PI), op0=ALU.mult, op1=ALU.mult), [tauT, om], [th])
            V(lambda e: e.tensor_scalar(out=th[:], in0=th[:], scalar1=ph / (2 * PI), scalar2=None, op0=ALU.add), [th], [th])
            V(lambda e: e.tensor_copy(out=thi[:], in_=th[:]), [th], [thi])
            V(lambda e: e.tensor_copy(out=thf[:], in_=thi[:]), [thi], [thf])
            V(lambda e: e.tensor_tensor(out=th[:], in0=th[:], in1=thf[:], op=ALU.subtract), [th, thf], [th])
            A_(lambda e: e.activation(out=dstT[:], in_=th[:], func=AF.Sin, scale=2 * PI), [th], [dstT])
        V(lambda e: e.tensor_scalar(out=th[:], in0=SIN[p][:], scalar1=qim[:, p:p + 1], scalar2=None, op0=ALU.mult), [SIN[p], qim], [th])
        V(lambda e: e.scalar_tensor_tensor(out=ERE[p][:], in0=COS[p][:], scalar=qre[:, p:p + 1], in1=th[:], op0=ALU.mult, op1=ALU.add), [COS[p], qre, th], [ERE[p]])
        V(lambda e: e.tensor_scalar(out=th[:], in0=SIN[p][:], scalar1=qre[:, p:p + 1], scalar2=None, op0=ALU.mult), [SIN[p], qre], [th])
        V(lambda e: e.scalar_tensor_tensor(out=EIM[p][:], in0=COS[p][:], scalar=qim[:, p:p + 1], in1=th[:], op0=ALU.mult, op1=ALU.subtract), [COS[p], qim, th], [EIM[p]])
        V(lambda e: e.tensor_scalar(out=DEC[p][:], in0=tauT[:], scalar1=0.0, scalar2=ea[:, p:p + 1], op0=ALU.mult, op1=ALU.add), [tauT, ea], [DEC[p]])
    zb = [sb(f"zb{i}", [96, TB]) for i in range(2)]
    PX = [[banks[0], banks[1]], [banks[0], banks[1]]]
    PY = [k.view(banks[2], banks[2][0:96, :]), k.view(banks[3], banks[3][0:96, :])]
    rr = [[sb(f"rr{p}_{c}", [128, TB]) for c in range(2)] for p in range(NP)]
    tm = [sb(f"tm{i}", [128, TB]) for i in range(2)]
    PP = [[sb(f"PP{i}_{j}", [128, TB]) for j in range(4)] for i in range(2)]
    ini = [[sb(f"ini{p}_{c}", [128, 1]) for c in range(2)] for p in range(NP)]
    it = [sb(f"it{p}", [128, 1]) for p in range(NP)]
    yout = [sb(f"yout{i}", [96, TB]) for i in range(2)]
    for p in range(NP):
        for c in range(2):
            k.op('pool', lambda e: e.memset(ini[p][c][:], 0.0), [], [ini[p][c]]); yield
    bc = lambda T: T[:].unsqueeze(1).to_broadcast([128, NCH, 128])
    v3 = lambda ap: ap.rearrange("p (c t) -> p c t", t=128)
    k.dma('sp', zb[0][:], zs[:, 0:TB], wbuf=zb[0], grp=gz[0]); yield
    for b in range(nb):
        Z = zb[b % 2]; ts = slice(b * TB, (b + 1) * TB)
        if b + 1 < nb:
            k.dma('sp', zb[(b + 1) % 2][:], zs[:, (b + 1) * TB:(b + 2) * TB], wbuf=zb[(b + 1) % 2], grp=gz[(b + 1) % 2]); yield
        PYb = PY[b % 2]
        for p in range(NP):
            Xr, Xi = PX[p % 2]
            k.op('pe', lambda e: e.matmul(Xr[:], breT[:, p, :], Z[:], start=True, stop=True), [breT, Z], [Xr]); yield
            k.op('pe', lambda e: e.matmul(Xi[:], bimT[:, p, :], Z[:], start=True, stop=True), [bimT, Z], [Xi]); yield
            Rr, Ri = rr[p]
            T0, T1 = tm
            V(lambda e: e.tensor_tensor(out=v3(T0[:]), in0=v3(Xr[:]), in1=bc(ERE[p]), op=ALU.mult), [Xr, ERE[p]], [T0]); yield
            V(lambda e: e.tensor_tensor(out=v3(T1[:]), in0=v3(Xi[:]), in1=bc(EIM[p]), op=ALU.mult), [Xi, EIM[p]], [T1]); yield
            k.op('pool', lambda e: e.tensor_tensor(out=Rr[:], in0=T0[:], in1=T1[:], op=ALU.subtract), [T0, T1], [Rr]); yield
            V(lambda e: e.tensor_tensor(out=v3(T0[:]), in0=v3(Xi[:]), in1=bc(ERE[p]), op=ALU.mult), [Xi, ERE[p]], [T0]); yield
            V(lambda e: e.tensor_tensor(out=v3(T1[:]), in0=v3(Xr[:]), in1=bc(EIM[p]), op=ALU.mult), [Xr, EIM[p]], [T1]); yield
            k.op('pool', lambda e: e.tensor_tensor(out=Ri[:], in0=T0[:], in1=T1[:], op=ALU.add), [T0, T1], [Ri]); yield
            for c in range(NCH):
                cs = slice(c * 128, (c + 1) * 128)
                Ir, Ii = ini[p]
                V(lambda e: e.tensor_tensor_scan(out=Rr[:, cs], data0=DEC[p][:], data1=Rr[:, cs], initial=Ir[:], op0=ALU.mult, op1=ALU.add), [DEC[p], Rr, Ir], [Rr]); yield
                V(lambda e: e.tensor_tensor_scan(out=Ri[:, cs], data0=DEC[p][:], data1=Ri[:, cs], initial=Ii[:], op0=ALU.mult, op1=ALU.add), [DEC[p], Ri, Ii], [Ri]); yield
                lastr = Rr[:, c * 128 + 127:c * 128 + 128]; lasti = Ri[:, c * 128 + 127:c * 128 + 128]
                V(lambda e: e.tensor_tensor(out=it[p][:], in0=lasti, in1=Rim[:, p:p + 1], op=ALU.mult), [Ri, Rim], [it[p]]); yield
                V(lambda e: e.scalar_tensor_tensor(out=Ir[:], in0=lastr, scalar=Rre[:, p:p + 1], in1=it[p][:], op0=ALU.mult, op1=ALU.subtract), [Rr, Rre, it[p]], [Ir]); yield
                V(lambda e: e.tensor_tensor(out=it[p][:], in0=lastr, in1=Rim[:, p:p + 1], op=ALU.mult), [Rr, Rim], [it[p]]); yield
                V(lambda e: e.scalar_tensor_tensor(out=Ii[:], in0=lasti, scalar=Rre[:, p:p + 1], in1=it[p][:], op0=ALU.mult, op1=ALU.add), [Ri, Rre, it[p]], [Ii]); yield
            P1, P2, P3, P4 = PP[p % 2]
            k.op('pool', lambda e: e.tensor_tensor(out=v3(P1[:]), in0=v3(Rr[:]), in1=bc(COS[p]), op=ALU.mult), [Rr, COS[p]], [P1]); yield
            k.op('pool', lambda e: e.tensor_tensor(out=v3(P2[:]), in0=v3(Ri[:]), in1=bc(SIN[p]), op=ALU.mult), [Ri, SIN[p]], [P2]); yield
            k.op('pool', lambda e: e.tensor_tensor(out=v3(P3[:]), in0=v3(Rr[:]), in1=bc(SIN[p]), op=ALU.mult), [Rr, SIN[p]], [P3]); yield
            k.op('pool', lambda e: e.tensor_tensor(out=v3(P4[:]), in0=v3(Ri[:]), in1=bc(COS[p]), op=ALU.mult), [Ri, COS[p]], [P4]); yield
            k.op('pe', lambda e: e.matmul(PYb[:], creT[:, p, :], P1[:], start=(p == 0), stop=False), [creT, P1], [PYb]); yield
            k.op('pe', lambda e: e.matmul(PYb[:], ncreT[:, p, :], P2[:], start=False, stop=False), [ncreT, P2], [PYb]); yield
            k.op('pe', lambda e: e.matmul(PYb[:], ncimT[:, p, :], P3[:], start=False, stop=False), [ncimT, P3], [PYb]); yield
            k.op('pe', lambda e: e.matmul(PYb[:], ncimT[:, p, :], P4[:], start=False, stop=(p == NP - 1)), [ncimT, P4], [PYb]); yield
        YO = yout[b % 2]
        V(lambda e: e.scalar_tensor_tensor(out=YO[:], in0=Z[:], scalar=dT[:, 0:1], in1=PYb[:], op0=ALU.mult, op1=ALU.add), [Z, dT, PYb], [YO]); yield
        k.dma('sp', yo[:, ts], YO[:], rbuf=YO, grp=gy[b % 2]); yield

def s5_layout(core, zT_s5, lam_re, lam_im, log_step, b_re, b_im, c_re, c_im, d):
    import numpy as np
    g0 = 6 * core
    f = np.float32
    lamre = np.zeros((128, 3), f); lamim = np.zeros((128, 3), f); lstep = np.zeros((128, 3), f)
    bre = np.zeros((96, 3, 128), f); bim = np.zeros((96, 3, 128), f); cre = np.zeros((128, 3, 96), f); cim = np.zeros((128, 3, 96), f)
    for p in range(3):
        for gl in range(2):
            g = g0 + 2 * p + gl; gi = 2 * p + gl
            lamre[gl * 64:(gl + 1) * 64, p] = lam_re[g]; lamim[gl * 64:(gl + 1) * 64, p] = lam_im[g]; lstep[gl * 64:(gl + 1) * 64, p] = log_step[g]
            bre[gi * 16:(gi + 1) * 16, p, gl * 64:(gl + 1) * 64] = b_re[g].T
            bim[gi * 16:(gi + 1) * 16, p, gl * 64:(gl + 1) * 64] = b_im[g].T
            cre[gl * 64:(gl + 1) * 64, p, gi * 16:(gi + 1) * 16] = c_re[g].T
            cim[gl * 64:(gl + 1) * 64, p, gi * 16:(gi + 1) * 16] = c_im[g].T
    return {"s_" + n_: v_ for n_, v_ in {"zs": np.ascontiguousarray(zT_s5[g0 * 16:(g0 + 6) * 16]), "lamre": lamre, "lamim": lamim, "lstep": lstep, "bre": bre, "bim": bim,
            "cre": cre, "cim": cim, "dvec": np.ascontiguousarray(d[g0 * 16:(g0 + 6) * 16].reshape(96, 1)),
            "tau": np.broadcast_to(np.arange(128, dtype=f), (128, 128)).copy()}.items()}

def gla_consts():
    import numpy as np
    s = np.arange(128)[:, None]; t = np.arange(128)[None, :]
    same = (s // 64) == (t // 64)
    f = np.float32
    return {"tri_le": (same & (s <= t)).astype(f), "tri_gt": (same & (s > t)).astype(f),
            "chind": np.stack([(np.arange(128) < 64), (np.arange(128) >= 64)], 1).astype(f)}
def emit_GLA(k, ntok, banks, pfx="g_"):
    D_ = lambda n, shp, kind="ExternalInput": k.dram(pfx + n, shp, F32, kind)
    qT = D_("qT", [64, ntok]); kT = D_("kT", [64, ntok]); ktok = D_("ktok", [ntok, 64]); vtok = D_("vtok", [ntok, 128]); gtok = D_("gtok", [ntok, 128])
    ainT = D_("ainT", [16, ntok]); alora = D_("alora", [16, 64]); abias = D_("abias", [128, 64]); normg = D_("normg", [128, 128])
    tri_le = D_("tri_le", [128, 128]); tri_gt = D_("tri_gt", [128, 128]); chind = D_("chind", [128, 2])
    otok = D_("otok", [ntok, 128], "ExternalOutput")
    sb = lambda n, shp, dt=F32: k.sb(pfx + n, shp, dt)
    aloraT = sb("aloraT", [16, 64]); abiasT = sb("abiasT", [128, 64]); normgT = sb("normgT", [128, 128])
    TLE = sb("TLE", [128, 128]); TGT = sb("TGT", [128, 128]); CHI = sb("CHI", [128, 2])
    gcon = k.group(pfx + "con"); gl = [k.group(pfx + f"l{i}") for i in range(3)]; go = [k.group(pfx + f"o{i}") for i in range(2)]
    for t, d in ((aloraT, alora), (abiasT, abias), (normgT, normg), (TLE, tri_le), (TGT, tri_gt), (CHI, chind)):
        k.dma('sp', t[:], d[:, :], wbuf=t, grp=gcon)
    NB = 3
    QT = [sb(f"QT{i}", [64, 128]) for i in range(NB)]; KT = [sb(f"KT{i}", [64, 128]) for i in range(NB)]
    KK = [sb(f"KK{i}", [128, 64]) for i in range(NB)]; VV = [sb(f"VV{i}", [128, 128]) for i in range(NB)]; GG = [sb(f"GG{i}", [128, 128]) for i in range(NB)]
    AIN = [sb(f"AIN{i}", [16, 128]) for i in range(NB)]
    xb = sb("xb", [128, 64]); ax = sb("ax", [128, 64]); la = sb("la", [128, 64]); mn = sb("mn", [128, 64])
    EP = sb("EP", [64, 128]); EN = sb("EN", [64, 128]); ER = sb("ER", [128, 64])
    QF = sb("QF", [64, 128]); KF = sb("KF", [64, 128]); KH_ = [sb(f"KH{i}", [128, 64]) for i in range(2)]
    Q0_ = [sb(f"Q0{i}", [64, 128]) for i in range(2)]; Q1_ = [sb(f"Q1{i}", [64, 128]) for i in range(2)]
    VVr_ = [sb(f"VVr{i}", [128, 128]) for i in range(2)]
    NM_ = [sb(f"NM{i}", [128, 128]) for i in range(2)]; EB_ = [sb(f"EB{i}", [64, 2]) for i in range(2)]
    S = [sb(f"S{i}", [64, 128]) for i in range(3)]
    ss = sb("ss", [128, 1]); junk = sb("junk", [128, 128]); SG = sb("SG", [128, 128]); O1 = sb("O1", [128, 128])
    OO = [sb(f"OO{i}", [128, 128]) for i in range(2)]
    bankA = [banks[0]]; bankB = [banks[1]]; bankC = [banks[2]]
    def views(i):
        A, B, C = bankA[i], bankB[i], bankC[i]
        return dict(px=k.view(A, A[:, 0:64]), pc=k.view(A, A[:, 64:128]), pr=k.view(A, A[:, 128:192]),
                    pbl=k.view(A, A[0:64, 192:194]), pcT=k.view(A, A[0:64, 256:384]), pN=k.view(A, A[:, 384:512]),
                    pO=k.view(B, B[:, 128:256]), pS0=k.view(C, C[0:64, 0:128]), pS1=k.view(C, C[0:64, 128:256]))
    PV = [views(0), views(0)]
    V = lambda fn, r, w: k.op('dve', fn, r, w)
    A_ = lambda fn, r, w: k.op('act', fn, r, w)
    P_ = lambda fn, r, w: k.op('pool', fn, r, w)
    T_ = lambda fn, r, w: k.op('pe', fn, r, w)
    P_(lambda e: e.memset(S[0][:], 0.0), [], [S[0]])
    for t_ in (*Q0_, *Q1_):
        P_(lambda e: e.memset(t_[:], 0.0), [], [t_])
    si = 0
    def load(ti):
        b = ti % NB; r0 = ti * 128
        k.dma('sp', QT[b][:], qT[:, r0:r0 + 128], wbuf=QT[b], grp=gl[b]); k.dma('sp', KT[b][:], kT[:, r0:r0 + 128], wbuf=KT[b], grp=gl[b])
        k.dma('sp', KK[b][:], ktok[r0:r0 + 128, :], wbuf=KK[b], grp=gl[b]); k.dma('sp', VV[b][:], vtok[r0:r0 + 128, :], wbuf=VV[b], grp=gl[b])
        k.dma('sp', GG[b][:], gtok[r0:r0 + 128, :], wbuf=GG[b], grp=gl[b]); k.dma('sp', AIN[b][:], ainT[:, r0:r0 + 128], wbuf=AIN[b], grp=gl[b])
    if USE_R:
        k.mark_r(QF, KF, *KH_, *VVr_, *Q0_, *Q1_, *NM_, *S)
    ntiles = ntok // 128
    def prep(ti):
        b = ti % NB; r0 = ti * 128; pv = PV[0]
        sl = ti % 2; NM = NM_[sl]; Q0 = Q0_[sl]; Q1 = Q1_[sl]; KH = KH_[sl]; EB = EB_[sl]; VVr = VVr_[sl]
        px, pc, pr, pbl, pcT, pN, pO, pS0, pS1 = (pv[n] for n in ("px", "pc", "pr", "pbl", "pcT", "pN", "pO", "pS0", "pS1"))
        if ti + 1 < ntiles: load(ti + 1)
        T_(lambda e: e.matmul(px[:], AIN[b][:], aloraT[:], start=True, stop=True), [AIN[b], aloraT], [px]); yield
        V(lambda e: e.tensor_tensor(out=xb[:], in0=px[:], in1=abiasT[:], op=ALU.add), [px, abiasT], [xb]); yield
        A_(lambda e: e.activation(out=ax[:], in_=xb[:], func=AF.Abs), [xb], [ax]); yield
        A_(lambda e: e.activation(out=ax[:], in_=ax[:], func=AF.Exp, scale=-1.0), [ax], [ax]); yield
        A_(lambda e: e.activation(out=ax[:], in_=ax[:], func=AF.Ln, bias=1.0), [ax], [ax]); yield
        V(lambda e: e.tensor_scalar(out=mn[:], in0=xb[:], scalar1=0.0, scalar2=None, op0=ALU.min), [xb], [mn]); yield
        V(lambda e: e.tensor_tensor(out=la[:], in0=mn[:], in1=ax[:], op=ALU.subtract), [mn, ax], [la]); yield
        T_(lambda e: e.matmul(pcT[:], la[:], TLE[:], start=True, stop=True), [la, TLE], [pcT]); yield
        T_(lambda e: e.matmul(pr[:], TGT[:], la[:], start=True, stop=True), [la, TGT], [pr]); yield
        T_(lambda e: e.matmul(pbl[:], la[:], CHI[:], start=True, stop=True), [la, CHI], [pbl]); yield
        A_(lambda e: e.activation(out=EP[:], in_=pcT[:], func=AF.Exp, scale=1.0 / 16, bias=math.log(0.125)), [pcT], [EP]); yield
        A_(lambda e: e.activation(out=EN[:], in_=pcT[:], func=AF.Exp, scale=-1.0 / 16), [pcT], [EN]); yield
        A_(lambda e: e.activation(out=ER[:], in_=pr[:], func=AF.Exp, scale=1.0 / 16), [pr], [ER]); yield
        A_(lambda e: e.activation(out=EB[:], in_=pbl[:], func=AF.Exp, scale=1.0 / 16), [pbl], [EB]); yield
        V(lambda e: e.tensor_tensor(out=QF[:], in0=QT[b][:], in1=EP[:], op=ALU.mult), [QT[b], EP], [QF]); yield
        V(lambda e: e.tensor_tensor(out=KF[:], in0=KT[b][:], in1=EN[:], op=ALU.mult), [KT[b], EN], [KF]); yield
        V(lambda e: e.tensor_tensor(out=KH[:], in0=KK[b][:], in1=ER[:], op=ALU.mult), [KK[b], ER], [KH]); yield
        P_(lambda e: e.tensor_copy(out=Q0[:, 0:64], in_=QF[:, 0:64]), [QF], [Q0]); yield
        P_(lambda e: e.tensor_copy(out=Q1[:, 64:128], in_=QF[:, 64:128]), [QF], [Q1]); yield
        T_(lambda e: e.matmul(pN[:], KF[:], QF[:], start=True, stop=True), [KF, QF], [pN]); yield
        V(lambda e: e.tensor_tensor(out=NM[:], in0=pN[:], in1=TLE[:], op=ALU.mult), [pN, TLE], [NM]); yield
        P_(lambda e: e.tensor_copy(out=VVr[:], in_=VV[b][:]), [VV[b]], [VVr]); yield
    def chain(ti):
        b = ti % NB; r0 = ti * 128; pv = PV[0]
        sl = ti % 2; NM = NM_[sl]; Q0 = Q0_[sl]; Q1 = Q1_[sl]; KH = KH_[sl]; EB = EB_[sl]; VVr = VVr_[sl]
        px, pc, pr, pbl, pcT, pN, pO, pS0, pS1 = (pv[n] for n in ("px", "pc", "pr", "pbl", "pcT", "pN", "pO", "pS0", "pS1"))
        S0 = S[(2 * ti) % 3]; S1 = S[(2 * ti + 1) % 3]; S2 = S[(2 * ti + 2) % 3]
        T_(lambda e: e.matmul(pO[:], NM[:], VVr[:], start=True, stop=False), [NM, VVr], [pO]); yield
        T_(lambda e: e.matmul(pO[:], Q0[:], S0[:], start=False, stop=False), [Q0, S0], [pO]); yield
        T_(lambda e: e.matmul(pS0[:], KH[0:64, :], VVr[0:64, :], start=True, stop=True), [KH, VVr], [pS0]); yield
        V(lambda e: e.scalar_tensor_tensor(out=S1[:], in0=S0[:], scalar=EB[:, 0:1], in1=pS0[:], op0=ALU.mult, op1=ALU.add), [S0, EB, pS0], [S1]); yield
        T_(lambda e: e.matmul(pO[:], Q1[:], S1[:], start=False, stop=True), [Q1, S1], [pO]); yield
        T_(lambda e: e.matmul(pS1[:], KH[64:128, :], VVr[64:128, :], start=True, stop=True), [KH, VVr], [pS1]); yield
        V(lambda e: e.scalar_tensor_tensor(out=S2[:], in0=S1[:], scalar=EB[:, 1:2], in1=pS1[:], op0=ALU.mult, op1=ALU.add), [S1, EB, pS1], [S2]); yield
        A_(lambda e: e.activation(out=junk[:], in_=pO[:], func=AF.Square, accum_out=ss[:]), [pO], [junk, ss]); yield
        A_(lambda e: e.activation(out=ss[:], in_=ss[:], func=AF.Sqrt, scale=1.0 / 128, bias=1e-6), [ss], [ss]); yield
        V(lambda e: e.reciprocal(out=ss[:], in_=ss[:]), [ss], [ss]); yield
        A_(lambda e: e.activation(out=SG[:], in_=GG[b][:], func=AF.Silu), [GG[b]], [SG]); yield
        V(lambda e: e.scalar_tensor_tensor(out=O1[:], in0=pO[:], scalar=ss[:, 0:1], in1=normgT[:], op0=ALU.mult, op1=ALU.mult), [pO, ss, normgT], [O1]); yield
        OB = OO[ti % 2]
        V(lambda e: e.tensor_tensor(out=OB[:], in0=O1[:], in1=SG[:], op=ALU.mult), [O1, SG], [OB]); yield
        k.dma('sp', otok[r0:r0 + 128, :], OB[:], rbuf=OB, grp=go[ti % 2]); yield

    load(0)
    yield from prep(0)
    for ti in range(ntiles):
        gc = chain(ti); gp = prep(ti + 1) if ti + 1 < ntiles else None
        while gc is not None or gp is not None:
            if gp is not None:
                try:
                    for _ in range(2):
                        next(gp); yield
                except StopIteration:
                    gp = None
            if gc is not None:
                try:
                    next(gc); yield
                except StopIteration:
                    gc = None
def gla_layout(h, z_gla, alpha_lora, alpha_bias, norm_g):
    import numpy as np
    f = np.float32; ntok = z_gla.shape[0]
    d = dict(gla_consts())
    if h is None:
        z = lambda *s: np.zeros(s, f)
        d.update(qT=z(64, ntok), kT=z(64, ntok), ktok=z(ntok, 64), vtok=z(ntok, 128), gtok=z(ntok, 128), ainT=z(16, ntok), alora=z(16, 64), abias=z(128, 64), normg=z(128, 128))
    else:
        q = z_gla[:, h * 64:(h + 1) * 64]; kk = z_gla[:, 320 + h * 64:320 + (h + 1) * 64]
        v = z_gla[:, 640 + h * 128:640 + (h + 1) * 128]; g = z_gla[:, 1280 + h * 128:1280 + (h + 1) * 128]; a = z_gla[:, 1920:1936]
        c = np.ascontiguousarray
        d.update(qT=c(q.T), kT=c(kk.T), ktok=c(kk), vtok=c(v), gtok=c(g), ainT=c(a.T), alora=c(alpha_lora[:, h * 64:(h + 1) * 64]),
                 abias=c(np.broadcast_to(alpha_bias[h * 64:(h + 1) * 64], (128, 64))), normg=c(np.broadcast_to(norm_g[h * 128:(h + 1) * 128], (128, 128))))
    return {"g_" + n: v for n, v in d.items()}

RW_SHARED = ("loraT", "mu_w", "mu_a", "mu_g", "tri_le", "tri_gt", "mask4", "ident", "chind", "vresT", "mu_v")
def rw_consts():
    import numpy as np
    s = np.arange(128)[:, None]; t = np.arange(128)[None, :]
    same = (s // 64) == (t // 64)
    f = np.float32
    tle = (same & (s <= t)).astype(f); tlt = (same & (s < t)).astype(f); tgt = (same & (s > t)).astype(f)
    return {"tri_le": tle, "tri_gt": tgt, "mask4": np.concatenate([tlt, tle, tlt, tle], 1),
            "ident": np.eye(128, dtype=f), "chind": np.stack([(np.arange(128) < 64), (np.arange(128) >= 64)], 1).astype(f)}

NLEV = 5
USE_R = True
def emit_RW(k, ntok, pfx, has_vres, banks, shared):
    def D_(n, shp, kind="ExternalInput"):
        if n in RW_SHARED:
            if n not in shared: shared[n] = k.dram("rs_" + n, shp, F32, kind)
            return shared[n]
        return k.dram(pfx + n, shp, F32, kind)
    rkv = D_("rkv", [ntok + 1, 192]); mu_rkv = D_("mu_rkv", [128, 192])
    loraT = D_("loraT", [480, ntok + 1]); mu_w = D_("mu_w", [96, 1]); mu_a = D_("mu_a", [128, 1]); mu_g = D_("mu_g", [128, 2])
    w_lora = D_("w_lora", [96, 64]); a_lora = D_("a_lora", [128, 64]); g_lora = D_("g_lora", [128, 2, 64])
    bc5 = D_("bc5", [128, 7, 64])
    tri_le = D_("tri_le", [128, 128]); tri_gt = D_("tri_gt", [128, 128]); mask4 = D_("mask4", [128, 512]); ident = D_("ident", [128, 128]); chind = D_("chind", [128, 2])
    if has_vres:
        vfirst = D_("vfirst", [ntok, 64]); vresT = D_("vresT", [64, ntok + 1]); mu_v = D_("mu_v", [64, 1]); vres_b = D_("vres_b", [64, 64]); vbias = D_("vbias", [128, 64])
    ytok = D_("ytok", [ntok, 64], "ExternalOutput")
    if not has_vres:
        vout = D_("vout", [ntok, 64], "ExternalOutput")
    sb = lambda n, shp, dt=F32: k.sb(pfx + n, shp, dt)
    MU = sb("MU", [128, 192]); MUW = sb("MUW", [96, 1]); MUA = sb("MUA", [128, 1]); MUG = sb("MUG", [128, 2])
    WL = sb("WL", [96, 64]); AL = sb("AL", [128, 64]); GL = sb("GL", [128, 2, 64]); BC = sb("BC", [128, 7, 64])
    TLE = sb("TLE", [128, 128]); TGT = sb("TGT", [128, 128]); M4 = sb("M4", [128, 512]); ID = sb("ID", [128, 128]); CHI = sb("CHI", [128, 2])
    loads = [(MU, mu_rkv), (MUW, mu_w), (MUA, mu_a), (MUG, mu_g), (WL, w_lora), (AL, a_lora), (GL, g_lora), (BC, bc5), (TLE, tri_le), (TGT, tri_gt), (M4, mask4), (ID, ident), (CHI, chind)]
    if has_vres:
        MUV = sb("MUV", [64, 1]); VB = sb("VB", [64, 64]); VBI = sb("VBI", [128, 64])
        loads += [(MUV, mu_v), (VB, vres_b), (VBI, vbias)]
    gcon = k.group(pfx + "con"); gl = [k.group(pfx + f"l{i}") for i in range(2)]; go = [k.group(pfx + f"o{i}") for i in range(2)]
    for t, d in loads:
        k.dma('sp', t[:], d, wbuf=t, grp=gcon)
    W0, A0, KKW, KA, RK, LNW, LNB = (BC[:, i, :] for i in range(7))
    if USE_R:
        cp = []
        for nm_, t_ in (("WL", WL), ("AL", AL), ("GL", GL), ("TLE", TLE), ("TGT", TGT), ("ID", ID), ("CHI", CHI)) + ((("VB", VB),) if has_vres else ()):
            t2 = sb(nm_ + "r", list(t_.t.shape)); k.mark_r(t2)
            k.op('dve', lambda e: e.tensor_copy(out=t2[:], in_=t_[:]), [t_], [t2]); cp.append(t2)
        if has_vres: WLr, ALr, GLr, TLEr, TGTr, IDr, CHIr, VBr = cp
        else: WLr, ALr, GLr, TLEr, TGTr, IDr, CHIr = cp
    else:
        WLr, ALr, GLr, TLEr, TGTr, IDr, CHIr = WL, AL, GL, TLE, TGT, ID, CHI
        if has_vres: VBr = VB
    NB = 2
    CUR = [sb(f"CUR{i}", [128, 192]) for i in range(NB)]; PRV = [sb(f"PRV{i}", [128, 192]) for i in range(NB)]
    LWc = [sb(f"LWc{i}", [96, 129]) for i in range(NB)]; LAc = [sb(f"LAc{i}", [128, 129]) for i in range(NB)]; LGc = [sb(f"LGc{i}", [128, 2, 129]) for i in range(NB)]
    if has_vres:
        VF = [sb(f"VF{i}", [128, 64]) for i in range(NB)]; VRc = [sb(f"VRc{i}", [64, 129]) for i in range(NB)]
        vS = sb("vS", [64, 128]); vD = sb("vD", [64, 128]); xv = sb("xv", [128, 64])
    Z = sb("Z", [128, 192]); Dz = sb("Dz", [128, 192])
    wS = sb("wS", [96, 128]); wD = sb("wD", [96, 128]); aS = sb("aS", [128, 128]); aD = sb("aD", [128, 128]); gS = sb("gS", [128, 2, 128]); gD = sb("gD", [128, 2, 128])
    xw = sb("xw", [128, 64]); t64 = [sb(f"t64_{i}", [128, 64]) for i in range(6)]
    ew = sb("ew", [128, 64]); asg = sb("asg", [128, 64]); gg_ = [sb(f"gg{i}", [128, 64]) for i in range(2)]; VP_ = [sb(f"VP{i}", [128, 64]) for i in range(2)]
    kk = sb("kk", [128, 64]); kkn = sb("kkn", [128, 64]); bv = sb("bv", [128, 64]); k2 = sb("k2", [128, 64])
    col = [sb(f"col{i}", [128, 1]) for i in range(6)]; junk = sb("junk", [128, 64]); junk2 = sb("junk2", [128, 64]); bsum_ = [sb(f"bsum{i}", [128, 1]) for i in range(2)]
    Em = sb("Em", [128, 64]); Ep = sb("Ep", [128, 64]); Eme = sb("Eme", [128, 64]); Er = sb("Er", [128, 64]); cex = sb("cex", [128, 64])
    T4 = sb("T4", [128, 4, 64])
    KH_ = [sb(f"KH{i}", [128, 64]) for i in range(2)]; BH_ = [sb(f"BH{i}", [128, 64]) for i in range(2)]; PC_ = [sb(f"PC{i}", [64, 2]) for i in range(2)]
    FT = sb("FT", [64, 512])
    AT0_ = [sb(f"AT0{i}", [64, 128]) for i in range(2)]; AT1_ = [sb(f"AT1{i}", [64, 128]) for i in range(2)]; RT0_ = [sb(f"RT0{i}", [64, 128]) for i in range(2)]; RT1_ = [sb(f"RT1{i}", [64, 128]) for i in range(2)]
    NM_ = [sb(f"NM{i}", [128, 512]) for i in range(2)]; Ncur = [sb(f"Ncur{i}", [128, 128]) for i in range(2)]; Lcur = [sb(f"Lcur{i}", [128, 128]) for i in range(2)]
    X_ = [sb(f"X{i}", [128, 128]) for i in range(2)]; Y = sb("Y", [128, 128])
    RHS = sb("RHS", [128, 64]); U = sb("U", [128, 64])
    S = [sb(f"S{i}", [64, 64]) for i in range(3)]
    yc = sb("yc", [128, 64]); yn = sb("yn", [128, 64]); YO = [sb(f"YO{i}", [128, 64]) for i in range(2)]; VO = [sb(f"VO{i}", [128, 64]) for i in range(2)]
    if USE_R:
        k.mark_r(wS, aS, gS, ew, T4, FT, *AT0_, *AT1_, *RT0_, *RT1_, *NM_, *Ncur, *Lcur, *X_, Y, RHS, U, *S, *KH_, *BH_, *VP_)
        if has_vres: k.mark_r(vS)
    B0, B1, B2, B3 = banks
    vw = k.view
    pw = vw(B0, B0[:, 0:64]); pa = vw(B0, B0[:, 64:128]); pg = vw(B0, B0[:, 128:192]); pvg = vw(B0, B0[:, 192:256])
    pce = vw(B0, B0[:, 256:320]); prem = vw(B0, B0[:, 320:384]); pPC = vw(B0, B0[0:64, 0:2]); pYu = vw(B0, B0[:, 384:512])
    pT = vw(B1, B1[0:64, :]); pNM = B1
    pL = vw(B2, B2[:, 0:128]); pN2 = vw(B2, B2[:, 128:256]); pL2 = vw(B2, B2[:, 256:384]); pXu = vw(B2, B2[:, 384:512])
    pR = vw(B3, B3[:, 0:64]); pU = vw(B3, B3[:, 64:128]); pS = vw(B3, B3[0:64, 128:192]); pYo = vw(B3, B3[:, 192:256])
    V = lambda fn, r, w: k.op('dve', fn, r, w)
    A_ = lambda fn, r, w: k.op('act', fn, r, w)
    P_ = lambda fn, r, w: k.op('pool', fn, r, w)
    T_ = lambda fn, r, w: k.op('pe', fn, r, w)
    for t in (*AT0_, *AT1_, *RT0_, *RT1_, S[0]):
        P_(lambda e: e.memset(t[:], 0.0), [], [t])
    si = 0
    def load(ti):
        b = ti % NB; r0 = ti * 128
        k.dma('sp', CUR[b][:], rkv[r0 + 1:r0 + 129, :], wbuf=CUR[b], grp=gl[b]); k.dma('sp', PRV[b][:], rkv[r0:r0 + 128, :], wbuf=PRV[b], grp=gl[b])
        k.dma('sp', LWc[b][:], loraT[0:96, r0:r0 + 129], wbuf=LWc[b], grp=gl[b]); k.dma('sp', LAc[b][:], loraT[96:224, r0:r0 + 129], wbuf=LAc[b], grp=gl[b])
        k.dma('sp', LGc[b][:], loraT[224:480, r0:r0 + 129].rearrange("(c p) t -> p c t", p=128), wbuf=LGc[b], grp=gl[b])
        if has_vres:
            k.dma('sp', VF[b][:], vfirst[r0:r0 + 128, :], wbuf=VF[b], grp=gl[b]); k.dma('sp', VRc[b][:], vresT[:, r0:r0 + 129], wbuf=VRc[b], grp=gl[b])
    ntiles = ntok // 128
    def prep(ti):
        b = ti % NB; r0 = ti * 128
        sl = ti % 2; NM = NM_[sl]; X = X_[sl]; AT0 = AT0_[sl]; AT1 = AT1_[sl]; RT0 = RT0_[sl]; RT1 = RT1_[sl]; KH = KH_[sl]; BH = BH_[sl]; PC = PC_[sl]; VP = VP_[sl]; gg = gg_[sl]; bsum = bsum_[sl]
        if ti + 1 < ntiles: load(ti + 1)
        V(lambda e: e.tensor_tensor(out=Dz[:], in0=PRV[b][:], in1=CUR[b][:], op=ALU.subtract), [PRV[b], CUR[b]], [Dz]); yield
        V(lambda e: e.tensor_tensor(out=Dz[:], in0=Dz[:], in1=MU[:], op=ALU.mult), [Dz, MU], [Dz]); yield
        V(lambda e: e.tensor_tensor(out=Z[:], in0=Dz[:], in1=CUR[b][:], op=ALU.add), [Dz, CUR[b]], [Z]); yield
        r_ = Z[:, 0:64]; k_ = Z[:, 64:128]; v_ = Z[:, 128:192]
        P_(lambda e: e.tensor_tensor(out=wD[:], in0=LWc[b][:, 0:128], in1=LWc[b][:, 1:129], op=ALU.subtract), [LWc[b]], [wD]); yield
        V(lambda e: e.scalar_tensor_tensor(out=wS[:], in0=wD[:], scalar=MUW[:, 0:1], in1=LWc[b][:, 1:129], op0=ALU.mult, op1=ALU.add), [wD, MUW, LWc[b]], [wS]); yield
        P_(lambda e: e.tensor_tensor(out=aD[:], in0=LAc[b][:, 0:128], in1=LAc[b][:, 1:129], op=ALU.subtract), [LAc[b]], [aD]); yield
        V(lambda e: e.scalar_tensor_tensor(out=aS[:], in0=aD[:], scalar=MUA[:, 0:1], in1=LAc[b][:, 1:129], op0=ALU.mult, op1=ALU.add), [aD, MUA, LAc[b]], [aS]); yield
        for c in range(2):
            P_(lambda e: e.tensor_tensor(out=gD[:, c, :], in0=LGc[b][:, c, 0:128], in1=LGc[b][:, c, 1:129], op=ALU.subtract), [LGc[b]], [gD]); yield
            V(lambda e: e.scalar_tensor_tensor(out=gS[:, c, :], in0=gD[:, c, :], scalar=MUG[:, c:c + 1], in1=LGc[b][:, c, 1:129], op0=ALU.mult, op1=ALU.add), [gD, MUG, LGc[b]], [gS]); yield
        A_(lambda e: e.activation(out=wS[:], in_=wS[:], func=AF.Tanh), [wS], [wS]); yield
        A_(lambda e: e.activation(out=gS[:], in_=gS[:], func=AF.Sigmoid), [gS], [gS]); yield
        T_(lambda e: e.matmul(pw[:], wS[:], WLr[:], start=True, stop=True), [wS, WLr], [pw]); yield
        T_(lambda e: e.matmul(pa[:], aS[:], ALr[:], start=True, stop=True), [aS, ALr], [pa]); yield
        T_(lambda e: e.matmul(pg[:], gS[:, 0, :], GLr[:, 0, :], start=True, stop=False), [gS, GLr], [pg]); yield
        T_(lambda e: e.matmul(pg[:], gS[:, 1, :], GLr[:, 1, :], start=False, stop=True), [gS, GLr], [pg]); yield
        if has_vres:
            P_(lambda e: e.tensor_tensor(out=vD[:], in0=VRc[b][:, 0:128], in1=VRc[b][:, 1:129], op=ALU.subtract), [VRc[b]], [vD]); yield
            V(lambda e: e.scalar_tensor_tensor(out=vS[:], in0=vD[:], scalar=MUV[:, 0:1], in1=VRc[b][:, 1:129], op0=ALU.mult, op1=ALU.add), [vD, MUV, VRc[b]], [vS]); yield
            T_(lambda e: e.matmul(pvg[:], vS[:], VBr[:], start=True, stop=True), [vS, VBr], [pvg]); yield
        V(lambda e: e.tensor_tensor(out=xw[:], in0=pw[:], in1=W0, op=ALU.add), [pw, BC], [xw]); yield
        ax, mn, ta = t64[0], t64[1], t64[2]
        A_(lambda e: e.activation(out=ax[:], in_=xw[:], func=AF.Abs), [xw], [ax]); yield
        A_(lambda e: e.activation(out=ax[:], in_=ax[:], func=AF.Exp, scale=-1.0), [ax], [ax]); yield
        A_(lambda e: e.activation(out=ax[:], in_=ax[:], func=AF.Ln, bias=1.0), [ax], [ax]); yield
        V(lambda e: e.tensor_scalar(out=mn[:], in0=xw[:], scalar1=0.0, scalar2=None, op0=ALU.min), [xw], [mn]); yield
        V(lambda e: e.tensor_tensor(out=mn[:], in0=mn[:], in1=ax[:], op=ALU.subtract), [mn, ax], [mn]); yield
        A_(lambda e: e.activation(out=ew[:], in_=mn[:], func=AF.Exp, bias=-0.5), [mn], [ew]); yield
        V(lambda e: e.tensor_tensor(out=ta[:], in0=pa[:], in1=A0, op=ALU.add), [pa, BC], [ta]); yield
        A_(lambda e: e.activation(out=asg[:], in_=ta[:], func=AF.Sigmoid), [ta], [asg]); yield
        A_(lambda e: e.copy(out=gg[:], in_=pg[:]), [pg], [gg]); yield
        if has_vres:
            V(lambda e: e.tensor_tensor(out=xv[:], in0=pvg[:], in1=VBI[:], op=ALU.add), [pvg, VBI], [xv]); yield
            A_(lambda e: e.activation(out=xv[:], in_=xv[:], func=AF.Sigmoid), [xv], [xv]); yield
            V(lambda e: e.tensor_tensor(out=VP[:], in0=VF[b][:], in1=v_, op=ALU.subtract), [VF[b], Z], [VP]); yield
            V(lambda e: e.tensor_tensor(out=VP[:], in0=VP[:], in1=xv[:], op=ALU.mult), [VP, xv], [VP]); yield
            V(lambda e: e.tensor_tensor(out=VP[:], in0=VP[:], in1=v_, op=ALU.add), [VP, Z], [VP]); yield
        else:
            P_(lambda e: e.tensor_copy(out=VP[:], in_=v_), [Z], [VP]); yield
        ssq, rn = col[0], col[1]
        V(lambda e: e.tensor_tensor(out=kk[:], in0=k_, in1=KKW, op=ALU.mult), [Z, BC], [kk]); yield
        A_(lambda e: e.activation(out=junk[:], in_=kk[:], func=AF.Square, accum_out=ssq[:]), [kk], [junk, ssq]); yield
        V(lambda e: e.tensor_scalar(out=rn[:], in0=ssq[:], scalar1=1e-24, scalar2=None, op0=ALU.max), [ssq], [rn]); yield
        A_(lambda e: e.activation(out=rn[:], in_=rn[:], func=AF.Sqrt), [rn], [rn]); yield
        V(lambda e: e.reciprocal(out=rn[:], in_=rn[:]), [rn], [rn]); yield
        V(lambda e: e.tensor_scalar(out=kkn[:], in0=kk[:], scalar1=rn[:, 0:1], scalar2=None, op0=ALU.mult), [kk, rn], [kkn]); yield
        V(lambda e: e.tensor_tensor(out=bv[:], in0=kkn[:], in1=asg[:], op=ALU.mult), [kkn, asg], [bv]); yield
        V(lambda e: e.scalar_tensor_tensor(out=k2[:], in0=asg[:], scalar=-1.0, in1=KA, op0=ALU.add, op1=ALU.mult), [asg, BC], [k2]); yield
        V(lambda e: e.scalar_tensor_tensor(out=k2[:], in0=k2[:], scalar=1.0, in1=k_, op0=ALU.add, op1=ALU.mult), [k2, Z], [k2]); yield
        tb = t64[3]
        V(lambda e: e.tensor_tensor(out=tb[:], in0=r_, in1=k2[:], op=ALU.mult), [Z, k2], [tb]); yield
        V(lambda e: e.scalar_tensor_tensor(out=junk[:], in0=tb[:], scalar=1.0, in1=RK, op0=ALU.mult, op1=ALU.mult, accum_out=bsum[:]), [tb, BC], [junk, bsum]); yield
        T_(lambda e: e.matmul(pce[:], TLEr[:], ew[:], start=True, stop=True), [TLEr, ew], [pce]); yield
        T_(lambda e: e.matmul(prem[:], TGTr[:], ew[:], start=True, stop=True), [TGTr, ew], [prem]); yield
        T_(lambda e: e.matmul(pPC[:], ew[:], CHIr[:], start=True, stop=True), [ew, CHIr], [pPC]); yield
        A_(lambda e: e.activation(out=Em[:], in_=pce[:], func=AF.Exp, scale=-1.0), [pce], [Em]); yield
        A_(lambda e: e.activation(out=Ep[:], in_=pce[:], func=AF.Exp), [pce], [Ep]); yield
        V(lambda e: e.tensor_tensor(out=cex[:], in0=pce[:], in1=ew[:], op=ALU.subtract), [pce, ew], [cex]); yield
        A_(lambda e: e.activation(out=Eme[:], in_=cex[:], func=AF.Exp, scale=-1.0), [cex], [Eme]); yield
        A_(lambda e: e.activation(out=Er[:], in_=prem[:], func=AF.Exp, scale=-1.0), [prem], [Er]); yield
        A_(lambda e: e.activation(out=PC[:], in_=pPC[:], func=AF.Exp, scale=-1.0), [pPC], [PC]); yield
        V(lambda e: e.scalar_tensor_tensor(out=T4[:, 0, :], in0=kkn[:], scalar=-1.0, in1=Eme[:], op0=ALU.mult, op1=ALU.mult), [kkn, Eme], [T4]); yield
        V(lambda e: e.tensor_tensor(out=T4[:, 1, :], in0=r_, in1=Em[:], op=ALU.mult), [Z, Em], [T4]); yield
        V(lambda e: e.tensor_tensor(out=T4[:, 2, :], in0=bv[:], in1=Ep[:], op=ALU.mult), [bv, Ep], [T4]); yield
        V(lambda e: e.tensor_tensor(out=T4[:, 3, :], in0=k2[:], in1=Ep[:], op=ALU.mult), [k2, Ep], [T4]); yield
        P_(lambda e: e.tensor_tensor(out=KH[:], in0=k2[:], in1=Er[:], op=ALU.mult), [k2, Er], [KH]); yield
        P_(lambda e: e.tensor_tensor(out=BH[:], in0=bv[:], in1=Er[:], op=ALU.mult), [bv, Er], [BH]); yield
        for j in range(4):
            T_(lambda e: e.matmul(pT[:, j * 128:(j + 1) * 128], T4[:, j, :], IDr[:], start=True, stop=True), [T4, IDr], [pT]); yield
        A_(lambda e: e.copy(out=FT[:], in_=pT[:]), [pT], [FT]); yield
        aT = FT[:, 0:128]; rT = FT[:, 128:256]; bT = FT[:, 256:384]; kT = FT[:, 384:512]
        P_(lambda e: e.tensor_copy(out=AT0[:, 0:64], in_=FT[:, 0:64]), [FT], [AT0]); yield
        P_(lambda e: e.tensor_copy(out=AT1[:, 64:128], in_=FT[:, 64:128]), [FT], [AT1]); yield
        P_(lambda e: e.tensor_copy(out=RT0[:, 0:64], in_=FT[:, 128:192]), [FT], [RT0]); yield
        P_(lambda e: e.tensor_copy(out=RT1[:, 64:128], in_=FT[:, 192:256]), [FT], [RT1]); yield
        T_(lambda e: e.matmul(pNM[:, 0:256], bT, FT[:, 0:256], start=True, stop=True), [FT], [pNM]); yield
        T_(lambda e: e.matmul(pNM[:, 256:512], kT, FT[:, 0:256], start=True, stop=True), [FT], [pNM]); yield
        T_(lambda e: e.matmul(pL[:], aT, bT, start=True, stop=True), [FT], [pL]); yield
        V(lambda e: e.tensor_tensor(out=NM[:], in0=pNM[:], in1=M4[:], op=ALU.mult), [pNM, M4], [NM]); yield
        NC_, LC_ = Ncur[0], Lcur[0]
        P_(lambda e: e.tensor_copy(out=NC_[:], in_=NM[:, 0:128]), [NM], [NC_]); yield
        V(lambda e: e.tensor_tensor(out=LC_[:], in0=pL[:], in1=TGT[:], op=ALU.mult), [pL, TGT], [LC_]); yield
        P_(lambda e: e.tensor_tensor(out=X[:], in0=NC_[:], in1=ID[:], op=ALU.add), [NC_, ID], [X]); yield
        P_(lambda e: e.tensor_tensor(out=Y[:], in0=LC_[:], in1=ID[:], op=ALU.add), [LC_, ID], [Y]); yield
        for lev in range(NLEV):
            last = lev == NLEV - 1
            Nn, Ln = Ncur[(lev + 1) % 2], Lcur[(lev + 1) % 2]
            T_(lambda e: e.matmul(pN2[:], LC_[:], NC_[:], start=True, stop=True), [LC_, NC_], [pN2]); yield
            if not last:
                T_(lambda e: e.matmul(pL2[:], NC_[:], LC_[:], start=True, stop=True), [LC_, NC_], [pL2]); yield
            A_(lambda e: e.copy(out=Nn[:], in_=pN2[:]), [pN2], [Nn]); yield
            if not last:
                V(lambda e: e.tensor_copy(out=Ln[:], in_=pL2[:]), [pL2], [Ln]); yield
            T_(lambda e: e.matmul(pXu[:], Y[:], Nn[:], start=True, stop=True), [Y, Nn], [pXu]); yield
            if not last:
                T_(lambda e: e.matmul(pYu[:], Nn[:], Y[:], start=True, stop=True), [Y, Nn], [pYu]); yield
            V(lambda e: e.tensor_tensor(out=X[:], in0=X[:], in1=pXu[:], op=ALU.add), [X, pXu], [X]); yield
            if not last:
                V(lambda e: e.tensor_tensor(out=Y[:], in0=Y[:], in1=pYu[:], op=ALU.add), [Y, pYu], [Y]); yield
            NC_, LC_ = Nn, Ln
    def chain(ti):
        r0 = ti * 128; s1, s2, rstd = col[3], col[4], col[5]
        sl = ti % 2; NM = NM_[sl]; X = X_[sl]; AT0 = AT0_[sl]; AT1 = AT1_[sl]; RT0 = RT0_[sl]; RT1 = RT1_[sl]; KH = KH_[sl]; BH = BH_[sl]; PC = PC_[sl]; VP = VP_[sl]; gg = gg_[sl]; bsum = bsum_[sl]
        Ss = [S[(2 * ti) % 3], S[(2 * ti + 1) % 3], S[(2 * ti + 2) % 3]]
        ATc = (AT0, AT1); RTc = (RT0, RT1)
        for c in range(2):
            ps_ = slice(c * 64, (c + 1) * 64)
            T_(lambda e: e.matmul(pR[:], NM[:, 256:384], VP[:], start=True, stop=False), [NM, VP], [pR]); yield
            T_(lambda e: e.matmul(pR[:], ATc[c][:], Ss[c][:], start=False, stop=True), [ATc[c], Ss[c]], [pR]); yield
            A_(lambda e: e.copy(out=RHS[ps_, :], in_=pR[ps_, :]), [pR], [RHS]); yield
            T_(lambda e: e.matmul(pU[:], X[ps_, :], RHS[ps_, :], start=True, stop=True), [X, RHS], [pU]); yield
            A_(lambda e: e.copy(out=U[ps_, :], in_=pU[ps_, :]), [pU], [U]); yield
            T_(lambda e: e.matmul(pS[:], BH[ps_, :], U[ps_, :], start=True, stop=False), [BH, U], [pS]); yield
            T_(lambda e: e.matmul(pS[:], KH[ps_, :], VP[ps_, :], start=False, stop=True), [KH, VP], [pS]); yield
            V(lambda e: e.scalar_tensor_tensor(out=Ss[c + 1][:], in0=Ss[c][:], scalar=PC[:, c:c + 1], in1=pS[:], op0=ALU.mult, op1=ALU.add), [Ss[c], PC, pS], [Ss[c + 1]]); yield
        T_(lambda e: e.matmul(pYo[:], NM[:, 128:256], U[:], start=True, stop=False), [NM, U], [pYo]); yield
        T_(lambda e: e.matmul(pYo[:], NM[:, 384:512], VP[:], start=False, stop=False), [NM, VP], [pYo]); yield
        T_(lambda e: e.matmul(pYo[:], RT0[:], Ss[0][:], start=False, stop=False), [RT0, Ss[0]], [pYo]); yield
        T_(lambda e: e.matmul(pYo[:], RT1[:], Ss[1][:], start=False, stop=True), [RT1, Ss[1]], [pYo]); yield
        A_(lambda e: e.activation(out=junk2[:], in_=pYo[:], func=AF.Copy, accum_out=s1[:]), [pYo], [junk2, s1]); yield
        V(lambda e: e.tensor_scalar(out=s1[:], in0=s1[:], scalar1=1.0 / 64, scalar2=None, op0=ALU.mult), [s1], [s1]); yield
        V(lambda e: e.tensor_scalar(out=yc[:], in0=pYo[:], scalar1=s1[:, 0:1], scalar2=None, op0=ALU.subtract), [pYo, s1], [yc]); yield
        A_(lambda e: e.activation(out=junk2[:], in_=yc[:], func=AF.Square, accum_out=s2[:]), [yc], [junk2, s2]); yield
        A_(lambda e: e.activation(out=rstd[:], in_=s2[:], func=AF.Sqrt, scale=1.0 / 64, bias=64e-5), [s2], [rstd]); yield
        V(lambda e: e.reciprocal(out=rstd[:], in_=rstd[:]), [rstd], [rstd]); yield
        V(lambda e: e.scalar_tensor_tensor(out=yn[:], in0=yc[:], scalar=rstd[:, 0:1], in1=LNW, op0=ALU.mult, op1=ALU.mult), [yc, rstd, BC], [yn]); yield
        V(lambda e: e.tensor_tensor(out=yn[:], in0=yn[:], in1=LNB, op=ALU.add), [yn, BC], [yn]); yield
        V(lambda e: e.scalar_tensor_tensor(out=yn[:], in0=VP[:], scalar=bsum[:, 0:1], in1=yn[:], op0=ALU.mult, op1=ALU.add), [VP, bsum, yn], [yn]); yield
        OB = YO[ti % 2]
        V(lambda e: e.tensor_tensor(out=OB[:], in0=yn[:], in1=gg[:], op=ALU.mult), [yn, gg], [OB]); yield
        k.dma('sp', ytok[r0:r0 + 128, :], OB[:], rbuf=OB, grp=go[ti % 2]); yield
        if not has_vres:
            OV = VO[ti % 2]
            P_(lambda e: e.tensor_copy(out=OV[:], in_=VP[:]), [VP], [OV]); yield
            k.dma('sp', vout[r0:r0 + 128, :], OV[:], rbuf=OV, grp=go[ti % 2]); yield

    load(0)
    yield from prep(0)
    for ti in range(ntiles):
        gc = chain(ti); gp = prep(ti + 1) if ti + 1 < ntiles else None
        while gc is not None or gp is not None:
            if gp is not None:
                try:
                    for _ in range(4):
                        next(gp); yield
                except StopIteration:
                    gp = None
            if gc is not None:
                try:
                    next(gc); yield
                except StopIteration:
                    gc = None
def rw_layout(pfx, h, z_rw, P, vfirst=None, vres_u=None):
    import numpy as np
    f = np.float32; ntok = z_rw.shape[0]; c = np.ascontiguousarray
    has_vres = vres_u is not None
    d = dict(rw_consts())
    zpad = lambda a: np.concatenate([np.zeros((1, a.shape[1]), f), a], 0)
    bc = lambda v: np.broadcast_to(v, (128, v.shape[-1]))
    if h is None:
        z = lambda *s: np.zeros(s, f)
        d.update(rkv=z(ntok + 1, 192), mu_rkv=z(128, 192), loraT=z(480, ntok + 1), mu_w=z(96, 1), mu_a=z(128, 1), mu_g=z(128, 2), w_lora=z(96, 64), a_lora=z(128, 64),
                 g_lora=z(128, 2, 64), bc5=z(128, 7, 64))
        if has_vres: d.update(vfirst=z(ntok, 64), vresT=z(64, ntok + 1), mu_v=z(64, 1), vres_b=z(64, 64), vbias=z(128, 64))
    else:
        hs = slice(h * 64, (h + 1) * 64)
        mu = P["rwkv_mu"]
        rkv = np.concatenate([z_rw[:, hs], z_rw[:, 640 + h * 64:640 + (h + 1) * 64], z_rw[:, 1280 + h * 64:1280 + (h + 1) * 64]], 1)
        mu_rkv = np.concatenate([mu[hs], mu[640 + h * 64:640 + (h + 1) * 64], mu[1280 + h * 64:1280 + (h + 1) * 64]])
        d.update(rkv=c(zpad(rkv)), mu_rkv=c(bc(mu_rkv)), loraT=c(zpad(z_rw[:, 1920:2400]).T),
                 mu_w=c(mu[1920:2016].reshape(96, 1)), mu_a=c(mu[2016:2144].reshape(128, 1)), mu_g=c(mu[2144:2400].reshape(2, 128).T),
                 w_lora=c(P["rwkv_w_lora"][:, hs]), a_lora=c(P["rwkv_a_lora"][:, hs]), g_lora=c(P["rwkv_g_lora"][:, hs].reshape(2, 128, 64).transpose(1, 0, 2)),
                 bc5=c(np.stack([bc(P[n][hs]) for n in ("rwkv_w0", "rwkv_a0", "rwkv_k_k", "rwkv_k_a", "rwkv_r_k", "rwkv_lnx_w", "rwkv_lnx_b")], 1)))
        if has_vres:
            d.update(vfirst=c(vfirst[:, hs]), vresT=c(zpad(vres_u).T), mu_v=c(P["rwkv_vres_mu"].reshape(64, 1)), vres_b=c(P["rwkv_vres_b"][:, hs]), vbias=c(bc(P["rwkv_vres_bias"][hs])))
    return {("rs_" if n in RW_SHARED else pfx) + n: v.astype(f) for n, v in d.items()}


def build_B(ntok, has_vres):
    k = KB()
    banks = [k.ps(f"bank{i}", [128, 512]) for i in range(8)]
    shared = {}
    ga = emit_RW(k, ntok, "r0_", has_vres, banks[0:4], shared)
    gb = emit_RW(k, ntok, "r1_", has_vres, banks[4:8], shared)
    alive = [ga, gb]
    while alive:
        for g_ in list(alive):
            try:
                next(g_)
            except StopIteration:
                alive.remove(g_)
    gg = emit_GLA(k, ntok, banks[0:3])
    gs = emit_S5(k, ntok, banks[3:7])
    alive = [(gg, 3), (gs, 2)]
    while alive:
        for it in list(alive):
            g_, n_ = it
            try:
                for _ in range(n_): next(g_)
            except StopIteration:
                alive.remove(it)
    return k.finish()

_PROGS = {}
def _prog(key, fn):
    if key not in _PROGS:
        _PROGS[key] = fn()
    return _PROGS[key]

def _tile16(v):
    return np.ascontiguousarray(np.asarray(v, np.float32).reshape(-1, 128).T)

def kernel(**inp):
    from concourse.bass_utils import run_bass_kernel_spmd
    f32 = np.float32
    inp = {n: np.asarray(v, f32) for n, v in inp.items()}
    x = inp["x"]; T = x.shape[1]; NCORE = 8; tpc = T // NCORE; depth = inp["w_in"].shape[0]
    cores = list(range(NCORE))
    c_ = np.ascontiguousarray
    xT = [c_(x[0, c * tpc:(c + 1) * tpc].T) for c in range(NCORE)]
    vfirst = None
    NCOLS = 11312
    for l in range(depth):
        extra = inp["rwkv_vres_a"][l - 1] if l > 0 else np.zeros((2048, 64), f32)
        wA = c_(np.concatenate([inp["w_in"][l], extra], 1))
        gA = _tile16(inp["norm_mix"][l])
        ncA = _prog(("A", tpc), lambda: build_A(tpc, NCOLS))
        res = run_bass_kernel_spmd(ncA, [{"xT": xT[c], "w": wA, "g": gA} for c in cores], core_ids=cores).results
        z = np.concatenate([res[c]["zT"].T for c in cores], 0)
        del res
        P = {n: inp[n][l] for n in ("rwkv_mu", "rwkv_w_lora", "rwkv_w0", "rwkv_a_lora", "rwkv_a0", "rwkv_g_lora", "rwkv_k_k", "rwkv_k_a", "rwkv_r_k", "rwkv_lnx_w", "rwkv_lnx_b")}
        has_vres = l > 0
        if has_vres:
            P.update(rwkv_vres_mu=inp["rwkv_vres_mu"][l - 1], rwkv_vres_b=inp["rwkv_vres_b"][l - 1], rwkv_vres_bias=inp["rwkv_vres_bias"][l - 1])
        zT_s5 = c_(z[:, :768].T); z_rw = z[:, 768:3168]; z_gla = z[:, 3168:5104]
        vres_u = c_(z[:, 11248:11312]) if has_vres else None
        ims = []
        for c in cores:
            m = {}
            m.update(s5_layout(c, zT_s5, inp["s5_lambda_re"][l], inp["s5_lambda_im"][l], inp["s5_log_step"][l], inp["s5_b_re"][l], inp["s5_b_im"][l],
                               inp["s5_c_re"][l], inp["s5_c_im"][l], inp["s5_d"][l]))
            for s in range(2):
                h = 2 * c + s
                m.update(rw_layout(f"r{s}_", h if h < 10 else None, z_rw, P, vfirst, vres_u))
            m.update(gla_layout(c if c < 5 else None, z_gla, inp["gla_alpha_lora"][l], inp["gla_alpha_bias"][l], inp["gla_norm_g"][l]))
            ims.append(m)
        ncB = _prog(("B", T, has_vres), lambda: build_B(T, has_vres))
        res = run_bass_kernel_spmd(ncB, ims, core_ids=cores).results
        del ims
        y = np.empty((T, 2048), f32)
        y[:, :768] = np.concatenate([res[c]["s_yo"] for c in cores], 0).T
        for h in range(10):
            y[:, 768 + h * 64:768 + (h + 1) * 64] = res[h // 2][f"r{h % 2}_ytok"]
        for h in range(5):
            y[:, 1408 + h * 128:1408 + (h + 1) * 128] = res[h]["g_otok"]
        if l == 0:
            vfirst = np.concatenate([res[h // 2][f"r{h % 2}_vout"] for h in range(10)], 1)
        del res
        last = l == depth - 1
        ncC = _prog(("C", tpc, last), lambda: build_C(tpc, last))
        gbt = _tile16(inp["gate_bias"][l])
        ims = []
        for c in cores:
            ts = slice(c * tpc, (c + 1) * tpc)
            ims.append({"xT": xT[c], "zgT": c_(z[ts, 5104:11248].T), "gb": gbt, "yT": c_(y[ts].T), "w_up": inp["w_up"][l], "w_out": inp["w_out"][l],
                        "nm": _tile16(inp["norm_mlp"][l]), "w1": inp["mlp_w1"][l], "w2": inp["mlp_w2"][l], "fn": _tile16(inp["final_norm"]),
                        "gluw": inp["s5_glu_w"][l], "glub": _tile16(inp["s5_glu_b"][l])})
        del z, y
        res = run_bass_kernel_spmd(ncC, ims, core_ids=cores).results
        del ims
        xT = [res[c]["xoT"] for c in cores]
        del res
    out = np.concatenate([xT[c].T for c in cores], 0)[None]
    return np.ascontiguousarray(out.astype(f32))
```

```python
import math
import numpy as np
from contextlib import ExitStack
import concourse.bass as bass
import concourse.mybir as mybir
F32 = mybir.dt.float32; BF16 = mybir.dt.bfloat16
AF = mybir.ActivationFunctionType; ALU = mybir.AluOpType; AX = mybir.AxisListType

class Buf:
    def __init__(self, t, name):
        self.t = t; self.name = name; self.lw = {}; self.rd = {}; self.sem = None; self.dcount = 0; self.excl = False
    def __getitem__(self, k):
        return self.t[k]

class Group:
    def __init__(self, name, sem):
        self.name = name; self.sem = sem; self.dcount = 0

class View:
    def __init__(self, parent, ap):
        object.__setattr__(self, 'parent', parent); object.__setattr__(self, 't', ap)
    def __getitem__(self, k): return self.t[k]
    def __getattr__(self, n): return getattr(self.parent, n)
    def __setattr__(self, n, v): setattr(self.parent, n, v)
    def __eq__(self, o): return (o.parent if isinstance(o, View) else o) is self.parent
    def __hash__(self): return id(self.parent)

F32R = mybir.dt.float32r
class _EngProxy:
    def __init__(self, eng, kb): self._e = eng; self._kb = kb
    def __getattr__(self, n):
        f = getattr(self._e, n)
        if not callable(f) or n in ("wait_ge", "dma_start"): return f
        kb = self._kb
        def w(*a, **kw):
            o = kw.get("out")
            if o is not None and o.dtype == F32 and o.name in kb.rset: kw["out"] = o.bitcast(F32R)
            return f(*a, **kw)
        return w
class _PEProxy:
    def __init__(self, eng, kb): self._e = eng; self._kb = kb
    def matmul(self, out, lhsT, rhs, **kw):
        rs = self._kb.rset
        if lhsT.dtype == F32 and rhs.dtype == F32 and lhsT.name in rs and rhs.name in rs:
            lhsT = lhsT.bitcast(F32R); rhs = rhs.bitcast(F32R)
        return self._e.matmul(out, lhsT, rhs, **kw)
    def __getattr__(self, n): return getattr(self._e, n)

class KB:
    SAME_ENGINE_SYNC = ('dve', 'act', 'pool')
    def view(self, parent, ap, name=None):
        return View(parent, ap)
    def __init__(self):
        self.nc = bass.Bass("TRN2", target_bir_lowering=False)
        self.es = ExitStack()
        nc = self.nc
        self.rset = set()
        self.eng = {'pe': _PEProxy(nc.tensor, self), 'dve': _EngProxy(nc.vector, self), 'act': _EngProxy(nc.scalar, self), 'pool': _EngProxy(nc.gpsimd, self), 'sp': nc.sync}
        self.sem = {k: self.es.enter_context(nc.semaphore("s_" + k)) for k in self.eng}
        self.cnt = {k: 0 for k in self.eng}
        self.waited = {}
        self.bufs = []
        self.n_inst = 0
        self.dma_sems = []
    def dram(self, name, shape, dt, kind):
        return self.nc.dram_tensor(name, list(shape), dt, kind=kind).ap()
    def sb(self, name, shape, dt=F32):
        b = Buf(self.es.enter_context(self.nc.sbuf_tensor(name, list(shape), dt)), name); self.bufs.append(b); return b
    def ps(self, name, shape, dt=F32):
        b = Buf(self.es.enter_context(self.nc.psum_tensor(name, list(shape), dt)), name); b.excl = True; self.bufs.append(b); return b
    def mark_r(self, *bufs):
        for b in bufs: self.rset.add(b.t.name if hasattr(b.t, 'name') else b.name)
    def group(self, name):
        g_ = Group(name, self.es.enter_context(self.nc.semaphore("g_" + name))); self.dma_sems.append(g_); return g_
    def _wait(self, e, key, semh, count):
        if isinstance(semh, Group):
            count = semh.dcount; semh = semh.sem
        if self.waited.get((e, key), 0) >= count: return
        self.eng[e].wait_ge(semh, count); self.waited[(e, key)] = count
    def _deps(self, e, reads, writes):
        for b in reads:
            for key, (semh, c) in b.lw.items():
                if key == e and e not in self.SAME_ENGINE_SYNC: continue
                self._wait(e, key, semh, c)
            if b.excl:
                for key, (semh, c) in b.rd.items():
                    if key != e: self._wait(e, key, semh, c)
        for b in writes:
            for d in (b.lw, b.rd):
                for key, (semh, c) in d.items():
                    if key == e and e not in self.SAME_ENGINE_SYNC: continue
                    self._wait(e, key, semh, c)
    def op(self, e, fn, reads=(), writes=()):
        self._deps(e, reads, writes)
        ins = fn(self.eng[e])
        self.cnt[e] += 1; c = self.cnt[e]
        ins.then_inc(self.sem[e], 1)
        for b in writes:
            b.lw = {e: (self.sem[e], c)}; b.rd = {}
        for b in reads:
            if b not in writes: b.rd[e] = (self.sem[e], c)
        self.n_inst += 1
        return ins
    def dma(self, q, out, in_, rbuf=None, wbuf=None, grp=None, **kw):
        b = wbuf if wbuf is not None else rbuf
        if grp is None:
            if b.sem is None:
                b.sem = self.es.enter_context(self.nc.semaphore("d_" + b.name)); self.dma_sems.append(b)
            holder = b
        else:
            holder = grp
        self._deps(q, [rbuf] if rbuf is not None else [], [wbuf] if wbuf is not None else [])
        ins = self.eng[q].dma_start(out=out, in_=in_, **kw)
        holder.dcount += 16
        ins.then_inc(holder.sem, 16)
        key = 'dma_' + holder.name
        ent = (grp, None) if grp is not None else (b.sem, b.dcount)
        if wbuf is not None:
            wbuf.lw = {key: ent}; wbuf.rd = {}
        else:
            rbuf.rd[key] = ent
        self.n_inst += 1
        return ins
    def finish(self, e='sp'):
        for b in self.dma_sems:
            self._wait(e, 'dma_' + b.name, b.sem, b.dcount)
        self.es.close()
        return self.nc

D = 2048; KC = 16
def build_A(ntok, ncols, TB=512):
    k = KB()
    xT = k.dram("xT", [D, ntok], F32, "ExternalInput")
    w = k.dram("w", [D, ncols], F32, "ExternalInput")
    g = k.dram("g", [128, KC], F32, "ExternalInput")
    zT = k.dram("zT", [ncols, ntok], F32, "ExternalOutput")
    ntb = ntok // TB
    xk = [k.sb(f"xk{i}", [128, KC, TB]) for i in range(2)]
    u = k.sb("u", [128, KC, ntok], BF16)
    sq = [k.sb(f"sq{i}", [128, TB], BF16) for i in range(2)]
    ones = k.sb("ones", [128, 128], BF16)
    gt = k.sb("gt", [128, KC])
    rstd = k.sb("rstd", [128, TB])
    psn = k.ps("psn", [128, TB])
    pz = [k.ps(f"pz{i}", [128, TB]) for i in range(4)]
    zo = [k.sb(f"zo{i}", [128, TB]) for i in range(3)]
    wb = [k.sb(f"wb{i}", [128, KC, 128], BF16) for i in range(2)]
    k.op('pool', lambda e: e.memset(ones[:], 1.0), [], [ones])
    k.dma('sp', gt[:], g[:, :], wbuf=gt)
    xv = xT.rearrange("(kc p) t -> p kc t", p=128)
    for tb in range(ntb):
        X = xk[tb % 2]
        k.dma('sp', X[:], xv[:, :, tb * TB:(tb + 1) * TB], wbuf=X)
        for kc in range(KC):
            S = sq[kc % 2]
            k.op('act', lambda e: e.activation(out=S[:], in_=X[:, kc, :], func=AF.Square), [X], [S])
            k.op('pe', lambda e: e.matmul(psn[:], ones[:], S[:], start=(kc == 0), stop=(kc == KC - 1)), [ones, S], [psn])
        k.op('act', lambda e: e.activation(out=rstd[:], in_=psn[:], func=AF.Sqrt, scale=1.0 / D, bias=1e-6), [psn], [rstd])
        k.op('dve', lambda e: e.reciprocal(out=rstd[:], in_=rstd[:]), [rstd], [rstd])
        for kc in range(KC):
            k.op('dve', lambda e: e.scalar_tensor_tensor(out=u[:, kc, tb * TB:(tb + 1) * TB], in0=X[:, kc, :], scalar=gt[:, kc:kc + 1],
                                                        in1=rstd[:], op0=ALU.mult, op1=ALU.mult), [X, gt, rstd], [u])
    wv = w.rearrange("(kc p) c -> p kc c", p=128)
    ncc = (ncols + 127) // 128
    i = 0
    for cc in range(ncc):
        cw = min(128, ncols - cc * 128)
        W = wb[cc % 2]
        k.dma('pool', W[:, :, :cw], wv[:, :, cc * 128:cc * 128 + cw], wbuf=W)
        for tb in range(ntb):
            P = pz[i % 4]; Z = zo[i % 3]
            for kc in range(KC):
                k.op('pe', lambda e: e.matmul(P[:cw, :], W[:, kc, :cw], u[:, kc, tb * TB:(tb + 1) * TB], start=(kc == 0), stop=(kc == KC - 1)), [W, u], [P])
            if i % 2 == 0:
                k.op('act', lambda e: e.copy(out=Z[:cw, :], in_=P[:cw, :]), [P], [Z])
            else:
                k.op('dve', lambda e: e.tensor_copy(out=Z[:cw, :], in_=P[:cw, :]), [P], [Z])
            k.dma('sp', zT[cc * 128:cc * 128 + cw, tb * TB:(tb + 1) * TB], Z[:cw, :], rbuf=Z)
            i += 1
    print("A n_inst", k.n_inst)
    return k.finish()

D = 2048; KC = 16; FF = 8192; FC = 64
def build_C(ntok, last, TB=512):
    k = KB()
    xT = k.dram("xT", [D, ntok], F32, "ExternalInput")
    zgT = k.dram("zgT", [3 * D, ntok], F32, "ExternalInput")
    gb = k.dram("gb", [128, 48], F32, "ExternalInput")
    yT = k.dram("yT", [D, ntok], F32, "ExternalInput")
    w_up = k.dram("w_up", [D, D], F32, "ExternalInput")
    w_out = k.dram("w_out", [D, D], F32, "ExternalInput")
    nm = k.dram("nm", [128, KC], F32, "ExternalInput")
    w1 = k.dram("w1", [D, FF], F32, "ExternalInput")
    w2 = k.dram("w2", [FF, D], F32, "ExternalInput")
    fn = k.dram("fn", [128, KC], F32, "ExternalInput")
    gluw = k.dram("gluw", [768, 768], F32, "ExternalInput")
    glub = k.dram("glub", [128, 6], F32, "ExternalInput")
    xoT = k.dram("xoT", [D, ntok], F32, "ExternalOutput")
    ntb = ntok // TB
    X = k.sb("X", [128, KC, TB])
    YH = k.sb("YH", [128, KC, TB], BF16)
    M = k.sb("M", [128, KC, TB], BF16)
    ACTB = k.sb("ACTB", [128, FC, TB], BF16)
    NWB = 4
    wb = [k.sb(f"wb{i}", [128, KC, 128], BF16) for i in range(NWB)]
    stg = [k.sb(f"stg{i}", [128, KC, 128]) for i in range(2)]
    G = [[k.sb(f"G{i}_{j}", [128, TB]) for j in range(3)] for i in range(2)]
    sq = [k.sb(f"sq{i}", [128, TB], BF16) for i in range(2)]
    tmp = [k.sb(f"tmp{i}", [128, TB]) for i in range(2)]
    mm = k.sb("mm", [128, TB])
    ones = k.sb("ones", [128, 128], BF16)
    gbt = k.sb("gbt", [128, 48]); nmt = k.sb("nmt", [128, KC]); fnt = k.sb("fnt", [128, KC]); glubt = k.sb("glubt", [128, 6])
    k.dma('sp', glubt[:], glub[:, :], wbuf=glubt)
    AF32 = ACTB[:].rearrange("p a b -> p (a b)").bitcast(F32)
    S5N = 6 * TB
    YA = k.view(ACTB, AF32[:, 0:S5N]); YG = k.view(ACTB, AF32[:, S5N:2 * S5N]); SQ = k.view(ACTB, AF32[:, 2 * S5N:3 * S5N])
    YGb = k.view(ACTB, ACTB[:].rearrange("p a b -> p (a b)")[:, 6 * S5N:7 * S5N])
    gluv = gluw.rearrange("(kc p) c -> p kc c", p=128)
    C1 = 0.7978845608028654; C2 = C1 * 0.044715
    rstd = k.sb("rstd", [128, TB])
    psn = k.ps("psn", [128, TB])
    pz = [k.ps(f"pz{i}", [128, TB]) for i in range(6)]
    k.op('pool', lambda e: e.memset(ones[:], 1.0), [], [ones])
    k.dma('sp', gbt[:], gb[:, :], wbuf=gbt); k.dma('sp', nmt[:], nm[:, :], wbuf=nmt); k.dma('sp', fnt[:], fn[:, :], wbuf=fnt)
    xv = xT.rearrange("(kc p) t -> p kc t", p=128); yv = yT.rearrange("(kc p) t -> p kc t", p=128)
    xov = xoT.rearrange("(kc p) t -> p kc t", p=128)
    wupv = w_up.rearrange("(kc p) c -> p kc c", p=128); woutv = w_out.rearrange("(kc p) c -> p kc c", p=128)
    w1v = w1.rearrange("(kc p) c -> p kc c", p=128); w2v = w2.rearrange("(fc p) c -> p fc c", p=128)
    wlist = []
    for tb_ in range(ntb):
        for fo in range(6): wlist.append((gluv[:, :, fo * 128:(fo + 1) * 128], 6))
        for fo in range(KC): wlist.append((wupv[:, :, fo * 128:(fo + 1) * 128], KC))
        for fo in range(KC): wlist.append((woutv[:, :, fo * 128:(fo + 1) * 128], KC))
        for fc in range(FC): wlist.append((w1v[:, :, fc * 128:(fc + 1) * 128], KC))
        for fo in range(KC):
            for q in range(4): wlist.append((w2v[:, q * 16:(q + 1) * 16, fo * 128:(fo + 1) * 128], KC))
    wst = {"dma": 0, "cast": 0, "taken": 0}
    def w_next():
        i = wst["taken"]
        while wst["dma"] < min(len(wlist), i + 4):
            j = wst["dma"]; src, nk = wlist[j]
            if j % 2 == 0:
                k.dma('pool', wb[j % NWB][:, 0:nk, :], src, wbuf=wb[j % NWB])
            else:
                S_ = stg[(j // 2) % 2]
                k.dma('sp', S_[:, 0:nk, :], src, wbuf=S_)
            wst["dma"] += 1
        while wst["cast"] < min(wst["dma"], i + 2):
            j = wst["cast"]; src, nk = wlist[j]
            if j % 2 == 1:
                S_ = stg[(j // 2) % 2]; W_ = wb[j % NWB]
                k.op('act', lambda e: e.copy(out=W_[:, 0:nk, :], in_=S_[:, 0:nk, :]), [S_], [W_])
            wst["cast"] += 1
        wst["taken"] += 1
        return wb[i % NWB]
    BR = [(0, 6), (6, 11), (11, 16)]
    wi = 0; pi = 0
    def rmsn(gt_tile, dst_fn):
        for kc in range(KC):
            S = sq[kc % 2]
            k.op('act', lambda e: e.activation(out=S[:], in_=X[:, kc, :], func=AF.Square), [X], [S])
            k.op('pe', lambda e: e.matmul(psn[:], ones[:], S[:], start=(kc == 0), stop=(kc == KC - 1)), [ones, S], [psn])
        k.op('act', lambda e: e.activation(out=rstd[:], in_=psn[:], func=AF.Sqrt, scale=1.0 / D, bias=1e-6), [psn], [rstd])
        k.op('dve', lambda e: e.reciprocal(out=rstd[:], in_=rstd[:]), [rstd], [rstd])
    for tb in range(ntb):
        ts = slice(tb * TB, (tb + 1) * TB)
        k.dma('sp', X[:], xv[:, :, ts], wbuf=X)
        k.dma('pool', YH[:, 6:16, :], yv[:, 6:16, ts], wbuf=YH)
        k.dma('sp', YA[:].rearrange("p (a b) -> p a b", a=6), yv[:, 0:6, ts], wbuf=YA)
        k.op('pool', lambda e: e.tensor_tensor(out=SQ[:], in0=YA[:], in1=YA[:], op=ALU.mult), [YA], [SQ])
        k.op('dve', lambda e: e.tensor_scalar(out=SQ[:], in0=SQ[:], scalar1=C2, scalar2=C1, op0=ALU.mult, op1=ALU.add), [SQ], [SQ])
        k.op('dve', lambda e: e.tensor_tensor(out=SQ[:], in0=SQ[:], in1=YA[:], op=ALU.mult), [SQ, YA], [SQ])
        k.op('act', lambda e: e.activation(out=SQ[:], in_=SQ[:], func=AF.Tanh), [SQ], [SQ])
        k.op('dve', lambda e: e.tensor_scalar(out=SQ[:], in0=SQ[:], scalar1=0.5, scalar2=0.5, op0=ALU.mult, op1=ALU.add), [SQ], [SQ])
        k.op('dve', lambda e: e.tensor_tensor(out=YG[:], in0=SQ[:], in1=YA[:], op=ALU.mult), [SQ, YA], [YG])
        k.op('pool', lambda e: e.tensor_copy(out=YGb[:], in_=YG[:]), [YG], [YGb])
        for fo in range(6):
            W = w_next()
            P = pz[pi % 6]; pi += 1
            for kc in range(6):
                k.op('pe', lambda e: e.matmul(P[:], W[:, kc, :], YGb[:, kc * TB:(kc + 1) * TB], start=(kc == 0), stop=(kc == 5)), [W, YGb], [P])
            T = tmp[fo % 2]
            k.op('act', lambda e: e.activation(out=T[:], in_=P[:], func=AF.Sigmoid, bias=glubt[:, fo:fo + 1]), [P, glubt], [T])
            k.op('dve', lambda e: e.tensor_tensor(out=YH[:, fo, :], in0=YG[:, fo * TB:(fo + 1) * TB], in1=T[:], op=ALU.mult), [YG, T], [YH])
        for fo in range(KC):
            W = w_next()
            Gs = G[fo % 2]
            for br in range(3):
                r0 = br * D + fo * 128
                k.dma('sp', Gs[br][:], zgT[r0:r0 + 128, ts], wbuf=Gs[br])
                k.op('act', lambda e: e.activation(out=Gs[br][:], in_=Gs[br][:], func=AF.Sigmoid, bias=gbt[:, br * 16 + fo:br * 16 + fo + 1]), [Gs[br], gbt], [Gs[br]])
            Ps = []
            for br in range(3):
                P = pz[pi % 6]; pi += 1; Ps.append(P)
                a, b = BR[br]
                for kc in range(a, b):
                    k.op('pe', lambda e: e.matmul(P[:], W[:, kc, :], YH[:, kc, :], start=(kc == a), stop=(kc == b - 1)), [W, YH], [P])
            k.op('dve', lambda e: e.tensor_tensor(out=mm[:], in0=Ps[0][:], in1=Gs[0][:], op=ALU.mult), [Ps[0], Gs[0]], [mm])
            T = tmp[0]
            k.op('dve', lambda e: e.tensor_tensor(out=T[:], in0=Ps[1][:], in1=Gs[1][:], op=ALU.mult), [Ps[1], Gs[1]], [T])
            k.op('dve', lambda e: e.tensor_tensor(out=mm[:], in0=mm[:], in1=T[:], op=ALU.add), [mm, T], [mm])
            T = tmp[1]
            k.op('dve', lambda e: e.tensor_tensor(out=T[:], in0=Ps[2][:], in1=Gs[2][:], op=ALU.mult), [Ps[2], Gs[2]], [T])
            k.op('dve', lambda e: e.tensor_tensor(out=M[:, fo, :], in0=mm[:], in1=T[:], op=ALU.add), [mm, T], [M])
        for fo in range(KC):
            W = w_next()
            P = pz[pi % 6]; pi += 1
            for kc in range(KC):
                k.op('pe', lambda e: e.matmul(P[:], W[:, kc, :], M[:, kc, :], start=(kc == 0), stop=(kc == KC - 1)), [W, M], [P])
            k.op('dve', lambda e: e.tensor_tensor(out=X[:, fo, :], in0=X[:, fo, :], in1=P[:], op=ALU.add), [X, P], [X])
        rmsn(nmt, None)
        for kc in range(KC):
            k.op('dve', lambda e: e.scalar_tensor_tensor(out=YH[:, kc, :], in0=X[:, kc, :], scalar=nmt[:, kc:kc + 1], in1=rstd[:], op0=ALU.mult, op1=ALU.mult), [X, nmt, rstd], [YH])
        for fc in range(FC):
            W = w_next()
            P = pz[pi % 6]; pi += 1
            for kc in range(KC):
                k.op('pe', lambda e: e.matmul(P[:], W[:, kc, :], YH[:, kc, :], start=(kc == 0), stop=(kc == KC - 1)), [W, YH], [P])
            T = tmp[fc % 2]
            k.op('act', lambda e: e.activation(out=T[:], in_=P[:], func=AF.Relu), [P], [T])
            k.op('pool', lambda e: e.tensor_tensor(out=ACTB[:, fc, :], in0=T[:], in1=T[:], op=ALU.mult), [T], [ACTB])
        for fo in range(KC):
            P = pz[pi % 6]; pi += 1
            for h in range(4):
                W = w_next()
                for f in range(16):
                    fc = h * 16 + f
                    k.op('pe', lambda e: e.matmul(P[:], W[:, f, :], ACTB[:, fc, :], start=(fc == 0), stop=(fc == FC - 1)), [W, ACTB], [P])
            k.op('dve', lambda e: e.tensor_tensor(out=X[:, fo, :], in0=X[:, fo, :], in1=P[:], op=ALU.add), [X, P], [X])
        if last:
            rmsn(fnt, None)
            for kc in range(KC):
                k.op('dve', lambda e: e.scalar_tensor_tensor(out=X[:, kc, :], in0=X[:, kc, :], scalar=fnt[:, kc:kc + 1], in1=rstd[:], op0=ALU.mult, op1=ALU.mult), [X, fnt, rstd], [X])
        k.dma('sp', xov[:, :, ts], X[:], rbuf=X)
    print("C n_inst", k.n_inst)
    return k.finish()

PI = math.pi
PI = math.pi
def emit_S5(k, ntok, banks, TB=512):
    NP = 3
    zs = k.dram("s_zs", [96, ntok], F32, "ExternalInput")
    lamre = k.dram("s_lamre", [128, NP], F32, "ExternalInput")
    lamim = k.dram("s_lamim", [128, NP], F32, "ExternalInput")
    lstep = k.dram("s_lstep", [128, NP], F32, "ExternalInput")
    bre = k.dram("s_bre", [96, NP, 128], F32, "ExternalInput")
    bim = k.dram("s_bim", [96, NP, 128], F32, "ExternalInput")
    cre = k.dram("s_cre", [128, NP, 96], F32, "ExternalInput")
    cim = k.dram("s_cim", [128, NP, 96], F32, "ExternalInput")
    dvec = k.dram("s_dvec", [96, 1], F32, "ExternalInput")
    tau = k.dram("s_tau", [128, 128], F32, "ExternalInput")
    yo = k.dram("s_yo", [96, ntok], F32, "ExternalOutput")
    nb = ntok // TB; NCH = TB // 128
    sb = lambda n, shp, dt=F32: k.sb("s_" + n, shp, dt)
    lre = sb("lre", [128, NP]); lim = sb("lim", [128, NP]); lst = sb("lst", [128, NP])
    breT = sb("breT", [96, NP, 128]); bimT = sb("bimT", [96, NP, 128])
    creT = sb("creT", [128, NP, 96]); cimT = sb("cimT", [128, NP, 96]); ncreT = sb("ncreT", [128, NP, 96]); ncimT = sb("ncimT", [128, NP, 96])
    dT = sb("dT", [96, 1]); tauT = sb("tauT", [128, 128])
    gcon = k.group("s_con"); gz = [k.group(f"s_z{i}") for i in range(2)]; gy = [k.group(f"s_y{i}") for i in range(2)]
    for t, d in ((lre, lamre), (lim, lamim), (lst, lstep), (breT, bre), (bimT, bim), (creT, cre), (cimT, cim), (dT, dvec), (tauT, tau)):
        k.dma('sp', t[:], d, wbuf=t, grp=gcon)
    k.op('dve', lambda e: e.tensor_scalar(out=ncreT[:], in0=creT[:], scalar1=-1.0, scalar2=None, op0=ALU.mult), [creT], [ncreT])
    k.op('dve', lambda e: e.tensor_scalar(out=ncimT[:], in0=cimT[:], scalar1=-1.0, scalar2=None, op0=ALU.mult), [cimT], [ncimT])
    col = lambda n: sb(n, [128, NP])
    dt = col("dt"); lr = col("lr"); al = col("al"); om = col("om"); ea = col("ea"); cw = col("cw"); sw = col("sw")
    lbr = col("lbr"); lbi = col("lbi"); den = col("den"); t1 = col("t1"); t2 = col("t2"); qre = col("qre"); qim = col("qim")
    Rre = col("Rre"); Rim = col("Rim"); ang = col("ang")
    V = lambda fn, r, w: k.op('dve', fn, r, w)
    A_ = lambda fn, r, w: k.op('act', fn, r, w)
    A_(lambda e: e.activation(out=dt[:], in_=lst[:], func=AF.Exp), [lst], [dt])
    V(lambda e: e.tensor_scalar(out=lr[:], in0=lre[:], scalar1=-1e-4, scalar2=None, op0=ALU.min), [lre], [lr])
    V(lambda e: e.tensor_tensor(out=al[:], in0=lr[:], in1=dt[:], op=ALU.mult), [lr, dt], [al])
    V(lambda e: e.tensor_tensor(out=om[:], in0=lim[:], in1=dt[:], op=ALU.mult), [lim, dt], [om])
    A_(lambda e: e.activation(out=ea[:], in_=al[:], func=AF.Exp), [al], [ea])
    def sincos(dst_s, dst_c, src_ap, srcbufs, shape_ap_fn, scale):
        pass
    angi = sb("angi", [128, NP], mybir.dt.int32); angf = sb("angf", [128, NP])
    def sin_of(dst, src, tmpb, mult, phase):
        (db, da), (sbf, sa), (tb_, ta) = dst, src, tmpb
        V(lambda e: e.tensor_scalar(out=ta, in0=sa, scalar1=mult / (2 * PI), scalar2=phase / (2 * PI), op0=ALU.mult, op1=ALU.add), [sbf], [tb_])
        V(lambda e: e.tensor_copy(out=angi[:], in_=ta), [tb_], [angi])
        V(lambda e: e.tensor_copy(out=angf[:], in_=angi[:]), [angi], [angf])
        V(lambda e: e.tensor_tensor(out=ta, in0=ta, in1=angf[:], op=ALU.subtract), [tb_, angf], [tb_])
        A_(lambda e: e.activation(out=da, in_=ta, func=AF.Sin, scale=2 * PI), [tb_], [db])
    sin_of((sw, sw[:]), (om, om[:]), (ang, ang[:]), 1.0, 0.0)
    sin_of((cw, cw[:]), (om, om[:]), (ang, ang[:]), 1.0, PI / 2)
    sin_of((Rim, Rim[:]), (om, om[:]), (ang, ang[:]), 128.0, 0.0)
    sin_of((Rre, Rre[:]), (om, om[:]), (ang, ang[:]), 128.0, PI / 2)
    V(lambda e: e.tensor_tensor(out=lbr[:], in0=ea[:], in1=cw[:], op=ALU.mult), [ea, cw], [lbr])
    V(lambda e: e.tensor_tensor(out=lbi[:], in0=ea[:], in1=sw[:], op=ALU.mult), [ea, sw], [lbi])
    V(lambda e: e.tensor_scalar(out=lbr[:], in0=lbr[:], scalar1=-1.0, scalar2=None, op0=ALU.add), [lbr], [lbr])
    V(lambda e: e.tensor_tensor(out=den[:], in0=lr[:], in1=lr[:], op=ALU.mult), [lr], [den])
    V(lambda e: e.tensor_tensor(out=t1[:], in0=lim[:], in1=lim[:], op=ALU.mult), [lim], [t1])
    V(lambda e: e.tensor_tensor(out=den[:], in0=den[:], in1=t1[:], op=ALU.add), [den, t1], [den])
    V(lambda e: e.reciprocal(out=den[:], in_=den[:]), [den], [den])
    V(lambda e: e.tensor_tensor(out=t1[:], in0=lbr[:], in1=lr[:], op=ALU.mult), [lbr, lr], [t1])
    V(lambda e: e.tensor_tensor(out=t2[:], in0=lbi[:], in1=lim[:], op=ALU.mult), [lbi, lim], [t2])
    V(lambda e: e.tensor_tensor(out=t1[:], in0=t1[:], in1=t2[:], op=ALU.add), [t1, t2], [t1])
    V(lambda e: e.tensor_tensor(out=qre[:], in0=t1[:], in1=den[:], op=ALU.mult), [t1, den], [qre])
    V(lambda e: e.tensor_tensor(out=t1[:], in0=lbi[:], in1=lr[:], op=ALU.mult), [lbi, lr], [t1])
    V(lambda e: e.tensor_tensor(out=t2[:], in0=lbr[:], in1=lim[:], op=ALU.mult), [lbr, lim], [t2])
    V(lambda e: e.tensor_tensor(out=t1[:], in0=t1[:], in1=t2[:], op=ALU.subtract), [t1, t2], [t1])
    V(lambda e: e.tensor_tensor(out=qim[:], in0=t1[:], in1=den[:], op=ALU.mult), [t1, den], [qim])
    COS = [sb(f"COS{p}", [128, 128]) for p in range(NP)]; SIN = [sb(f"SIN{p}", [128, 128]) for p in range(NP)]
    ERE = [sb(f"ERE{p}", [128, 128]) for p in range(NP)]; EIM = [sb(f"EIM{p}", [128, 128]) for p in range(NP)]
    DEC = [sb(f"DEC{p}", [128, 128]) for p in range(NP)]
    th = sb("th", [128, 128]); thi = sb("thi", [128, 128], mybir.dt.int32); thf = sb("thf", [128, 128])
    for p in range(NP):
        for dstT, ph in ((SIN[p], 0.0), (COS[p], PI / 2)):
            V(lambda e: e.tensor_scalar(out=th[:], in0=tauT[:], scalar1=om[:, p:p + 1], scalar2=1.0 / (2 * PI), op0=ALU.mult, op1=ALU.mult), [tauT, om], [th])
            V(lambda e: e.tensor_scalar(out=th[:], in0=th[:], scalar1=ph / (2 * PI), scalar2=None, op0=ALU.add), [th], [th])
            V(lambda e: e.tensor_copy(out=thi[:], in_=th[:]), [th], [thi])
            V(lambda e: e.tensor_copy(out=thf[:], in_=thi[:]), [thi], [thf])
            V(lambda e: e.tensor_tensor(out=th[:], in0=th[:], in1=thf[:], op=ALU.subtract), [th, thf], [th])
            A_(lambda e: e.activation(out=dstT[:], in_=th[:], func=AF.Sin, scale=2 * PI), [th], [dstT])
        V(lambda e: e.tensor_scalar(out=th[:], in0=SIN[p][:], scalar1=qim[:, p:p + 1], scalar2=None, op0=ALU.mult), [SIN[p], qim], [th])
        V(lambda e: e.scalar_tensor_tensor(out=ERE[p][:], in0=COS[p][:], scalar=qre[:, p:p + 1], in1=th[:], op0=ALU.mult, op1=ALU.add), [COS[p], qre, th], [ERE[p]])
        V(lambda e: e.tensor_scalar(out=th[:], in0=SIN[p][:], scalar1=qre[:, p:p + 1], scalar2=None, op0=ALU.mult), [SIN[p], qre], [th])
        V(lambda e: e.scalar_tensor_tensor(out=EIM[p][:], in0=COS[p][:], scalar=qim[:, p:p + 1], in1=th[:], op0=ALU.mult, op1=ALU.subtract), [COS[p], qim, th], [EIM[p]])
        V(lambda e: e.tensor_scalar(out=DEC[p][:], in0=tauT[:], scalar1=0.0, scalar2=ea[:, p:p + 1], op0=ALU.mult, op1=ALU.add), [tauT, ea], [DEC[p]])
    zb = [sb(f"zb{i}", [96, TB]) for i in range(2)]
    PX = [[banks[0], banks[1]], [banks[0], banks[1]]]
    PY = [k.view(banks[2], banks[2][0:96, :]), k.view(banks[3], banks[3][0:96, :])]
    rr = [[sb(f"rr{p}_{c}", [128, TB]) for c in range(2)] for p in range(NP)]
    tm = [sb(f"tm{i}", [128, TB]) for i in range(2)]
    PP = [[sb(f"PP{i}_{j}", [128, TB]) for j in range(4)] for i in range(2)]
    ini = [[sb(f"ini{p}_{c}", [128, 1]) for c in range(2)] for p in range(NP)]
    it = [sb(f"it{p}", [128, 1]) for p in range(NP)]
    yout = [sb(f"yout{i}", [96, TB]) for i in range(2)]
    for p in range(NP):
        for c in range(2):
            k.op('pool', lambda e: e.memset(ini[p][c][:], 0.0), [], [ini[p][c]]); yield
    bc = lambda T: T[:].unsqueeze(1).to_broadcast([128, NCH, 128])
    v3 = lambda ap: ap.rearrange("p (c t) -> p c t", t=128)
    k.dma('sp', zb[0][:], zs[:, 0:TB], wbuf=zb[0], grp=gz[0]); yield
    for b in range(nb):
        Z = zb[b % 2]; ts = slice(b * TB, (b + 1) * TB)
        if b + 1 < nb:
            k.dma('sp', zb[(b + 1) % 2][:], zs[:, (b + 1) * TB:(b + 2) * TB], wbuf=zb[(b + 1) % 2], grp=gz[(b + 1) % 2]); yield
        PYb = PY[b % 2]
        for p in range(NP):
            Xr, Xi = PX[p % 2]
            k.op('pe', lambda e: e.matmul(Xr[:], breT[:, p, :], Z[:], start=True, stop=True), [breT, Z], [Xr]); yield
            k.op('pe', lambda e: e.matmul(Xi[:], bimT[:, p, :], Z[:], start=True, stop=True), [bimT, Z], [Xi]); yield
            Rr, Ri = rr[p]
            T0, T1 = tm
            V(lambda e: e.tensor_tensor(out=v3(T0[:]), in0=v3(Xr[:]), in1=bc(ERE[p]), op=ALU.mult), [Xr, ERE[p]], [T0]); yield
            V(lambda e: e.tensor_tensor(out=v3(T1[:]), in0=v3(Xi[:]), in1=bc(EIM[p]), op=ALU.mult), [Xi, EIM[p]], [T1]); yield
            k.op('pool', lambda e: e.tensor_tensor(out=Rr[:], in0=T0[:], in1=T1[:], op=ALU.subtract), [T0, T1], [Rr]); yield
            V(lambda e: e.tensor_tensor(out=v3(T0[:]), in0=v3(Xi[:]), in1=bc(ERE[p]), op=ALU.mult), [Xi, ERE[p]], [T0]); yield
            V(lambda e: e.tensor_tensor(out=v3(T1[:]), in0=v3(Xr[:]), in1=bc(EIM[p]), op=ALU.mult), [Xr, EIM[p]], [T1]); yield
            k.op('pool', lambda e: e.tensor_tensor(out=Ri[:], in0=T0[:], in1=T1[:], op=ALU.add), [T0, T1], [Ri]); yield
            for c in range(NCH):
                cs = slice(c * 128, (c + 1) * 128)
                Ir, Ii = ini[p]
                V(lambda e: e.tensor_tensor_scan(out=Rr[:, cs], data0=DEC[p][:], data1=Rr[:, cs], initial=Ir[:], op0=ALU.mult, op1=ALU.add), [DEC[p], Rr, Ir], [Rr]); yield
                V(lambda e: e.tensor_tensor_scan(out=Ri[:, cs], data0=DEC[p][:], data1=Ri[:, cs], initial=Ii[:], op0=ALU.mult, op1=ALU.add), [DEC[p], Ri, Ii], [Ri]); yield
                lastr = Rr[:, c * 128 + 127:c * 128 + 128]; lasti = Ri[:, c * 128 + 127:c * 128 + 128]
                V(lambda e: e.tensor_tensor(out=it[p][:], in0=lasti, in1=Rim[:, p:p + 1], op=ALU.mult), [Ri, Rim], [it[p]]); yield
                V(lambda e: e.scalar_tensor_tensor(out=Ir[:], in0=lastr, scalar=Rre[:, p:p + 1], in1=it[p][:], op0=ALU.mult, op1=ALU.subtract), [Rr, Rre, it[p]], [Ir]); yield
                V(lambda e: e.tensor_tensor(out=it[p][:], in0=lastr, in1=Rim[:, p:p + 1], op=ALU.mult), [Rr, Rim], [it[p]]); yield
                V(lambda e: e.scalar_tensor_tensor(out=Ii[:], in0=lasti, scalar=Rre[:, p:p + 1], in1=it[p][:], op0=ALU.mult, op1=ALU.add), [Ri, Rre, it[p]], [Ii]); yield
            P1, P2, P3, P4 = PP[p % 2]
            k.op('pool', lambda e: e.tensor_tensor(out=v3(P1[:]), in0=v3(Rr[:]), in1=bc(COS[p]), op=ALU.mult), [Rr, COS[p]], [P1]); yield
            k.op('pool', lambda e: e.tensor_tensor(out=v3(P2[:]), in0=v3(Ri[:]), in1=bc(SIN[p]), op=ALU.mult), [Ri, SIN[p]], [P2]); yield
            k.op('pool', lambda e: e.tensor_tensor(out=v3(P3[:]), in0=v3(Rr[:]), in1=bc(SIN[p]), op=ALU.mult), [Rr, SIN[p]], [P3]); yield
            k.op('pool', lambda e: e.tensor_tensor(out=v3(P4[:]), in0=v3(Ri[:]), in1=bc(COS[p]), op=ALU.mult), [Ri, COS[p]], [P4]); yield
            k.op('pe', lambda e: e.matmul(PYb[:], creT[:, p, :], P1[:], start=(p == 0), stop=False), [creT, P1], [PYb]); yield
            k.op('pe', lambda e: e.matmul(PYb[:], ncreT[:, p, :], P2[:], start=False, stop=False), [ncreT, P2], [PYb]); yield
            k.op('pe', lambda e: e.matmul(PYb[:], ncimT[:, p, :], P3[:], start=False, stop=False), [ncimT, P3], [PYb]); yield
            k.op('pe', lambda e: e.matmul(PYb[:], ncimT[:, p, :], P4[:], start=False, stop=(p == NP - 1)), [ncimT, P4], [PYb]); yield
        YO = yout[b % 2]
        V(lambda e: e.scalar_tensor_tensor(out=YO[:], in0=Z[:], scalar=dT[:, 0:1], in1=PYb[:], op0=ALU.mult, op1=ALU.add), [Z, dT, PYb], [YO]); yield
        k.dma('sp', yo[:, ts], YO[:], rbuf=YO, grp=gy[b % 2]); yield

def s5_layout(core, zT_s5, lam_re, lam_im, log_step, b_re, b_im, c_re, c_im, d):
    import numpy as np
    g0 = 6 * core
    f = np.float32
    lamre = np.zeros((128, 3), f); lamim = np.zeros((128, 3), f); lstep = np.zeros((128, 3), f)
    bre = np.zeros((96, 3, 128), f); bim = np.zeros((96, 3, 128), f); cre = np.zeros((128, 3, 96), f); cim = np.zeros((128, 3, 96), f)
    for p in range(3):
        for gl in range(2):
            g = g0 + 2 * p + gl; gi = 2 * p + gl
            lamre[gl * 64:(gl + 1) * 64, p] = lam_re[g]; lamim[gl * 64:(gl + 1) * 64, p] = lam_im[g]; lstep[gl * 64:(gl + 1) * 64, p] = log_step[g]
            bre[gi * 16:(gi + 1) * 16, p, gl * 64:(gl + 1) * 64] = b_re[g].T
            bim[gi * 16:(gi + 1) * 16, p, gl * 64:(gl + 1) * 64] = b_im[g].T
            cre[gl * 64:(gl + 1) * 64, p, gi * 16:(gi + 1) * 16] = c_re[g].T
            cim[gl * 64:(gl + 1) * 64, p, gi * 16:(gi + 1) * 16] = c_im[g].T
    return {"s_" + n_: v_ for n_, v_ in {"zs": np.ascontiguousarray(zT_s5[g0 * 16:(g0 + 6) * 16]), "lamre": lamre, "lamim": lamim, "lstep": lstep, "bre": bre, "bim": bim,
            "cre": cre, "cim": cim, "dvec": np.ascontiguousarray(d[g0 * 16:(g0 + 6) * 16].reshape(96, 1)),
            "tau": np.broadcast_to(np.arange(128, dtype=f), (128, 128)).copy()}.items()}

def gla_consts():
    import numpy as np
    s = np.arange(128)[:, None]; t = np.arange(128)[None, :]
    same = (s // 64) == (t // 64)
    f = np.float32
    return {"tri_le": (same & (s <= t)).astype(f), "tri_gt": (same & (s > t)).astype(f),
            "chind": np.stack([(np.arange(128) < 64), (np.arange(128) >= 64)], 1).astype(f)}
def emit_GLA(k, ntok, banks, pfx="g_"):
    D_ = lambda n, shp, kind="ExternalInput": k.dram(pfx + n, shp, F32, kind)
    qT = D_("qT", [64, ntok]); kT = D_("kT", [64, ntok]); ktok = D_("ktok", [ntok, 64]); vtok = D_("vtok", [ntok, 128]); gtok = D_("gtok", [ntok, 128])
    ainT = D_("ainT", [16, ntok]); alora = D_("alora", [16, 64]); abias = D_("abias", [128, 64]); normg = D_("normg", [128, 128])
    tri_le = D_("tri_le", [128, 128]); tri_gt = D_("tri_gt", [128, 128]); chind = D_("chind", [128, 2])
    otok = D_("otok", [ntok, 128], "ExternalOutput")
    sb = lambda n, shp, dt=F32: k.sb(pfx + n, shp, dt)
    aloraT = sb("aloraT", [16, 64]); abiasT = sb("abiasT", [128, 64]); normgT = sb("normgT", [128, 128])
    TLE = sb("TLE", [128, 128]); TGT = sb("TGT", [128, 128]); CHI = sb("CHI", [128, 2])
    gcon = k.group(pfx + "con"); gl = [k.group(pfx + f"l{i}") for i in range(3)]; go = [k.group(pfx + f"o{i}") for i in range(2)]
    for t, d in ((aloraT, alora), (abiasT, abias), (normgT, normg), (TLE, tri_le), (TGT, tri_gt), (CHI, chind)):
        k.dma('sp', t[:], d[:, :], wbuf=t, grp=gcon)
    NB = 3
    QT = [sb(f"QT{i}", [64, 128]) for i in range(NB)]; KT = [sb(f"KT{i}", [64, 128]) for i in range(NB)]
    KK = [sb(f"KK{i}", [128, 64]) for i in range(NB)]; VV = [sb(f"VV{i}", [128, 128]) for i in range(NB)]; GG = [sb(f"GG{i}", [128, 128]) for i in range(NB)]
    AIN = [sb(f"AIN{i}", [16, 128]) for i in range(NB)]
    xb = sb("xb", [128, 64]); ax = sb("ax", [128, 64]); la = sb("la", [128, 64]); mn = sb("mn", [128, 64])
    EP = sb("EP", [64, 128]); EN = sb("EN", [64, 128]); ER = sb("ER", [128, 64])
    QF = sb("QF", [64, 128]); KF = sb("KF", [64, 128]); KH_ = [sb(f"KH{i}", [128, 64]) for i in range(2)]
    Q0_ = [sb(f"Q0{i}", [64, 128]) for i in range(2)]; Q1_ = [sb(f"Q1{i}", [64, 128]) for i in range(2)]
    VVr_ = [sb(f"VVr{i}", [128, 128]) for i in range(2)]
    NM_ = [sb(f"NM{i}", [128, 128]) for i in range(2)]; EB_ = [sb(f"EB{i}", [64, 2]) for i in range(2)]
    S = [sb(f"S{i}", [64, 128]) for i in range(3)]
    ss = sb("ss", [128, 1]); junk = sb("junk", [128, 128]); SG = sb("SG", [128, 128]); O1 = sb("O1", [128, 128])
    OO = [sb(f"OO{i}", [128, 128]) for i in range(2)]
    bankA = [banks[0]]; bankB = [banks[1]]; bankC = [banks[2]]
    def views(i):
        A, B, C = bankA[i], bankB[i], bankC[i]
        return dict(px=k.view(A, A[:, 0:64]), pc=k.view(A, A[:, 64:128]), pr=k.view(A, A[:, 128:192]),
                    pbl=k.view(A, A[0:64, 192:194]), pcT=k.view(A, A[0:64, 256:384]), pN=k.view(A, A[:, 384:512]),
                    pO=k.view(B, B[:, 128:256]), pS0=k.view(C, C[0:64, 0:128]), pS1=k.view(C, C[0:64, 128:256]))
    PV = [views(0), views(0)]
    V = lambda fn, r, w: k.op('dve', fn, r, w)
    A_ = lambda fn, r, w: k.op('act', fn, r, w)
    P_ = lambda fn, r, w: k.op('pool', fn, r, w)
    T_ = lambda fn, r, w: k.op('pe', fn, r, w)
    P_(lambda e: e.memset(S[0][:], 0.0), [], [S[0]])
    for t_ in (*Q0_, *Q1_):
        P_(lambda e: e.memset(t_[:], 0.0), [], [t_])
    si = 0
    def load(ti):
        b = ti % NB; r0 = ti * 128
        k.dma('sp', QT[b][:], qT[:, r0:r0 + 128], wbuf=QT[b], grp=gl[b]); k.dma('sp', KT[b][:], kT[:, r0:r0 + 128], wbuf=KT[b], grp=gl[b])
        k.dma('sp', KK[b][:], ktok[r0:r0 + 128, :], wbuf=KK[b], grp=gl[b]); k.dma('sp', VV[b][:], vtok[r0:r0 + 128, :], wbuf=VV[b], grp=gl[b])
        k.dma('sp', GG[b][:], gtok[r0:r0 + 128, :], wbuf=GG[b], grp=gl[b]); k.dma('sp', AIN[b][:], ainT[:, r0:r0 + 128], wbuf=AIN[b], grp=gl[b])
    if USE_R:
        k.mark_r(QF, KF, *KH_, *VVr_, *Q0_, *Q1_, *NM_, *S)
    ntiles = ntok // 128
    def prep(ti):
        b = ti % NB; r0 = ti * 128; pv = PV[0]
        sl = ti % 2; NM = NM_[sl]; Q0 = Q0_[sl]; Q1 = Q1_[sl]; KH = KH_[sl]; EB = EB_[sl]; VVr = VVr_[sl]
        px, pc, pr, pbl, pcT, pN, pO, pS0, pS1 = (pv[n] for n in ("px", "pc", "pr", "pbl", "pcT", "pN", "pO", "pS0", "pS1"))
        if ti + 1 < ntiles: load(ti + 1)
        T_(lambda e: e.matmul(px[:], AIN[b][:], aloraT[:], start=True, stop=True), [AIN[b], aloraT], [px]); yield
        V(lambda e: e.tensor_tensor(out=xb[:], in0=px[:], in1=abiasT[:], op=ALU.add), [px, abiasT], [xb]); yield
        A_(lambda e: e.activation(out=ax[:], in_=xb[:], func=AF.Abs), [xb], [ax]); yield
        A_(lambda e: e.activation(out=ax[:], in_=ax[:], func=AF.Exp, scale=-1.0), [ax], [ax]); yield
        A_(lambda e: e.activation(out=ax[:], in_=ax[:], func=AF.Ln, bias=1.0), [ax], [ax]); yield
        V(lambda e: e.tensor_scalar(out=mn[:], in0=xb[:], scalar1=0.0, scalar2=None, op0=ALU.min), [xb], [mn]); yield
        V(lambda e: e.tensor_tensor(out=la[:], in0=mn[:], in1=ax[:], op=ALU.subtract), [mn, ax], [la]); yield
        T_(lambda e: e.matmul(pcT[:], la[:], TLE[:], start=True, stop=True), [la, TLE], [pcT]); yield
        T_(lambda e: e.matmul(pr[:], TGT[:], la[:], start=True, stop=True), [la, TGT], [pr]); yield
        T_(lambda e: e.matmul(pbl[:], la[:], CHI[:], start=True, stop=True), [la, CHI], [pbl]); yield
        A_(lambda e: e.activation(out=EP[:], in_=pcT[:], func=AF.Exp, scale=1.0 / 16, bias=math.log(0.125)), [pcT], [EP]); yield
        A_(lambda e: e.activation(out=EN[:], in_=pcT[:], func=AF.Exp, scale=-1.0 / 16), [pcT], [EN]); yield
        A_(lambda e: e.activation(out=ER[:], in_=pr[:], func=AF.Exp, scale=1.0 / 16), [pr], [ER]); yield
        A_(lambda e: e.activation(out=EB[:], in_=pbl[:], func=AF.Exp, scale=1.0 / 16), [pbl], [EB]); yield
        V(lambda e: e.tensor_tensor(out=QF[:], in0=QT[b][:], in1=EP[:], op=ALU.mult), [QT[b], EP], [QF]); yield
        V(lambda e: e.tensor_tensor(out=KF[:], in0=KT[b][:], in1=EN[:], op=ALU.mult), [KT[b], EN], [KF]); yield
        V(lambda e: e.tensor_tensor(out=KH[:], in0=KK[b][:], in1=ER[:], op=ALU.mult), [KK[b], ER], [KH]); yield
        P_(lambda e: e.tensor_copy(out=Q0[:, 0:64], in_=QF[:, 0:64]), [QF], [Q0]); yield
        P_(lambda e: e.tensor_copy(out=Q1[:, 64:128], in_=QF[:, 64:128]), [QF], [Q1]); yield
        T_(lambda e: e.matmul(pN[:], KF[:], QF[:], start=True, stop=True), [KF, QF], [pN]); yield
        V(lambda e: e.tensor_tensor(out=NM[:], in0=pN[:], in1=TLE[:], op=ALU.mult), [pN, TLE], [NM]); yield
        P_(lambda e: e.tensor_copy(out=VVr[:], in_=VV[b][:]), [VV[b]], [VVr]); yield
    def chain(ti):
        b = ti % NB; r0 = ti * 128; pv = PV[0]
        sl = ti % 2; NM = NM_[sl]; Q0 = Q0_[sl]; Q1 = Q1_[sl]; KH = KH_[sl]; EB = EB_[sl]; VVr = VVr_[sl]
        px, pc, pr, pbl, pcT, pN, pO, pS0, pS1 = (pv[n] for n in ("px", "pc", "pr", "pbl", "pcT", "pN", "pO", "pS0", "pS1"))
        S0 = S[(2 * ti) % 3]; S1 = S[(2 * ti + 1) % 3]; S2 = S[(2 * ti + 2) % 3]
        T_(lambda e: e.matmul(pO[:], NM[:], VVr[:], start=True, stop=False), [NM, VVr], [pO]); yield
        T_(lambda e: e.matmul(pO[:], Q0[:], S0[:], start=False, stop=False), [Q0, S0], [pO]); yield
        T_(lambda e: e.matmul(pS0[:], KH[0:64, :], VVr[0:64, :], start=True, stop=True), [KH, VVr], [pS0]); yield
        V(lambda e: e.scalar_tensor_tensor(out=S1[:], in0=S0[:], scalar=EB[:, 0:1], in1=pS0[:], op0=ALU.mult, op1=ALU.add), [S0, EB, pS0], [S1]); yield
        T_(lambda e: e.matmul(pO[:], Q1[:], S1[:], start=False, stop=True), [Q1, S1], [pO]); yield
        T_(lambda e: e.matmul(pS1[:], KH[64:128, :], VVr[64:128, :], start=True, stop=True), [KH, VVr], [pS1]); yield
        V(lambda e: e.scalar_tensor_tensor(out=S2[:], in0=S1[:], scalar=EB[:, 1:2], in1=pS1[:], op0=ALU.mult, op1=ALU.add), [S1, EB, pS1], [S2]); yield
        A_(lambda e: e.activation(out=junk[:], in_=pO[:], func=AF.Square, accum_out=ss[:]), [pO], [junk, ss]); yield
        A_(lambda e: e.activation(out=ss[:], in_=ss[:], func=AF.Sqrt, scale=1.0 / 128, bias=1e-6), [ss], [ss]); yield
        V(lambda e: e.reciprocal(out=ss[:], in_=ss[:]), [ss], [ss]); yield
        A_(lambda e: e.activation(out=SG[:], in_=GG[b][:], func=AF.Silu), [GG[b]], [SG]); yield
        V(lambda e: e.scalar_tensor_tensor(out=O1[:], in0=pO[:], scalar=ss[:, 0:1], in1=normgT[:], op0=ALU.mult, op1=ALU.mult), [pO, ss, normgT], [O1]); yield
        OB = OO[ti % 2]
        V(lambda e: e.tensor_tensor(out=OB[:], in0=O1[:], in1=SG[:], op=ALU.mult), [O1, SG], [OB]); yield
        k.dma('sp', otok[r0:r0 + 128, :], OB[:], rbuf=OB, grp=go[ti % 2]); yield

    load(0)
    yield from prep(0)
    for ti in range(ntiles):
        gc = chain(ti); gp = prep(ti + 1) if ti + 1 < ntiles else None
        while gc is not None or gp is not None:
            if gp is not None:
                try:
                    for _ in range(2):
                        next(gp); yield
                except StopIteration:
                    gp = None
            if gc is not None:
                try:
                    next(gc); yield
                except StopIteration:
                    gc = None
def gla_layout(h, z_gla, alpha_lora, alpha_bias, norm_g):
    import numpy as np
    f = np.float32; ntok = z_gla.shape[0]
    d = dict(gla_consts())
    if h is None:
        z = lambda *s: np.zeros(s, f)
        d.update(qT=z(64, ntok), kT=z(64, ntok), ktok=z(ntok, 64), vtok=z(ntok, 128), gtok=z(ntok, 128), ainT=z(16, ntok), alora=z(16, 64), abias=z(128, 64), normg=z(128, 128))
    else:
        q = z_gla[:, h * 64:(h + 1) * 64]; kk = z_gla[:, 320 + h * 64:320 + (h + 1) * 64]
        v = z_gla[:, 640 + h * 128:640 + (h + 1) * 128]; g = z_gla[:, 1280 + h * 128:1280 + (h + 1) * 128]; a = z_gla[:, 1920:1936]
        c = np.ascontiguousarray
        d.update(qT=c(q.T), kT=c(kk.T), ktok=c(kk), vtok=c(v), gtok=c(g), ainT=c(a.T), alora=c(alpha_lora[:, h * 64:(h + 1) * 64]),
                 abias=c(np.broadcast_to(alpha_bias[h * 64:(h + 1) * 64], (128, 64))), normg=c(np.broadcast_to(norm_g[h * 128:(h + 1) * 128], (128, 128))))
    return {"g_" + n: v for n, v in d.items()}

RW_SHARED = ("loraT", "mu_w", "mu_a", "mu_g", "tri_le", "tri_gt", "mask4", "ident", "chind", "vresT", "mu_v")
def rw_consts():
    import numpy as np
    s = np.arange(128)[:, None]; t = np.arange(128)[None, :]
    same = (s // 64) == (t // 64)
    f = np.float32
    tle = (same & (s <= t)).astype(f); tlt = (same & (s < t)).astype(f); tgt = (same & (s > t)).astype(f)
    return {"tri_le": tle, "tri_gt": tgt, "mask4": np.concatenate([tlt, tle, tlt, tle], 1),
            "ident": np.eye(128, dtype=f), "chind": np.stack([(np.arange(128) < 64), (np.arange(128) >= 64)], 1).astype(f)}

NLEV = 5
USE_R = True
def emit_RW(k, ntok, pfx, has_vres, banks, shared):
    def D_(n, shp, kind="ExternalInput"):
        if n in RW_SHARED:
            if n not in shared: shared[n] = k.dram("rs_" + n, shp, F32, kind)
            return shared[n]
        return k.dram(pfx + n, shp, F32, kind)
    rkv = D_("rkv", [ntok + 1, 192]); mu_rkv = D_("mu_rkv", [128, 192])
    loraT = D_("loraT", [480, ntok + 1]); mu_w = D_("mu_w", [96, 1]); mu_a = D_("mu_a", [128, 1]); mu_g = D_("mu_g", [128, 2])
    w_lora = D_("w_lora", [96, 64]); a_lora = D_("a_lora", [128, 64]); g_lora = D_("g_lora", [128, 2, 64])
    bc5 = D_("bc5", [128, 7, 64])
    tri_le = D_("tri_le", [128, 128]); tri_gt = D_("tri_gt", [128, 128]); mask4 = D_("mask4", [128, 512]); ident = D_("ident", [128, 128]); chind = D_("chind", [128, 2])
    if has_vres:
        vfirst = D_("vfirst", [ntok, 64]); vresT = D_("vresT", [64, ntok + 1]); mu_v = D_("mu_v", [64, 1]); vres_b = D_("vres_b", [64, 64]); vbias = D_("vbias", [128, 64])
    ytok = D_("ytok", [ntok, 64], "ExternalOutput")
    if not has_vres:
        vout = D_("vout", [ntok, 64], "ExternalOutput")
    sb = lambda n, shp, dt=F32: k.sb(pfx + n, shp, dt)
    MU = sb("MU", [128, 192]); MUW = sb("MUW", [96, 1]); MUA = sb("MUA", [128, 1]); MUG = sb("MUG", [128, 2])
    WL = sb("WL", [96, 64]); AL = sb("AL", [128, 64]); GL = sb("GL", [128, 2, 64]); BC = sb("BC", [128, 7, 64])
    TLE = sb("TLE", [128, 128]); TGT = sb("TGT", [128, 128]); M4 = sb("M4", [128, 512]); ID = sb("ID", [128, 128]); CHI = sb("CHI", [128, 2])
    loads = [(MU, mu_rkv), (MUW, mu_w), (MUA, mu_a), (MUG, mu_g), (WL, w_lora), (AL, a_lora), (GL, g_lora), (BC, bc5), (TLE, tri_le), (TGT, tri_gt), (M4, mask4), (ID, ident), (CHI, chind)]
    if has_vres:
        MUV = sb("MUV", [64, 1]); VB = sb("VB", [64, 64]); VBI = sb("VBI", [128, 64])
        loads += [(MUV, mu_v), (VB, vres_b), (VBI, vbias)]
    gcon = k.group(pfx + "con"); gl = [k.group(pfx + f"l{i}") for i in range(2)]; go = [k.group(pfx + f"o{i}") for i in range(2)]
    for t, d in loads:
        k.dma('sp', t[:], d, wbuf=t, grp=gcon)
    W0, A0, KKW, KA, RK, LNW, LNB = (BC[:, i, :] for i in range(7))
    if USE_R:
        cp = []
        for nm_, t_ in (("WL", WL), ("AL", AL), ("GL", GL), ("TLE", TLE), ("TGT", TGT), ("ID", ID), ("CHI", CHI)) + ((("VB", VB),) if has_vres else ()):
            t2 = sb(nm_ + "r", list(t_.t.shape)); k.mark_r(t2)
            k.op('dve', lambda e: e.tensor_copy(out=t2[:], in_=t_[:]), [t_], [t2]); cp.append(t2)
        if has_vres: WLr, ALr, GLr, TLEr, TGTr, IDr, CHIr, VBr = cp
        else: WLr, ALr, GLr, TLEr, TGTr, IDr, CHIr = cp
    else:
        WLr, ALr, GLr, TLEr, TGTr, IDr, CHIr = WL, AL, GL, TLE, TGT, ID, CHI
        if has_vres: VBr = VB
    NB = 2
    CUR = [sb(f"CUR{i}", [128, 192]) for i in range(NB)]; PRV = [sb(f"PRV{i}", [128, 192]) for i in range(NB)]
    LWc = [sb(f"LWc{i}", [96, 129]) for i in range(NB)]; LAc = [sb(f"LAc{i}", [128, 129]) for i in range(NB)]; LGc = [sb(f"LGc{i}", [128, 2, 129]) for i in range(NB)]
    if has_vres:
        VF = [sb(f"VF{i}", [128, 64]) for i in range(NB)]; VRc = [sb(f"VRc{i}", [64, 129]) for i in range(NB)]
        vS = sb("vS", [64, 128]); vD = sb("vD", [64, 128]); xv = sb("xv", [128, 64])
    Z = sb("Z", [128, 192]); Dz = sb("Dz", [128, 192])
    wS = sb("wS", [96, 128]); wD = sb("wD", [96, 128]); aS = sb("aS", [128, 128]); aD = sb("aD", [128, 128]); gS = sb("gS", [128, 2, 128]); gD = sb("gD", [128, 2, 128])
    xw = sb("xw", [128, 64]); t64 = [sb(f"t64_{i}", [128, 64]) for i in range(6)]
    ew = sb("ew", [128, 64]); asg = sb("asg", [128, 64]); gg_ = [sb(f"gg{i}", [128, 64]) for i in range(3)]; VP_ = [sb(f"VP{i}", [128, 64]) for i in range(3)]
    kk = sb("kk", [128, 64]); kkn = sb("kkn", [128, 64]); bv = sb("bv", [128, 64]); k2 = sb("k2", [128, 64])
    col = [sb(f"col{i}", [128, 1]) for i in range(6)]; junk = sb("junk", [128, 64]); junk2 = sb("junk2", [128, 64]); bsum_ = [sb(f"bsum{i}", [128, 1]) for i in range(3)]
    Em = sb("Em", [128, 64]); Ep = sb("Ep", [128, 64]); Eme = sb("Eme", [128, 64]); Er = sb("Er", [128, 64]); cex = sb("cex", [128, 64])
    T4 = sb("T4", [128, 4, 64])
    KH_ = [sb(f"KH{i}", [128, 64]) for i in range(3)]; BH_ = [sb(f"BH{i}", [128, 64]) for i in range(3)]; PC_ = [sb(f"PC{i}", [64, 2]) for i in range(3)]
    FT = sb("FT", [64, 512])
    AT0_ = [sb(f"AT0{i}", [64, 128]) for i in range(3)]; AT1_ = [sb(f"AT1{i}", [64, 128]) for i in range(3)]; RT0_ = [sb(f"RT0{i}", [64, 128]) for i in range(3)]; RT1_ = [sb(f"RT1{i}", [64, 128]) for i in range(3)]
    NM_ = [sb(f"NM{i}", [128, 512]) for i in range(3)]; Ncur = [sb(f"Ncur{i}", [128, 128]) for i in range(2)]; Lcur = [sb(f"Lcur{i}", [128, 128]) for i in range(2)]
    X_ = [sb(f"X{i}", [128, 128]) for i in range(2)]; Y = sb("Y", [128, 128]); Ninit_ = [sb(f"Ninit{i}", [128, 128]) for i in range(2)]; Linit_ = [sb(f"Linit{i}", [128, 128]) for i in range(2)]
    RHS = sb("RHS", [128, 64]); U = sb("U", [128, 64])
    S = [sb(f"S{i}", [64, 64]) for i in range(3)]
    yc = sb("yc", [128, 64]); yn = sb("yn", [128, 64]); YO = [sb(f"YO{i}", [128, 64]) for i in range(2)]; VO = [sb(f"VO{i}", [128, 64]) for i in range(2)]
    if USE_R:
        k.mark_r(*Ninit_, *Linit_, wS, aS, gS, ew, T4, FT, *AT0_, *AT1_, *RT0_, *RT1_, *NM_, *Ncur, *Lcur, *X_, Y, RHS, U, *S, *KH_, *BH_, *VP_)
        if has_vres: k.mark_r(vS)
    B0, B1, B2, B3 = banks
    vw = k.view
    pw = vw(B0, B0[:, 0:64]); pa = vw(B0, B0[:, 64:128]); pg = vw(B0, B0[:, 128:192]); pvg = vw(B0, B0[:, 192:256])
    pce = vw(B0, B0[:, 256:320]); prem = vw(B0, B0[:, 320:384]); pPC = vw(B0, B0[0:64, 0:2]); pL = vw(B0, B0[:, 384:512])
    pT = vw(B1, B1[0:64, :]); pNM = B1
    pYu = vw(B2, B2[:, 0:128]); pN2 = vw(B2, B2[:, 128:256]); pL2 = vw(B2, B2[:, 256:384]); pXu = vw(B2, B2[:, 384:512])
    pR = vw(B3, B3[:, 0:64]); pU = vw(B3, B3[:, 64:128]); pS = vw(B3, B3[0:64, 128:192]); pYo = vw(B3, B3[:, 192:256])
    V = lambda fn, r, w: k.op('dve', fn, r, w)
    A_ = lambda fn, r, w: k.op('act', fn, r, w)
    P_ = lambda fn, r, w: k.op('pool', fn, r, w)
    T_ = lambda fn, r, w: k.op('pe', fn, r, w)
    for t in (*AT0_, *AT1_, *RT0_, *RT1_, S[0]):
        P_(lambda e: e.memset(t[:], 0.0), [], [t])
    si = 0
    def load(ti):
        b = ti % NB; r0 = ti * 128
        k.dma('sp', CUR[b][:], rkv[r0 + 1:r0 + 129, :], wbuf=CUR[b], grp=gl[b]); k.dma('sp', PRV[b][:], rkv[r0:r0 + 128, :], wbuf=PRV[b], grp=gl[b])
        k.dma('sp', LWc[b][:], loraT[0:96, r0:r0 + 129], wbuf=LWc[b], grp=gl[b]); k.dma('sp', LAc[b][:], loraT[96:224, r0:r0 + 129], wbuf=LAc[b], grp=gl[b])
        k.dma('sp', LGc[b][:], loraT[224:480, r0:r0 + 129].rearrange("(c p) t -> p c t", p=128), wbuf=LGc[b], grp=gl[b])
        if has_vres:
            k.dma('sp', VF[b][:], vfirst[r0:r0 + 128, :], wbuf=VF[b], grp=gl[b]); k.dma('sp', VRc[b][:], vresT[:, r0:r0 + 129], wbuf=VRc[b], grp=gl[b])
    ntiles = ntok // 128
    def prep(ti):
        b = ti % NB; r0 = ti * 128
        sl = ti % 3; NM = NM_[sl]; X = X_[ti % 2]; AT0 = AT0_[sl]; AT1 = AT1_[sl]; RT0 = RT0_[sl]; RT1 = RT1_[sl]; KH = KH_[sl]; BH = BH_[sl]; PC = PC_[sl]; VP = VP_[sl]; gg = gg_[sl]; bsum = bsum_[sl]
        if ti + 1 < ntiles: load(ti + 1)
        V(lambda e: e.tensor_tensor(out=Dz[:], in0=PRV[b][:], in1=CUR[b][:], op=ALU.subtract), [PRV[b], CUR[b]], [Dz]); yield
        V(lambda e: e.tensor_tensor(out=Dz[:], in0=Dz[:], in1=MU[:], op=ALU.mult), [Dz, MU], [Dz]); yield
        V(lambda e: e.tensor_tensor(out=Z[:], in0=Dz[:], in1=CUR[b][:], op=ALU.add), [Dz, CUR[b]], [Z]); yield
        r_ = Z[:, 0:64]; k_ = Z[:, 64:128]; v_ = Z[:, 128:192]
        P_(lambda e: e.tensor_tensor(out=wD[:], in0=LWc[b][:, 0:128], in1=LWc[b][:, 1:129], op=ALU.subtract), [LWc[b]], [wD]); yield
        V(lambda e: e.scalar_tensor_tensor(out=wS[:], in0=wD[:], scalar=MUW[:, 0:1], in1=LWc[b][:, 1:129], op0=ALU.mult, op1=ALU.add), [wD, MUW, LWc[b]], [wS]); yield
        P_(lambda e: e.tensor_tensor(out=aD[:], in0=LAc[b][:, 0:128], in1=LAc[b][:, 1:129], op=ALU.subtract), [LAc[b]], [aD]); yield
        V(lambda e: e.scalar_tensor_tensor(out=aS[:], in0=aD[:], scalar=MUA[:, 0:1], in1=LAc[b][:, 1:129], op0=ALU.mult, op1=ALU.add), [aD, MUA, LAc[b]], [aS]); yield
        for c in range(2):
            P_(lambda e: e.tensor_tensor(out=gD[:, c, :], in0=LGc[b][:, c, 0:128], in1=LGc[b][:, c, 1:129], op=ALU.subtract), [LGc[b]], [gD]); yield
            V(lambda e: e.scalar_tensor_tensor(out=gS[:, c, :], in0=gD[:, c, :], scalar=MUG[:, c:c + 1], in1=LGc[b][:, c, 1:129], op0=ALU.mult, op1=ALU.add), [gD, MUG, LGc[b]], [gS]); yield
        A_(lambda e: e.activation(out=wS[:], in_=wS[:], func=AF.Tanh), [wS], [wS]); yield
        A_(lambda e: e.activation(out=gS[:], in_=gS[:], func=AF.Sigmoid), [gS], [gS]); yield
        T_(lambda e: e.matmul(pw[:], wS[:], WLr[:], start=True, stop=True), [wS, WLr], [pw]); yield
        T_(lambda e: e.matmul(pa[:], aS[:], ALr[:], start=True, stop=True), [aS, ALr], [pa]); yield
        T_(lambda e: e.matmul(pg[:], gS[:, 0, :], GLr[:, 0, :], start=True, stop=False), [gS, GLr], [pg]); yield
        T_(lambda e: e.matmul(pg[:], gS[:, 1, :], GLr[:, 1, :], start=False, stop=True), [gS, GLr], [pg]); yield
        if has_vres:
            P_(lambda e: e.tensor_tensor(out=vD[:], in0=VRc[b][:, 0:128], in1=VRc[b][:, 1:129], op=ALU.subtract), [VRc[b]], [vD]); yield
            V(lambda e: e.scalar_tensor_tensor(out=vS[:], in0=vD[:], scalar=MUV[:, 0:1], in1=VRc[b][:, 1:129], op0=ALU.mult, op1=ALU.add), [vD, MUV, VRc[b]], [vS]); yield
            T_(lambda e: e.matmul(pvg[:], vS[:], VBr[:], start=True, stop=True), [vS, VBr], [pvg]); yield
        V(lambda e: e.tensor_tensor(out=xw[:], in0=pw[:], in1=W0, op=ALU.add), [pw, BC], [xw]); yield
        ax, mn, ta = t64[0], t64[1], t64[2]
        A_(lambda e: e.activation(out=ax[:], in_=xw[:], func=AF.Abs), [xw], [ax]); yield
        A_(lambda e: e.activation(out=ax[:], in_=ax[:], func=AF.Exp, scale=-1.0), [ax], [ax]); yield
        A_(lambda e: e.activation(out=ax[:], in_=ax[:], func=AF.Ln, bias=1.0), [ax], [ax]); yield
        V(lambda e: e.tensor_scalar(out=mn[:], in0=xw[:], scalar1=0.0, scalar2=None, op0=ALU.min), [xw], [mn]); yield
        V(lambda e: e.tensor_tensor(out=mn[:], in0=mn[:], in1=ax[:], op=ALU.subtract), [mn, ax], [mn]); yield
        A_(lambda e: e.activation(out=ew[:], in_=mn[:], func=AF.Exp, bias=-0.5), [mn], [ew]); yield
        V(lambda e: e.tensor_tensor(out=ta[:], in0=pa[:], in1=A0, op=ALU.add), [pa, BC], [ta]); yield
        A_(lambda e: e.activation(out=asg[:], in_=ta[:], func=AF.Sigmoid), [ta], [asg]); yield
        A_(lambda e: e.copy(out=gg[:], in_=pg[:]), [pg], [gg]); yield
        if has_vres:
            V(lambda e: e.tensor_tensor(out=xv[:], in0=pvg[:], in1=VBI[:], op=ALU.add), [pvg, VBI], [xv]); yield
            A_(lambda e: e.activation(out=xv[:], in_=xv[:], func=AF.Sigmoid), [xv], [xv]); yield
            V(lambda e: e.tensor_tensor(out=VP[:], in0=VF[b][:], in1=v_, op=ALU.subtract), [VF[b], Z], [VP]); yield
            V(lambda e: e.tensor_tensor(out=VP[:], in0=VP[:], in1=xv[:], op=ALU.mult), [VP, xv], [VP]); yield
            V(lambda e: e.tensor_tensor(out=VP[:], in0=VP[:], in1=v_, op=ALU.add), [VP, Z], [VP]); yield
        else:
            P_(lambda e: e.tensor_copy(out=VP[:], in_=v_), [Z], [VP]); yield
        ssq, rn = col[0], col[1]
        V(lambda e: e.tensor_tensor(out=kk[:], in0=k_, in1=KKW, op=ALU.mult), [Z, BC], [kk]); yield
        A_(lambda e: e.activation(out=junk[:], in_=kk[:], func=AF.Square, accum_out=ssq[:]), [kk], [junk, ssq]); yield
        V(lambda e: e.tensor_scalar(out=rn[:], in0=ssq[:], scalar1=1e-24, scalar2=None, op0=ALU.max), [ssq], [rn]); yield
        A_(lambda e: e.activation(out=rn[:], in_=rn[:], func=AF.Sqrt), [rn], [rn]); yield
        V(lambda e: e.reciprocal(out=rn[:], in_=rn[:]), [rn], [rn]); yield
        V(lambda e: e.tensor_scalar(out=kkn[:], in0=kk[:], scalar1=rn[:, 0:1], scalar2=None, op0=ALU.mult), [kk, rn], [kkn]); yield
        V(lambda e: e.tensor_tensor(out=bv[:], in0=kkn[:], in1=asg[:], op=ALU.mult), [kkn, asg], [bv]); yield
        V(lambda e: e.scalar_tensor_tensor(out=k2[:], in0=asg[:], scalar=-1.0, in1=KA, op0=ALU.add, op1=ALU.mult), [asg, BC], [k2]); yield
        V(lambda e: e.scalar_tensor_tensor(out=k2[:], in0=k2[:], scalar=1.0, in1=k_, op0=ALU.add, op1=ALU.mult), [k2, Z], [k2]); yield
        tb = t64[3]
        V(lambda e: e.tensor_tensor(out=tb[:], in0=r_, in1=k2[:], op=ALU.mult), [Z, k2], [tb]); yield
        V(lambda e: e.scalar_tensor_tensor(out=junk[:], in0=tb[:], scalar=1.0, in1=RK, op0=ALU.mult, op1=ALU.mult, accum_out=bsum[:]), [tb, BC], [junk, bsum]); yield
        T_(lambda e: e.matmul(pce[:], TLEr[:], ew[:], start=True, stop=True), [TLEr, ew], [pce]); yield
        T_(lambda e: e.matmul(prem[:], TGTr[:], ew[:], start=True, stop=True), [TGTr, ew], [prem]); yield
        T_(lambda e: e.matmul(pPC[:], ew[:], CHIr[:], start=True, stop=True), [ew, CHIr], [pPC]); yield
        A_(lambda e: e.activation(out=Em[:], in_=pce[:], func=AF.Exp, scale=-1.0), [pce], [Em]); yield
        A_(lambda e: e.activation(out=Ep[:], in_=pce[:], func=AF.Exp), [pce], [Ep]); yield
        V(lambda e: e.tensor_tensor(out=cex[:], in0=pce[:], in1=ew[:], op=ALU.subtract), [pce, ew], [cex]); yield
        A_(lambda e: e.activation(out=Eme[:], in_=cex[:], func=AF.Exp, scale=-1.0), [cex], [Eme]); yield
        A_(lambda e: e.activation(out=Er[:], in_=prem[:], func=AF.Exp, scale=-1.0), [prem], [Er]); yield
        A_(lambda e: e.activation(out=PC[:], in_=pPC[:], func=AF.Exp, scale=-1.0), [pPC], [PC]); yield
        V(lambda e: e.scalar_tensor_tensor(out=T4[:, 0, :], in0=kkn[:], scalar=-1.0, in1=Eme[:], op0=ALU.mult, op1=ALU.mult), [kkn, Eme], [T4]); yield
        V(lambda e: e.tensor_tensor(out=T4[:, 1, :], in0=r_, in1=Em[:], op=ALU.mult), [Z, Em], [T4]); yield
        V(lambda e: e.tensor_tensor(out=T4[:, 2, :], in0=bv[:], in1=Ep[:], op=ALU.mult), [bv, Ep], [T4]); yield
        V(lambda e: e.tensor_tensor(out=T4[:, 3, :], in0=k2[:], in1=Ep[:], op=ALU.mult), [k2, Ep], [T4]); yield
        P_(lambda e: e.tensor_tensor(out=KH[:], in0=k2[:], in1=Er[:], op=ALU.mult), [k2, Er], [KH]); yield
        P_(lambda e: e.tensor_tensor(out=BH[:], in0=bv[:], in1=Er[:], op=ALU.mult), [bv, Er], [BH]); yield
        for j in range(4):
            T_(lambda e: e.matmul(pT[:, j * 128:(j + 1) * 128], T4[:, j, :], IDr[:], start=True, stop=True), [T4, IDr], [pT]); yield
        A_(lambda e: e.copy(out=FT[:], in_=pT[:]), [pT], [FT]); yield
        aT = FT[:, 0:128]; rT = FT[:, 128:256]; bT = FT[:, 256:384]; kT = FT[:, 384:512]
        P_(lambda e: e.tensor_copy(out=AT0[:, 0:64], in_=FT[:, 0:64]), [FT], [AT0]); yield
        P_(lambda e: e.tensor_copy(out=AT1[:, 64:128], in_=FT[:, 64:128]), [FT], [AT1]); yield
        P_(lambda e: e.tensor_copy(out=RT0[:, 0:64], in_=FT[:, 128:192]), [FT], [RT0]); yield
        P_(lambda e: e.tensor_copy(out=RT1[:, 64:128], in_=FT[:, 192:256]), [FT], [RT1]); yield
        T_(lambda e: e.matmul(pNM[:, 0:256], bT, FT[:, 0:256], start=True, stop=True), [FT], [pNM]); yield
        T_(lambda e: e.matmul(pNM[:, 256:512], kT, FT[:, 0:256], start=True, stop=True), [FT], [pNM]); yield
        T_(lambda e: e.matmul(pL[:], aT, bT, start=True, stop=True), [FT], [pL]); yield
        V(lambda e: e.tensor_tensor(out=NM[:], in0=pNM[:], in1=M4[:], op=ALU.mult), [pNM, M4], [NM]); yield
        NC_, LC_ = Ninit_[ti % 2], Linit_[ti % 2]
        P_(lambda e: e.tensor_copy(out=NC_[:], in_=NM[:, 0:128]), [NM], [NC_]); yield
        V(lambda e: e.tensor_tensor(out=LC_[:], in0=pL[:], in1=TGT[:], op=ALU.mult), [pL, TGT], [LC_]); yield
    def prepB(ti):
        X = X_[ti % 2]; NC_, LC_ = Ninit_[ti % 2], Linit_[ti % 2]
        P_(lambda e: e.tensor_tensor(out=X[:], in0=NC_[:], in1=ID[:], op=ALU.add), [NC_, ID], [X]); yield
        P_(lambda e: e.tensor_tensor(out=Y[:], in0=LC_[:], in1=ID[:], op=ALU.add), [LC_, ID], [Y]); yield
        for lev in range(NLEV):
            last = lev == NLEV - 1
            Nn, Ln = Ncur[(lev + 1) % 2], Lcur[(lev + 1) % 2]
            T_(lambda e: e.matmul(pN2[:], LC_[:], NC_[:], start=True, stop=True), [LC_, NC_], [pN2]); yield
            if not last:
                T_(lambda e: e.matmul(pL2[:], NC_[:], LC_[:], start=True, stop=True), [LC_, NC_], [pL2]); yield
            A_(lambda e: e.copy(out=Nn[:], in_=pN2[:]), [pN2], [Nn]); yield
            if not last:
                V(lambda e: e.tensor_copy(out=Ln[:], in_=pL2[:]), [pL2], [Ln]); yield
            T_(lambda e: e.matmul(pXu[:], Y[:], Nn[:], start=True, stop=True), [Y, Nn], [pXu]); yield
            if not last:
                T_(lambda e: e.matmul(pYu[:], Nn[:], Y[:], start=True, stop=True), [Y, Nn], [pYu]); yield
            V(lambda e: e.tensor_tensor(out=X[:], in0=X[:], in1=pXu[:], op=ALU.add), [X, pXu], [X]); yield
            if not last:
                V(lambda e: e.tensor_tensor(out=Y[:], in0=Y[:], in1=pYu[:], op=ALU.add), [Y, pYu], [Y]); yield
            NC_, LC_ = Nn, Ln
    def chain(ti):
        r0 = ti * 128; s1, s2, rstd = col[3], col[4], col[5]
        sl = ti % 3; NM = NM_[sl]; X = X_[ti % 2]; AT0 = AT0_[sl]; AT1 = AT1_[sl]; RT0 = RT0_[sl]; RT1 = RT1_[sl]; KH = KH_[sl]; BH = BH_[sl]; PC = PC_[sl]; VP = VP_[sl]; gg = gg_[sl]; bsum = bsum_[sl]
        Ss = [S[(2 * ti) % 3], S[(2 * ti + 1) % 3], S[(2 * ti + 2) % 3]]
        ATc = (AT0, AT1); RTc = (RT0, RT1)
        for c in range(2):
            ps_ = slice(c * 64, (c + 1) * 64)
            T_(lambda e: e.matmul(pR[:], NM[:, 256:384], VP[:], start=True, stop=False), [NM, VP], [pR]); yield
            T_(lambda e: e.matmul(pR[:], ATc[c][:], Ss[c][:], start=False, stop=True), [ATc[c], Ss[c]], [pR]); yield
            A_(lambda e: e.copy(out=RHS[ps_, :], in_=pR[ps_, :]), [pR], [RHS]); yield
            T_(lambda e: e.matmul(pU[:], X[ps_, :], RHS[ps_, :], start=True, stop=True), [X, RHS], [pU]); yield
            A_(lambda e: e.copy(out=U[ps_, :], in_=pU[ps_, :]), [pU], [U]); yield
            T_(lambda e: e.matmul(pS[:], BH[ps_, :], U[ps_, :], start=True, stop=False), [BH, U], [pS]); yield
            T_(lambda e: e.matmul(pS[:], KH[ps_, :], VP[ps_, :], start=False, stop=True), [KH, VP], [pS]); yield
            V(lambda e: e.scalar_tensor_tensor(out=Ss[c + 1][:], in0=Ss[c][:], scalar=PC[:, c:c + 1], in1=pS[:], op0=ALU.mult, op1=ALU.add), [Ss[c], PC, pS], [Ss[c + 1]]); yield
        T_(lambda e: e.matmul(pYo[:], NM[:, 128:256], U[:], start=True, stop=False), [NM, U], [pYo]); yield
        T_(lambda e: e.matmul(pYo[:], NM[:, 384:512], VP[:], start=False, stop=False), [NM, VP], [pYo]); yield
        T_(lambda e: e.matmul(pYo[:], RT0[:], Ss[0][:], start=False, stop=False), [RT0, Ss[0]], [pYo]); yield
        T_(lambda e: e.matmul(pYo[:], RT1[:], Ss[1][:], start=False, stop=True), [RT1, Ss[1]], [pYo]); yield
        A_(lambda e: e.activation(out=junk2[:], in_=pYo[:], func=AF.Copy, accum_out=s1[:]), [pYo], [junk2, s1]); yield
        V(lambda e: e.tensor_scalar(out=s1[:], in0=s1[:], scalar1=1.0 / 64, scalar2=None, op0=ALU.mult), [s1], [s1]); yield
        V(lambda e: e.tensor_scalar(out=yc[:], in0=pYo[:], scalar1=s1[:, 0:1], scalar2=None, op0=ALU.subtract), [pYo, s1], [yc]); yield
        A_(lambda e: e.activation(out=junk2[:], in_=yc[:], func=AF.Square, accum_out=s2[:]), [yc], [junk2, s2]); yield
        A_(lambda e: e.activation(out=rstd[:], in_=s2[:], func=AF.Sqrt, scale=1.0 / 64, bias=64e-5), [s2], [rstd]); yield
        V(lambda e: e.reciprocal(out=rstd[:], in_=rstd[:]), [rstd], [rstd]); yield
        V(lambda e: e.scalar_tensor_tensor(out=yn[:], in0=yc[:], scalar=rstd[:, 0:1], in1=LNW, op0=ALU.mult, op1=ALU.mult), [yc, rstd, BC], [yn]); yield
        V(lambda e: e.tensor_tensor(out=yn[:], in0=yn[:], in1=LNB, op=ALU.add), [yn, BC], [yn]); yield
        V(lambda e: e.scalar_tensor_tensor(out=yn[:], in0=VP[:], scalar=bsum[:, 0:1], in1=yn[:], op0=ALU.mult, op1=ALU.add), [VP, bsum, yn], [yn]); yield
        OB = YO[ti % 2]
        V(lambda e: e.tensor_tensor(out=OB[:], in0=yn[:], in1=gg[:], op=ALU.mult), [yn, gg], [OB]); yield
        k.dma('sp', ytok[r0:r0 + 128, :], OB[:], rbuf=OB, grp=go[ti % 2]); yield
        if not has_vres:
            OV = VO[ti % 2]
            P_(lambda e: e.tensor_copy(out=OV[:], in_=VP[:]), [VP], [OV]); yield
            k.dma('sp', vout[r0:r0 + 128, :], OV[:], rbuf=OV, grp=go[ti % 2]); yield

    def mix(parts):
        parts = [[g_, n_] for g_, n_ in parts if g_ is not None]
        while parts:
            for it_ in list(parts):
                try:
                    for _ in range(it_[1]):
                        next(it_[0]); yield
                except StopIteration:
                    parts.remove(it_)
    load(0)
    yield from prep(0)
    yield from mix([(prep(1) if ntiles > 1 else None, 3), (prepB(0), 1)])
    for ti in range(ntiles):
        yield from mix([(prep(ti + 2) if ti + 2 < ntiles else None, 3), (prepB(ti + 1) if ti + 1 < ntiles else None, 1), (chain(ti), 1)])
def rw_layout(pfx, h, z_rw, P, vfirst=None, vres_u=None):
    import numpy as np
    f = np.float32; ntok = z_rw.shape[0]; c = np.ascontiguousarray
    has_vres = vres_u is not None
    d = dict(rw_consts())
    zpad = lambda a: np.concatenate([np.zeros((1, a.shape[1]), f), a], 0)
    bc = lambda v: np.broadcast_to(v, (128, v.shape[-1]))
    if h is None:
        z = lambda *s: np.zeros(s, f)
        d.update(rkv=z(ntok + 1, 192), mu_rkv=z(128, 192), loraT=z(480, ntok + 1), mu_w=z(96, 1), mu_a=z(128, 1), mu_g=z(128, 2), w_lora=z(96, 64), a_lora=z(128, 64),
                 g_lora=z(128, 2, 64), bc5=z(128, 7, 64))
        if has_vres: d.update(vfirst=z(ntok, 64), vresT=z(64, ntok + 1), mu_v=z(64, 1), vres_b=z(64, 64), vbias=z(128, 64))
    else:
        hs = slice(h * 64, (h + 1) * 64)
        mu = P["rwkv_mu"]
        rkv = np.concatenate([z_rw[:, hs], z_rw[:, 640 + h * 64:640 + (h + 1) * 64], z_rw[:, 1280 + h * 64:1280 + (h + 1) * 64]], 1)
        mu_rkv = np.concatenate([mu[hs], mu[640 + h * 64:640 + (h + 1) * 64], mu[1280 + h * 64:1280 + (h + 1) * 64]])
        d.update(rkv=c(zpad(rkv)), mu_rkv=c(bc(mu_rkv)), loraT=c(zpad(z_rw[:, 1920:2400]).T),
                 mu_w=c(mu[1920:2016].reshape(96, 1)), mu_a=c(mu[2016:2144].reshape(128, 1)), mu_g=c(mu[2144:2400].reshape(2, 128).T),
                 w_lora=c(P["rwkv_w_lora"][:, hs]), a_lora=c(P["rwkv_a_lora"][:, hs]), g_lora=c(P["rwkv_g_lora"][:, hs].reshape(2, 128, 64).transpose(1, 0, 2)),
                 bc5=c(np.stack([bc(P[n][hs]) for n in ("rwkv_w0", "rwkv_a0", "rwkv_k_k", "rwkv_k_a", "rwkv_r_k", "rwkv_lnx_w", "rwkv_lnx_b")], 1)))
        if has_vres:
            d.update(vfirst=c(vfirst[:, hs]), vresT=c(zpad(vres_u).T), mu_v=c(P["rwkv_vres_mu"].reshape(64, 1)), vres_b=c(P["rwkv_vres_b"][:, hs]), vbias=c(bc(P["rwkv_vres_bias"][hs])))
    return {("rs_" if n in RW_SHARED else pfx) + n: v.astype(f) for n, v in d.items()}


def build_B(ntok, has_vres):
    k = KB()
    banks = [k.ps(f"bank{i}", [128, 512]) for i in range(8)]
    shared = {}
    ga = emit_RW(k, ntok, "r0_", has_vres, banks[0:4], shared)
    gb = emit_RW(k, ntok, "r1_", has_vres, banks[4:8], shared)
    alive = [ga, gb]
    while alive:
        for g_ in list(alive):
            try:
                next(g_)
            except StopIteration:
                alive.remove(g_)
    gg = emit_GLA(k, ntok, banks[0:3])
    gs = emit_S5(k, ntok, banks[3:7])
    alive = [(gg, 3), (gs, 2)]
    while alive:
        for it in list(alive):
            g_, n_ = it
            try:
                for _ in range(n_): next(g_)
            except StopIteration:
                alive.remove(it)
    return k.finish()

_PROGS = {}
def _prog(key, fn):
    if key not in _PROGS:
        _PROGS[key] = fn()
    return _PROGS[key]

def _tile16(v):
    return np.ascontiguousarray(np.asarray(v, np.float32).reshape(-1, 128).T)

def kernel(**inp):
    from concourse.bass_utils import run_bass_kernel_spmd
    f32 = np.float32
    inp = {n: np.asarray(v, f32) for n, v in inp.items()}
    x = inp["x"]; T = x.shape[1]; NCORE = 8; tpc = T // NCORE; depth = inp["w_in"].shape[0]
    cores = list(range(NCORE))
    c_ = np.ascontiguousarray
    xT = [c_(x[0, c * tpc:(c + 1) * tpc].T) for c in range(NCORE)]
    vfirst = None
    NCOLS = 11312
    for l in range(depth):
        extra = inp["rwkv_vres_a"][l - 1] if l > 0 else np.zeros((2048, 64), f32)
        wA = c_(np.concatenate([inp["w_in"][l], extra], 1))
        gA = _tile16(inp["norm_mix"][l])
        ncA = _prog(("A", tpc), lambda: build_A(tpc, NCOLS))
        res = run_bass_kernel_spmd(ncA, [{"xT": xT[c], "w": wA, "g": gA} for c in cores], core_ids=cores).results
        z = np.concatenate([res[c]["zT"].T for c in cores], 0)
        del res
        P = {n: inp[n][l] for n in ("rwkv_mu", "rwkv_w_lora", "rwkv_w0", "rwkv_a_lora", "rwkv_a0", "rwkv_g_lora", "rwkv_k_k", "rwkv_k_a", "rwkv_r_k", "rwkv_lnx_w", "rwkv_lnx_b")}
        has_vres = l > 0
        if has_vres:
            P.update(rwkv_vres_mu=inp["rwkv_vres_mu"][l - 1], rwkv_vres_b=inp["rwkv_vres_b"][l - 1], rwkv_vres_bias=inp["rwkv_vres_bias"][l - 1])
        zT_s5 = c_(z[:, :768].T); z_rw = z[:, 768:3168]; z_gla = z[:, 3168:5104]
        vres_u = c_(z[:, 11248:11312]) if has_vres else None
        ims = []
        for c in cores:
            m = {}
            m.update(s5_layout(c, zT_s5, inp["s5_lambda_re"][l], inp["s5_lambda_im"][l], inp["s5_log_step"][l], inp["s5_b_re"][l], inp["s5_b_im"][l],
                               inp["s5_c_re"][l], inp["s5_c_im"][l], inp["s5_d"][l]))
            for s in range(2):
                h = 2 * c + s
                m.update(rw_layout(f"r{s}_", h if h < 10 else None, z_rw, P, vfirst, vres_u))
            m.update(gla_layout(c if c < 5 else None, z_gla, inp["gla_alpha_lora"][l], inp["gla_alpha_bias"][l], inp["gla_norm_g"][l]))
            ims.append(m)
        ncB = _prog(("B", T, has_vres), lambda: build_B(T, has_vres))
        res = run_bass_kernel_spmd(ncB, ims, core_ids=cores).results
        del ims
        y = np.empty((T, 2048), f32)
        y[:, :768] = np.concatenate([res[c]["s_yo"] for c in cores], 0).T
        for h in range(10):
            y[:, 768 + h * 64:768 + (h + 1) * 64] = res[h // 2][f"r{h % 2}_ytok"]
        for h in range(5):
            y[:, 1408 + h * 128:1408 + (h + 1) * 128] = res[h]["g_otok"]
        if l == 0:
            vfirst = np.concatenate([res[h // 2][f"r{h % 2}_vout"] for h in range(10)], 1)
        del res
        last = l == depth - 1
        ncC = _prog(("C", tpc, last), lambda: build_C(tpc, last))
        gbt = _tile16(inp["gate_bias"][l])
        ims = []
        for c in cores:
            ts = slice(c * tpc, (c + 1) * tpc)
            ims.append({"xT": xT[c], "zgT": c_(z[ts, 5104:11248].T), "gb": gbt, "yT": c_(y[ts].T), "w_up": inp["w_up"][l], "w_out": inp["w_out"][l],
                        "nm": _tile16(inp["norm_mlp"][l]), "w1": inp["mlp_w1"][l], "w2": inp["mlp_w2"][l], "fn": _tile16(inp["final_norm"]),
                        "gluw": inp["s5_glu_w"][l], "glub": _tile16(inp["s5_glu_b"][l])})
        del z, y
        res = run_bass_kernel_spmd(ncC, ims, core_ids=cores).results
        del ims
        xT = [res[c]["xoT"] for c in cores]
        del res
    out = np.concatenate([xT[c].T for c in cores], 0)[None]
    return np.ascontiguousarray(out.astype(f32))
```
